# Optimizing a Trainium2 kernel written in Bass

```python
import jax
import jax.numpy as jnp
from jax import lax
import numpy as np

D_MODEL = 1024
BATCH = 2
SEQ = 16384
DEPTH = 2
DEC_BATCH = 32
DEC_SEQ = 64
PAST_LEN = 4096

CHUNK = 64
N_A = DEPTH // 2
N_B = DEPTH - N_A
RET_HEADS = 4
RET_DK = D_MODEL // RET_HEADS
RET_DV = 2 * D_MODEL // RET_HEADS
RET_QK = RET_HEADS * RET_DK
RET_V = RET_HEADS * RET_DV
RET_IN = 2 * RET_QK + 2 * RET_V
MLA_HEADS = 16
QK_NOPE = 64
QK_ROPE = 32
V_HEAD = 64
Q_LORA = 384
KV_LORA = 256
Q_BLOCK = 128
D_FF = ((8 * D_MODEL // 3 + 255) // 256) * 256
ROPE_THETA = 10000.0
NORM_EPS = 1e-6
MLA_SCALE = (QK_NOPE + QK_ROPE) ** -0.5
NEG_INF = -1e30

kernel_name = 'yoco_retention_mla_stream_step'


def rmsnorm(x, g):
    xf = x.astype(jnp.float32)
    y = xf * lax.rsqrt(jnp.mean(xf * xf, axis=-1, keepdims=True) + NORM_EPS)
    return (y * g.astype(jnp.float32)).astype(x.dtype)


def rope(x, pos):
    half = x.shape[-1] // 2
    inv_freq = ROPE_THETA ** (-jnp.arange(half, dtype=jnp.float32) / half)
    ang = pos.astype(jnp.float32)[:, None] * inv_freq[None, :]
    cos = jnp.cos(ang)[:, None, :]
    sin = jnp.sin(ang)[:, None, :]
    xf = x.astype(jnp.float32)
    x1, x2 = xf[..., :half], xf[..., half:]
    return jnp.concatenate([x1 * cos - x2 * sin, x2 * cos + x1 * sin], axis=-1).astype(x.dtype)


def swiglu(h, w_in, w_out):
    gate, up = jnp.split(h @ w_in, 2, axis=-1)
    return (jax.nn.silu(gate) * up) @ w_out


def retention_layer(h, w_in, w_out, state0, pos, chunk):
    b, t, _ = h.shape
    nc = t // chunk
    q, k, v, g = jnp.split(h @ w_in, [RET_QK, 2 * RET_QK, 2 * RET_QK + RET_V], axis=-1)
    q = rope(q.reshape(b, t, RET_HEADS, RET_DK), pos).astype(jnp.float32)
    k = rope(k.reshape(b, t, RET_HEADS, RET_DK), pos).astype(jnp.float32) * (RET_DK ** -0.5)
    v = v.reshape(b, t, RET_HEADS, RET_DV).astype(jnp.float32)
    lg = jnp.log(1.0 - 2.0 ** (-5.0 - jnp.arange(RET_HEADS, dtype=jnp.float32)))
    idx = jnp.arange(chunk, dtype=jnp.float32)
    diff = idx[:, None] - idx[None, :]
    dmask = jnp.where(diff >= 0, jnp.exp(jnp.maximum(diff, 0.0)[None] * lg[:, None, None]), 0.0)
    q_dec = jnp.exp((idx + 1.0)[:, None] * lg[None, :])[:, :, None]
    k_dec = jnp.exp((chunk - 1.0 - idx)[:, None] * lg[None, :])[:, :, None]
    c_dec = jnp.exp(chunk * lg)[:, None, None]

    def to_chunks(a):
        return a.reshape(b, nc, chunk, RET_HEADS, a.shape[-1]).swapaxes(0, 1)

    def step(s, inp):
        qc, kc, vc = inp
        a = jnp.einsum('bqhd,bkhd->bhqk', qc, kc) * dmask
        y = jnp.einsum('bhqk,bkhe->bqhe', a, vc) + jnp.einsum('bqhd,bhde->bqhe', qc * q_dec, s)
        s = s * c_dec + jnp.einsum('bkhd,bkhe->bhde', kc * k_dec, vc)
        return s, y

    s_fin, y = lax.scan(step, state0.astype(jnp.float32), (to_chunks(q), to_chunks(k), to_chunks(v)))
    y = y.swapaxes(0, 1).reshape(b, t, RET_HEADS, RET_DV)
    y = y * lax.rsqrt(jnp.mean(y * y, axis=-1, keepdims=True) + NORM_EPS)
    y = y.reshape(b, t, RET_V).astype(h.dtype)
    return (jax.nn.silu(g) * y) @ w_out, s_fin


def mla_shared_kv(x, pos, g_kv_in, w_dkv, g_ckv):
    h = rmsnorm(x, g_kv_in)
    ckr = h @ w_dkv
    c_kv = rmsnorm(ckr[..., :KV_LORA], g_ckv)
    k_rope = rope(ckr[..., KV_LORA:][:, :, None, :], pos)[:, :, 0, :]
    return c_kv, k_rope


def mla_attend(q_lat, q_rope, c_kv, k_rope, w_uv, mask):
    s = (jnp.einsum('bqhc,bkc->bhqk', q_lat, c_kv)
         + jnp.einsum('bqhr,bkr->bhqk', q_rope, k_rope)).astype(jnp.float32) * MLA_SCALE
    if mask is not None:
        s = jnp.where(mask, s, NEG_INF)
    p = jax.nn.softmax(s, axis=-1).astype(c_kv.dtype)
    o_lat = jnp.einsum('bhqk,bkc->bqhc', p, c_kv)
    return jnp.einsum('bqhc,chv->bqhv', o_lat, w_uv)


def mla_attend_prompt(q_lat, q_rope, c_kv, k_rope, w_uv):
    b, s_len = q_lat.shape[:2]
    nb = s_len // Q_BLOCK
    ql = q_lat.reshape(b, nb, Q_BLOCK, MLA_HEADS, KV_LORA).swapaxes(0, 1)
    qr = q_rope.reshape(b, nb, Q_BLOCK, MLA_HEADS, QK_ROPE).swapaxes(0, 1)
    key_chunk = jnp.arange(s_len, dtype=jnp.int32) // CHUNK

    def block(args):
        ql_b, qr_b, bi = args
        q_chunk = (bi * Q_BLOCK + jnp.arange(Q_BLOCK, dtype=jnp.int32)) // CHUNK
        mask = key_chunk[None, :] <= q_chunk[:, None]
        return mla_attend(ql_b, qr_b, c_kv, k_rope, w_uv, mask)

    out = lax.map(block, (ql, qr, jnp.arange(nb, dtype=jnp.int32)))
    return out.swapaxes(0, 1).reshape(b, s_len, MLA_HEADS, V_HEAD)


def mla_layer(h, pos, c_keys, kr_keys, w_dq, g_q, w_uq, w_uk, w_uv, w_o, is_prompt):
    b, t, _ = h.shape
    c_q = rmsnorm(h @ w_dq, g_q)
    q = (c_q @ w_uq).reshape(b, t, MLA_HEADS, QK_NOPE + QK_ROPE)
    q_rope = rope(q[..., QK_NOPE:], pos)
    q_lat = jnp.einsum('bthn,chn->bthc', q[..., :QK_NOPE], w_uk)
    if is_prompt:
        o = mla_attend_prompt(q_lat, q_rope, c_keys, kr_keys, w_uv)
    else:
        o = mla_attend(q_lat, q_rope, c_keys, kr_keys, w_uv, None)
    return o.reshape(b, t, MLA_HEADS * V_HEAD) @ w_o


def run_trunk(x, pos, ret_state0, past_ckv, past_krope, is_prompt,
              g_mix, g_ffn, w_ret_in, w_ret_out, g_kv_in, w_dkv, g_ckv, w_uk, w_uv,
              w_dq, g_q, w_uq, w_mla_out, w_ffn_in, w_ffn_out, g_final):
    chunk = CHUNK if is_prompt else x.shape[1]
    ret_states = []
    c_kv = k_rope = c_keys = kr_keys = None
    for l in range(DEPTH):
        if l < N_A:
            out, s_new = retention_layer(rmsnorm(x, g_mix[l]), w_ret_in[l], w_ret_out[l],
                                         ret_state0[l], pos, chunk)
            x = x + out
            ret_states.append(s_new)
        else:
            if l == N_A:
                c_kv, k_rope = mla_shared_kv(x, pos, g_kv_in, w_dkv, g_ckv)
                if is_prompt:
                    c_keys, kr_keys = c_kv, k_rope
                else:
                    c_keys = jnp.concatenate([past_ckv.astype(c_kv.dtype), c_kv], axis=1)
                    kr_keys = jnp.concatenate([past_krope.astype(k_rope.dtype), k_rope], axis=1)
            j = l - N_A
            x = x + mla_layer(rmsnorm(x, g_mix[l]), pos, c_keys, kr_keys, w_dq[j], g_q[j], w_uq[j],
                              w_uk, w_uv, w_mla_out[j], is_prompt)
        x = x + swiglu(rmsnorm(x, g_ffn[l]), w_ffn_in[l], w_ffn_out[l])
    return rmsnorm(x, g_final), jnp.stack(ret_states), c_kv, k_rope


def setup_inputs(seed: int = 0) -> dict:
    key = jax.random.key(seed)
    ks = jax.random.split(key, 24)
    f32 = jnp.float32

    def w(k, shape, fan_in):
        return jax.random.normal(k, shape, f32) * (fan_in ** -0.5)

    def gain(k, shape):
        return 1.0 + 0.1 * jax.random.normal(k, shape, f32)

    return {
        'x_prompt': jax.random.normal(ks[0], (BATCH, SEQ, D_MODEL), f32),
        'x_sample': jax.random.normal(ks[1], (DEC_BATCH, DEC_SEQ, D_MODEL), f32),
        'state_ret': 0.5 * jax.random.normal(ks[2], (N_A, DEC_BATCH, RET_HEADS, RET_DK, RET_DV), f32),
        'cache_ckv': jax.random.normal(ks[3], (DEC_BATCH, PAST_LEN, KV_LORA), f32),
        'cache_krope': jax.random.normal(ks[4], (DEC_BATCH, PAST_LEN, QK_ROPE), f32),
        'g_mix': gain(ks[5], (DEPTH, D_MODEL)),
        'g_ffn': gain(ks[6], (DEPTH, D_MODEL)),
        'w_ret_in': w(ks[7], (N_A, D_MODEL, RET_IN), D_MODEL),
        'w_ret_out': w(ks[8], (N_A, RET_V, D_MODEL), RET_V),
        'g_kv_in': gain(ks[9], (D_MODEL,)),
        'w_dkv': w(ks[10], (D_MODEL, KV_LORA + QK_ROPE), D_MODEL),
        'g_ckv': gain(ks[11], (KV_LORA,)),
        'w_uk': w(ks[12], (KV_LORA, MLA_HEADS, QK_NOPE), KV_LORA),
        'w_uv': w(ks[13], (KV_LORA, MLA_HEADS, V_HEAD), KV_LORA),
        'w_dq': w(ks[14], (N_B, D_MODEL, Q_LORA), D_MODEL),
        'g_q': gain(ks[15], (N_B, Q_LORA)),
        'w_uq': w(ks[16], (N_B, Q_LORA, MLA_HEADS * (QK_NOPE + QK_ROPE)), Q_LORA),
        'w_mla_out': w(ks[17], (N_B, MLA_HEADS * V_HEAD, D_MODEL), MLA_HEADS * V_HEAD),
        'w_ffn_in': w(ks[18], (DEPTH, D_MODEL, 2 * D_FF), D_MODEL),
        'w_ffn_out': w(ks[19], (DEPTH, D_FF, D_MODEL), D_FF),
        'g_final': gain(ks[20], (D_MODEL,)),
    }


def reference(x_prompt, x_sample, state_ret, cache_ckv, cache_krope, g_mix, g_ffn, w_ret_in, w_ret_out,
              g_kv_in, w_dkv, g_ckv, w_uk, w_uv, w_dq, g_q, w_uq, w_mla_out, w_ffn_in, w_ffn_out, g_final):
    past = cache_ckv.shape[1]
    pos_p = jnp.arange(x_prompt.shape[1], dtype=jnp.int32)
    pos_s = past + jnp.arange(x_sample.shape[1], dtype=jnp.int32)
    zero_state = jnp.zeros((N_A, x_prompt.shape[0], RET_HEADS, RET_DK, RET_DV), jnp.float32)
    y_prompt, st_p, ckv_p, kr_p = run_trunk(
        x_prompt, pos_p, zero_state, None, None, True,
        g_mix, g_ffn, w_ret_in, w_ret_out, g_kv_in, w_dkv, g_ckv, w_uk, w_uv,
        w_dq, g_q, w_uq, w_mla_out, w_ffn_in, w_ffn_out, g_final)
    y_sample, st_s, ckv_s, kr_s = run_trunk(
        x_sample, pos_s, state_ret, cache_ckv, cache_krope, False,
        g_mix, g_ffn, w_ret_in, w_ret_out, g_kv_in, w_dkv, g_ckv, w_uk, w_uv,
        w_dq, g_q, w_uq, w_mla_out, w_ffn_in, w_ffn_out, g_final)
    return (y_prompt, y_sample, st_p.astype(state_ret.dtype), st_s.astype(state_ret.dtype),
            ckv_p, kr_p, ckv_s, kr_s)
```

```python
import contextlib
import numpy as np
import ml_dtypes
import concourse.bass as bass
import concourse.mybir as mybir
from concourse.bass_utils import run_bass_kernel_spmd

F32 = mybir.dt.float32
BF16 = mybir.dt.bfloat16
AF = mybir.ActivationFunctionType
ALU = mybir.AluOpType

D = 1024
RET_IN = 6144
DFF = 2816
EPS = 1e-6
NEG = -30000.0
MLA_SCALE = 96 ** -0.5
GAM = [1.0 - 2.0 ** (-5.0 - h) for h in range(4)]

ALL_ENG = ("pe", "act", "dve", "pool", "sp")


class Op:
    __slots__ = ("eng", "fn", "deps", "dma", "semkey", "ticket", "signal", "idx", "inc", "raw")

    def __init__(self, eng, fn, dma, semkey, idx, inc):
        self.eng, self.fn, self.dma, self.semkey, self.idx, self.inc = eng, fn, dma, semkey, idx, inc
        self.deps = []
        self.raw = set()
        self.ticket = None
        self.signal = False


class _Rec:
    def __getattr__(self, name):
        def f(*a, **k):
            self.call = (name, a, k)
            return self
        return f


class Sched:
    def __init__(self, nc):
        self.nc = nc
        self.ops = []
        self.last_w = {}
        self.readers = {}
        self.last_dma_on_sem = {}
        self.last_compute = {}
        self.mute = False

    def op(self, eng, fn, reads=(), writes=(), dma=False, semkey=None, inc=None):
        if self.mute:
            return None
        rec = _Rec()
        fn(rec)
        o = Op(eng, rec.call, dma, semkey, len(self.ops), inc if inc is not None else (16 if dma else 1))
        deps = {}
        raw = set()
        for r in reads:
            lw = self.last_w.get(r)
            if lw is not None:
                deps[lw.idx] = lw
                raw.add(lw.idx)
        for w in writes:
            lw = self.last_w.get(w)
            if lw is not None:
                deps[lw.idx] = lw
            for rd in self.readers.get(w, ()):
                deps[rd.idx] = rd
        if dma:
            prev = self.last_dma_on_sem.get(semkey)
            if prev is not None:
                deps[prev.idx] = prev
            self.last_dma_on_sem[semkey] = o
        else:
            self.last_compute[eng] = o
        o.deps = list(deps.values())
        o.raw = raw if eng != "pe" else set()
        for r in reads:
            self.readers.setdefault(r, []).append(o)
        for w in writes:
            self.last_w[w] = o
            self.readers[w] = []
        self.ops.append(o)
        return o

    def barrier(self):
        if self.mute:
            return
        lasts = list(self.last_compute.values()) + list(self.last_dma_on_sem.values())
        for e in ALL_ENG:
            o = Op(e, None, False, None, len(self.ops), 1)
            o.deps = list(lasts)
            self.ops.append(o)
        self.last_w = {}
        self.readers = {}

    def emit(self):
        nc = self.nc
        ops = self.ops
        for o in ops:
            for d in o.deps:
                if d.dma or d.eng != o.eng or o.dma or d.idx in o.raw:
                    d.signal = True
        for o in ops:
            if o.dma:
                o.signal = True
        cnt = {}
        for o in ops:
            if not o.signal or o.fn is None:
                o.signal = False if o.fn is None else o.signal
                continue
            key = ("dma", o.semkey) if o.dma else ("eng", o.eng)
            cnt[key] = cnt.get(key, 0) + o.inc
            o.ticket = (key, cnt[key])
        keys = list(cnt.keys())
        self.n_sems = len(keys)
        print("sched: %d ops, %d semaphores" % (len(ops), len(keys)))
        with contextlib.ExitStack() as st:
            sems = {}
            for i, k in enumerate(keys):
                sems[k] = st.enter_context(nc.semaphore("s%d" % i))
            block = st.enter_context(nc.Block())
            per_eng = {e: [o for o in ops if o.eng == e] for e in ALL_ENG}
            final = dict(cnt)

            def run(engname, engobj):
                waited = {}
                for o in per_eng[engname]:
                    need = {}
                    for d in o.deps:
                        if d.ticket is None:
                            continue
                        if (not d.dma) and d.eng == engname and not o.dma and d.idx not in o.raw:
                            continue
                        k, v = d.ticket
                        if need.get(k, 0) < v:
                            need[k] = v
                    for k, v in need.items():
                        if waited.get(k, 0) >= v:
                            continue
                        engobj.wait_ge(sems[k], v)
                        waited[k] = v
                    if o.fn is None:
                        continue
                    name_, a_, k_ = o.fn
                    ins = getattr(engobj, name_)(*a_, **k_)
                    if o.signal:
                        ins.then_inc(sems[o.ticket[0]], o.inc)
                if engname == "sp":
                    for k, v in final.items():
                        engobj.wait_ge(sems[k], v)

            block.sync(lambda e: run("sp", e))
            block.tensor(lambda e: run("pe", e))
            block.scalar(lambda e: run("act", e))
            block.vector(lambda e: run("dve", e))
            block.gpsimd(lambda e: run("pool", e))


class Cfg:
    def __init__(self, nb=8, past=4096, upto=9, dbg=0):
        self.upto = upto
        self.dbg = dbg
        self.pairs = False
        import os
        self.skip = [int(v) for v in os.environ.get("KSKIP", "").split(",") if v]
        self.NB = nb
        self.SEQ = nb * 4 * 512
        self.PAST = past
        self.NT0 = 2 * nb + 2


W_SPECS = [
    ("w_ret_in", 1024, 6144, 0), ("w_ret_out", 2048, 1024, None),
    ("w_ffn_in0", 1024, 5632, 8), ("w_ffn_out0", 2816, 1024, None),
    ("w_ffn_in1", 1024, 5632, 16), ("w_ffn_out1", 2816, 1024, None),
    ("w_dkv", 1024, 288, 24), ("w_dq", 1024, 384, 32), ("w_uq", 384, 1536, 40),
    ("w_o", 1024, 1024, None), ("w_uk", 256, 1024, None), ("w_uv", 256, 1024, None),
]


class Prog:
    def __init__(self, cfg):
        self.cfg = cfg
        self.nc = bass.Bass("TRN2", target_bir_lowering=False)
        self.S = Sched(self.nc)
        self.rr = 0
        self.wslot = 0
        self.ctr = {}

    def din(self, name, shape, dt=F32):
        return self.nc.dram_tensor(name, list(shape), dt, kind="ExternalInput").ap()

    def dout(self, name, shape, dt=F32):
        return self.nc.dram_tensor(name, list(shape), dt, kind="ExternalOutput").ap()

    def dint(self, name, shape, dt):
        return self.nc.dram_tensor(name, list(shape), dt).ap()

    def view(self, nf32, shape, dt):
        off = self.aoff
        self.aoff += nf32
        assert self.aoff <= self.ARENA, (self.aoff, self.ARENA)
        a = self.AR[:shape[0], off:off + nf32]
        if dt != F32:
            a = a.bitcast(dt)
        if len(shape) == 3:
            a = a.rearrange("p (a b) -> p a b", b=shape[2])
        elif len(shape) == 4:
            a = a.rearrange("p (a b c) -> p a b c", b=shape[2], c=shape[3])
        return a

    def nxt(self, name, n):
        v = self.ctr.get(name, 0)
        self.ctr[name] = v + 1
        return v % n

    def dma(self, eng, out, in_, reads, writes, semkey):
        self.S.op(eng, lambda e: e.dma_start(out=out, in_=in_), reads=reads, writes=writes, dma=True, semkey=semkey)

    def ev_engine(self):
        self.rr += 1
        return ("dve", "act")[self.rr % 2]

    def copy(self, eng, out, in_, reads, writes):
        if eng == "act":
            self.S.op("act", lambda e: e.activation(out, in_, AF.Copy), reads=reads, writes=writes)
        else:
            self.S.op(eng, lambda e: e.tensor_copy(out, in_), reads=reads, writes=writes)

    def wslab(self, wname, r0, nk, kp, c0, ncols):
        slot = self.wslot % 4
        self.wslot += 1
        buf = self.WB[slot]
        key = ("W", slot)
        dst = buf[:kp, 0:nk * ncols].rearrange("p (k n) -> p k n", n=ncols)
        src = self.wb[wname][r0:r0 + nk * kp, c0:c0 + ncols].rearrange("(k p) n -> p k n", p=kp)
        self.dma("sp", dst, src, reads=[("wb", wname)], writes=[key], semkey=key)
        return dst, key

    def transposes(self, srcs, src_keys, dst, dst_keys, P, rows):
        n = len(srcs)
        for g0 in range(0, n, 8):
            g1 = min(n, g0 + 8)
            b = self.nxt("tpb", 2)
            pk = ("ps", b)
            pb = self.PSB[b]
            for i in range(g0, g1):
                o_ap = pb[:rows, (i - g0) * P:(i - g0 + 1) * P]
                s_ap = srcs[i]
                self.S.op("pe", lambda e, o_ap=o_ap, s_ap=s_ap: e.transpose(o_ap, s_ap, self.ident[:P, :P]),
                          reads=list(src_keys) + ["ident"], writes=[pk])
            src_v = pb[:rows, 0:(g1 - g0) * P].rearrange("p (a b) -> p a b", b=P)
            self.copy(self.ev_engine(), dst[:rows, g0:g1, :], src_v, reads=[pk], writes=list(dst_keys))

    def rsqrt(self, dst, src, scale, rkeys, wkeys):
        self.S.op("dve", lambda e: e.tensor_scalar(dst, src, scale, EPS, ALU.mult, ALU.add), reads=list(rkeys), writes=list(wkeys))
        self.S.op("act", lambda e: e.activation(dst, dst, AF.Sqrt), reads=list(wkeys), writes=list(wkeys))
        self.S.op("dve", lambda e: e.reciprocal(dst, dst), reads=list(wkeys), writes=list(wkeys))

    def rms_to_bf16(self, x_ap_fn, xkeys, out_fn, okeys, subs, P, tag):
        for s in subs:
            ssk = (tag, "ss", s)
            ss = self.ss[:P, s:s + 1]
            rs = self.rstd[:P, s:s + 1]
            self.S.op("dve", lambda e, ss=ss: e.memset(ss, 0.0), writes=[ssk])
            xa = x_ap_fn(s)
            self.S.op("act", lambda e, xa=xa, ss=ss: e.activation(self.junk[:P, :D], xa, AF.Square, accum_out=ss),
                      reads=list(xkeys(s)), writes=[ssk, "junk"])
            self.rsqrt(rs, ss, 1.0 / D, [ssk], [(tag, "rs", s)])
            oa = out_fn(s)
            self.S.op("act", lambda e, oa=oa, xa=xa, rs=rs: e.activation(oa, xa, AF.Copy, scale=rs),
                      reads=list(xkeys(s)) + [(tag, "rs", s)], writes=list(okeys(s)))

    def linear_T(self, actT, act_keys, wname, K, kp, col0, ncols_total, slabw, subs, P, evac, bank_of):
        nkc = K // kp
        kgs = [(a, min(8, nkc - a)) for a in range(0, nkc, 8)]
        j = 0
        for c0 in range(0, ncols_total, slabw):
            w = min(slabw, ncols_total - c0)
            par = self.nxt("linpar", 2)
            for gi, (k0, nk) in enumerate(kgs):
                wap, wkey = self.wslab(wname, k0 * kp, nk, kp, col0 + c0, w)
                for si, s in enumerate(subs):
                    b = bank_of(si, par)
                    pk = ("ps", b)
                    for kc in range(nk):
                        first = (gi == 0 and kc == 0)
                        last = (gi == len(kgs) - 1 and kc == nk - 1)
                        o_ap = self.PS[b][:P, 0:w]
                        l_ap = actT[:kp, k0 + kc, si * P:(si + 1) * P]
                        r_ap = wap[:kp, kc, :]
                        self.S.op("pe", lambda e, o_ap=o_ap, l_ap=l_ap, r_ap=r_ap, first=first, last=last:
                                  e.matmul(o_ap, l_ap, r_ap, start=first, stop=last),
                                  reads=list(act_keys) + [wkey], writes=[pk])
            for si, s in enumerate(subs):
                b = bank_of(si, par)
                evac(s, j, c0, w, self.PS[b][:P, 0:w], ("ps", b))
            j += 1

    def build(self):
        cfg, nc, S = self.cfg, self.nc, self.S
        NB, NT0, PAST, SEQ = cfg.NB, cfg.NT0, cfg.PAST, cfg.SEQ
        NKP = SEQ
        NKS = PAST + 64
        xp = self.din("xp", [NB, 512, D])
        xs = self.din("xs", [4, 64, D])
        st0 = self.din("st0", [4, 4, 256, 512])
        cckv = self.din("cckv", [4, PAST, 256])
        ckr = self.din("ckr", [4, PAST, 32])
        win = {}
        for name, K, N, g in W_SPECS:
            win[name] = self.din(name, [K, N])
        gtab = self.din("gtab", [128, 43])
        gckv = self.din("gckv", [128, 256])
        gfin = self.din("gfin", [128, D])
        dec_d = self.din("dec", [128, 12])
        m128_d = self.din("m128", [128, 4, 128])
        m64_d = self.din("m64", [128, 4, 128])
        cosr_d = self.din("cosr", [NT0, 128, 2, 128])
        sinr_d = self.din("sinr", [NT0, 128, 2, 128])
        cosm_d = self.din("cosm", [NT0, 128, 2, 16])
        sinm_d = self.din("sinm", [NT0, 128, 2, 16])
        pcoef_d = self.din("pcoef", [128, 24])
        aind_d = self.din("aind", [8, 4, 128], BF16)
        bm_d = self.din("bm", [8, 4, 512], BF16)
        ident_d = self.din("ident", [128, 128], BF16)
        esel_d = self.din("esel", [65, 64])

        yp = self.dout("yp", [NB, 512, D])
        ys = self.dout("ys", [4, 64, D])
        stp = self.dout("stp", [4, 256, 512])
        sts = self.dout("sts", [4, 4, 256, 512])
        ckvp = self.dout("ckvp", [NB, 512, 256])
        krp = self.dout("krp", [NB, 512, 32])
        ckvs = self.dout("ckvs", [4, 64, 256])
        krs = self.dout("krs", [4, 64, 32])

        self.wb = {name: self.dint("wb_" + name, [K, N], BF16) for name, K, N, g in W_SPECS}
        Lloc = self.dint("Lloc", [NB * 128, 4096], F32)
        Lall = self.dint("Lall", [4 * NB * 128, 4096], F32)
        Sst = self.dint("Sst", [NB, 128, 4096], F32)
        x2s = self.dint("x2s", [NB * 512 + 256, D], F32)
        cloc = self.dint("cloc", [NB * 512, 288], BF16)
        call = self.dint("call", [4 * NB * 512, 288], BF16)
        clocs = self.dint("clocs", [4, 64, 288], BF16)
        KTp = self.dint("KTp", [16, 96, NKP], BF16)
        Vp = self.dint("Vp", [NKP, 1040], BF16)
        KTs = self.dint("KTs", [4, 16, 96, NKS], BF16)
        Vs = self.dint("Vs", [4, NKS, 1040], BF16)

        st = contextlib.ExitStack()
        with st:
            sb = lambda name, shape, dt: st.enter_context(nc.sbuf_tensor("s_" + name, list(shape), dt))
            self.ident = sb("ident", [128, 128], BF16)
            dec = sb("dec", [128, 12], F32)
            m128 = sb("m128", [128, 4, 128], F32)
            m64 = sb("m64", [128, 4, 128], F32)
            pcoef = sb("pcoef", [128, 24], F32)
            gck = sb("gck", [128, 256], F32)
            gfn = sb("gfn", [128, D], F32)
            gt = sb("gt", [128, 43], F32)
            aind = sb("aind", [8, 4, 128], BF16)
            bm = sb("bm", [8, 4, 512], BF16)
            esel = sb("esel", [65, 64], F32)
            self.ss = sb("ss", [128, 8], F32)
            self.rstd = sb("rstd", [128, 8], F32)
            ssy = sb("ssy", [128, 8], F32)
            rsy = sb("rsy", [128, 8], F32)
            self.junk = sb("junk", [128, D], BF16)
            self.WB = [sb("wb%d" % i, [128, 4096], BF16) for i in range(4)]
            self.ARENA = 33280
            self.AR = sb("arena", [128, self.ARENA], F32)
            psall = st.enter_context(nc.psum_tensor("psall", [128, 4096], F32))
            self.PS = [psall[:, i * 512:(i + 1) * 512] for i in range(8)]
            self.PSB = [p.bitcast(BF16) for p in self.PS]
            PS, PSB = self.PS, self.PSB

            for (t, d_, k) in ((self.ident, ident_d, "ident"), (dec, dec_d, "dec"), (m128, m128_d, "m128"),
                               (m64, m64_d, "m64"), (pcoef, pcoef_d, "pcoef"), (gck, gckv, "gck"),
                               (gfn, gfin, "gfn"), (gt, gtab, "gt"), (aind, aind_d, "aind"), (bm, bm_d, "bm"),
                               (esel, esel_d, "esel")):
                nd = len(t.shape)
                self.dma("sp", t[tuple([slice(None)] * nd)], d_, reads=[], writes=[k], semkey=("c", self.nxt("csem", 2)))

            CW = 1536
            self.aoff = self.ARENA - 3 * (CW + CW // 2)
            conv_base = self.aoff
            cin = [self.view(CW, [128, CW], F32) for _ in range(3)]
            cout = [self.view(CW // 2, [128, CW], BF16) for _ in range(3)]
            conv_state = {"ci": 0}

            def conv_chunk(name, g, kc, n0, w):
                ci = conv_state["ci"]
                sl = ci % 3
                conv_state["ci"] = ci + 1
                self.dma("sp", cin[sl][:, 0:w], win[name][kc * 128:(kc + 1) * 128, n0:n0 + w],
                         reads=[], writes=[("cin", sl)], semkey=("cin", sl))
                eng = ("dve", "pool", "act")[ci % 3] if g is None else ("dve", "act")[ci % 2]
                o_ap, i_ap = cout[sl][:, 0:w], cin[sl][:, 0:w]
                if g is None:
                    self.copy(eng, o_ap, i_ap, reads=[("cin", sl)], writes=[("cout", sl)])
                else:
                    sc = gt[:, g + kc:g + kc + 1]
                    if eng == "act":
                        S.op("act", lambda e: e.activation(o_ap, i_ap, AF.Copy, scale=sc),
                             reads=[("cin", sl), "gt"], writes=[("cout", sl)])
                    else:
                        S.op(eng, lambda e: e.tensor_scalar(o_ap, i_ap, sc, None, ALU.mult),
                             reads=[("cin", sl), "gt"], writes=[("cout", sl)])
                self.dma("pool", self.wb[name][kc * 128:(kc + 1) * 128, n0:n0 + w], o_ap,
                         reads=[("cout", sl)], writes=[("wb", name)], semkey="wst")

            conv_todo = []
            for name, K, N, g in W_SPECS:
                for kc in range(K // 128):
                    for n0 in range(0, N, CW):
                        conv_todo.append((name, g, kc, n0, min(CW, N - n0)))
            n_first = sum(1 for c_ in conv_todo if c_[0] == "w_ret_in")
            for c_ in conv_todo[:n_first]:
                conv_chunk(*c_)
            conv_todo = conv_todo[n_first:]
            S.mute = cfg.upto < 1 or 1 in cfg.skip

            self.aoff = 0
            xt = self.view(2048, [128, 2, D], F32)
            hb = self.view(1024, [128, 2, D], BF16)
            hT = self.view(1024, [128, 8, 256], BF16)
            qb = self.view(1024, [128, 2, D], BF16)
            kb = self.view(1024, [128, 2, D], BF16)
            vb = self.view(2048, [128, 2, 2048], BF16)
            sgb = self.view(2048, [128, 2, 2048], BF16)
            qT = self.view(512, [128, 8, 128], BF16)
            kT = self.view(512, [128, 8, 128], BF16)
            aTs = self.view(256, [128, 4, 128], BF16)
            ygb = self.view(1024, [128, 2048], BF16)
            ygT = self.view(2048, [128, 16, 256], BF16)
            hidT = self.view(2816, [128, 22, 256], BF16)
            cosr = self.view(256, [128, 2, 128], F32)
            sinr = self.view(256, [128, 2, 128], F32)
            cosm = self.view(32, [128, 2, 16], F32)
            sinm = self.view(32, [128, 2, 16], F32)
            rt = [self.view(128, [128, 128], F32) for _ in range(8)]
            Sf = self.view(4096, [128, 8, 512], F32)
            Sb = self.view(2048, [128, 8, 512], BF16)
            sgt = self.view(256, [128, 256], F32)
            ckvo = self.view(512, [128, 2, 256], F32)
            kro = self.view(64, [128, 2, 32], F32)
            cbf = self.view(288, [128, 2, 288], BF16)
            krt = [self.view(16, [128, 16], F32) for _ in range(4)]
            assert self.aoff <= conv_base, (self.aoff, conv_base)

            def bank_l0(si, par):
                return 2 + si + 2 * par

            def x_src(t):
                if t < 2 * NB:
                    m, hf = t // 2, t % 2
                    return xp[m, hf * 256:(hf + 1) * 256, :].rearrange("(s p) d -> p s d", p=128), 128
                j = t - 2 * NB
                return xs[2 * j:2 * j + 2, :, :].rearrange("s p d -> p s d"), 64

            def load_x(t):
                src, P = x_src(t)
                self.dma("sp", xt[:P, :, :], src, reads=[], writes=[("xt", 0), ("xt", 1)], semkey="xt")
                return P

            def load_rope(t):
                self.dma("sp", cosr[:, :, :], cosr_d[t], reads=[], writes=["cosr"], semkey="cosr")
                self.dma("sp", sinr[:, :, :], sinr_d[t], reads=[], writes=["sinr"], semkey="sinr")
                self.dma("sp", cosm[:, :, :], cosm_d[t], reads=[], writes=["cosm"], semkey="cosm")
                self.dma("sp", sinm[:, :, :], sinm_d[t], reads=[], writes=["sinm"], semkey="sinm")

            def norm_T(P, tag):
                self.rms_to_bf16(lambda s: xt[:P, s, :], lambda s: [("xt", s)],
                                 lambda s: hb[:P, s, :], lambda s: [("hb", s)], (0, 1), P, tag)
                for s in (0, 1):
                    self.transposes([hb[:P, s, kc * 128:(kc + 1) * 128] for kc in range(8)], [("hb", s)],
                                    hT[:, :, s * P:(s + 1) * P], [("hT", s)], P, 128)

            def rope_evac(ps_ap, pk, s, j, dst, dkey, P, dcol):
                for hh in range(2):
                    h = 2 * (j % 2) + hh
                    dsc = dec[:P, dcol + h:dcol + h + 1]
                    x1 = ps_ap[:, hh * 256:hh * 256 + 128]
                    x2 = ps_ap[:, hh * 256 + 128:hh * 256 + 256]
                    cs = cosr[:P, s, :]
                    sn = sinr[:P, s, :]
                    i0 = self.nxt("rt", 2) * 4
                    t1, t2, t3, t4 = rt[i0][:P, :], rt[i0 + 1][:P, :], rt[i0 + 2][:P, :], rt[i0 + 3][:P, :]
                    rk = [("rt", i0 + i) for i in range(4)]
                    for (tt, xx, tb, tk, tbk) in ((t1, x1, cs, rk[0], "cosr"), (t2, x2, sn, rk[1], "sinr"),
                                                  (t3, x2, cs, rk[2], "cosr"), (t4, x1, sn, rk[3], "sinr")):
                        S.op("dve", lambda e, tt=tt, xx=xx, tb=tb, dsc=dsc: e.scalar_tensor_tensor(tt, xx, dsc, tb, ALU.mult, ALU.mult),
                             reads=[pk, "dec", tbk], writes=[tk])
                    c = (j % 2) * 512 + hh * 256
                    o1 = dst[:P, s, c:c + 128]
                    o2 = dst[:P, s, c + 128:c + 256]
                    S.op("pool", lambda e, o1=o1, t1=t1, t2=t2: e.tensor_tensor(o1, t1, t2, ALU.subtract),
                         reads=[rk[0], rk[1]], writes=[dkey])
                    S.op("pool", lambda e, o2=o2, t3=t3, t4=t4: e.tensor_tensor(o2, t3, t4, ALU.add),
                         reads=[rk[2], rk[3]], writes=[dkey])

            def proj_qkvg(P, pass1):
                dk_col = 4 if P == 128 else 8

                def evac(s, j, c0, w, ps_ap, pk):
                    jj = c0 // 512
                    if jj < 2:
                        rope_evac(ps_ap, pk, s, jj, qb, ("qb", s), P, 0)
                    elif jj < 4:
                        rope_evac(ps_ap, pk, s, jj, kb, ("kb", s), P, dk_col)
                    elif jj < 8:
                        o_ap = vb[:P, s, (jj - 4) * 512:(jj - 3) * 512]
                        self.copy(self.ev_engine(), o_ap, ps_ap, reads=[pk], writes=[("vb", s)])
                    else:
                        o_ap = sgb[:P, s, (jj - 8) * 512:(jj - 7) * 512]
                        S.op("act", lambda e, o_ap=o_ap, ps_ap=ps_ap: e.activation(o_ap, ps_ap, AF.Silu),
                             reads=[pk], writes=[("sgb", s)])
                if pass1:
                    self.linear_T(hT, [("hT", 0), ("hT", 1)], "w_ret_in", 1024, 128, 1024, 3072, 512, (0, 1), P,
                                  lambda s, j, c0, w, a, k: evac(s, j, c0 + 1024, w, a, k), bank_l0)
                else:
                    self.linear_T(hT, [("hT", 0), ("hT", 1)], "w_ret_in", 1024, 128, 0, 6144, 512, (0, 1), P, evac, bank_l0)

            def state_update(s, P, h):
                gP = GAM[h] ** P
                for dc in range(2):
                    b = 5 + dc
                    o_ap = PS[b][:, :]
                    l_ap = kb[:P, s, (2 * h + dc) * 128:(2 * h + dc + 1) * 128]
                    r_ap = vb[:P, s, h * 512:(h + 1) * 512]
                    S.op("pe", lambda e, o_ap=o_ap, l_ap=l_ap, r_ap=r_ap: e.matmul(o_ap, l_ap, r_ap, start=True, stop=True),
                         reads=[("kb", s), ("vb", s)], writes=[("ps", b)])
                    sf = Sf[:, 2 * h + dc, :]
                    S.op("dve", lambda e, sf=sf, o_ap=o_ap, gP=gP: e.scalar_tensor_tensor(sf, sf, gP, o_ap, ALU.mult, ALU.add),
                         reads=[("ps", b), ("Sf", h)], writes=[("Sf", h)])

            def cast_state(h):
                for dc in range(2):
                    self.copy("pool", Sb[:, 2 * h + dc, :], Sf[:, 2 * h + dc, :], reads=[("Sf", h)], writes=[("Sb", h)])

            def ret_core(s, P):
                msk = m128 if P == 128 else m64
                mk = "m128" if P == 128 else "m64"
                self.transposes([qb[:P, s, c * 128:(c + 1) * 128] for c in range(8)], [("qb", s)], qT[:, :, :P], ["qT"], P, 128)
                self.transposes([kb[:P, s, c * 128:(c + 1) * 128] for c in range(8)], [("kb", s)], kT[:, :, :P], ["kT"], P, 128)
                for h in range(4):
                    for dc in range(2):
                        o_ap = PS[2][:P, h * 128:h * 128 + P]
                        l_ap, r_ap = kT[:, 2 * h + dc, :P], qT[:, 2 * h + dc, :P]
                        S.op("pe", lambda e, o_ap=o_ap, l_ap=l_ap, r_ap=r_ap, dc=dc: e.matmul(o_ap, l_ap, r_ap, start=(dc == 0), stop=(dc == 1)),
                             reads=["qT", "kT"], writes=[("ps", 2)])
                    a_ap = aTs[:P, h, :P]
                    p_ap = PS[2][:P, h * 128:h * 128 + P]
                    m_ap = msk[:P, h, :P]
                    S.op("dve", lambda e, a_ap=a_ap, p_ap=p_ap, m_ap=m_ap: e.tensor_tensor(a_ap, p_ap, m_ap, ALU.mult),
                         reads=[("ps", 2), mk], writes=[("aTs", h)])
                    yb = 3 + (h % 2)
                    y_ap = PS[yb][:P, :]
                    r_ap = vb[:P, s, h * 512:(h + 1) * 512]
                    S.op("pe", lambda e, y_ap=y_ap, a_ap=a_ap, r_ap=r_ap: e.matmul(y_ap, a_ap, r_ap, start=True, stop=False),
                         reads=[("aTs", h), ("vb", s)], writes=[("ps", yb)])
                    for dc in range(2):
                        l_ap, r2 = qT[:, 2 * h + dc, :P], Sb[:, 2 * h + dc, :]
                        S.op("pe", lambda e, y_ap=y_ap, l_ap=l_ap, r2=r2, dc=dc: e.matmul(y_ap, l_ap, r2, start=False, stop=(dc == 1)),
                             reads=["qT", ("Sb", h)], writes=[("ps", yb)])
                    state_update(s, P, h)
                    cast_state(h)
                    sy = ssy[:P, h:h + 1]
                    ry = rsy[:P, h:h + 1]
                    S.op("dve", lambda e, sy=sy: e.memset(sy, 0.0), writes=[("ssy", h)])
                    S.op("act", lambda e, y_ap=y_ap, sy=sy: e.activation(self.junk[:P, 0:512], y_ap, AF.Square, accum_out=sy),
                         reads=[("ps", yb)], writes=[("ssy", h), "junk"])
                    self.rsqrt(ry, sy, 1.0 / 512, [("ssy", h)], [("rsy", h)])
                    g_ap = sgb[:P, s, h * 512:(h + 1) * 512]
                    o_ap = ygb[:P, h * 512:(h + 1) * 512]
                    S.op("dve", lambda e, o_ap=o_ap, y_ap=y_ap, ry=ry, g_ap=g_ap: e.scalar_tensor_tensor(o_ap, y_ap, ry, g_ap, ALU.mult, ALU.mult),
                         reads=[("ps", yb), ("rsy", h), ("sgb", s)], writes=["ygb"])
                self.transposes([ygb[:P, c * 128:(c + 1) * 128] for c in range(16)], ["ygb"], ygT[:, :, s * P:(s + 1) * P], [("ygT", s)], P, 128)

            def resid_evac(s, j, c0, w, ps_ap, pk, P):
                x_ap = xt[:P, s, c0:c0 + w]
                S.op("dve", lambda e, x_ap=x_ap, ps_ap=ps_ap: e.tensor_tensor(x_ap, ps_ap, x_ap, ALU.add),
                     reads=[pk, ("xt", s)], writes=[("xt", s)])

            for m in range(NB):
                for h in range(4):
                    for dc in range(2):
                        sf = Sf[:, 2 * h + dc, :]
                        S.op("pool", lambda e, sf=sf: e.memset(sf, 0.0), writes=[("Sf", h)])
                for hf in range(2):
                    t = 2 * m + hf
                    load_rope(t)
                    P = load_x(t)
                    norm_T(P, "n0")
                    proj_qkvg(P, True)
                    for s in (0, 1):
                        for h in range(4):
                            state_update(s, P, h)
                    per_tile = -(-len(conv_todo) // max(1, 2 * NB - t)) if t < 2 * NB - 1 else len(conv_todo)
                    for c_ in conv_todo[:per_tile]:
                        conv_chunk(*c_)
                    conv_todo = conv_todo[per_tile:]
                self.dma("pool", Lloc[m * 128:(m + 1) * 128, :], Sf[:, :, :].rearrange("p a b -> p (a b)"),
                         reads=[("Sf", h) for h in range(4)], writes=[("Lloc", m)], semkey="Lst")
                for half in range(2):
                    ch = 2 * m + half
                    i_ap = Lloc[ch * 64:(ch + 1) * 64, :]
                    o_ap = Lall[ch * 256:(ch + 1) * 256, :]
                    S.op("pool", lambda e, i_ap=i_ap, o_ap=o_ap: e.collective_compute(
                        "AllGather", ALU.bypass, replica_groups=[[0, 1, 2, 3], [4, 5, 6, 7]], ins=[i_ap.opt()], outs=[o_ap.opt()]),
                         reads=[("Lloc", m)], writes=[("Lall", ch)], dma=True, semkey="cc1", inc=1)
            S.barrier()
            S.mute = cfg.upto < 2 or 2 in cfg.skip

            save = self.aoff
            self.aoff = 0
            Tst = self.view(4096, [128, 8, 512], F32)
            Acc = self.view(4096, [128, 8, 512], F32)
            Lb = [self.view(4096, [128, 8, 512], F32) for _ in range(2)]
            for a in range(8):
                ta = Tst[:, a, :]
                S.op("pool", lambda e, ta=ta: e.memset(ta, 0.0), writes=[("Tst", a)])
            for m in range(NB):
                for h in range(4):
                    for dc in range(2):
                        a_ap, t_ap = Acc[:, 2 * h + dc, :], Tst[:, 2 * h + dc, :]
                        sc = pcoef[:, h:h + 1]
                        S.op("dve", lambda e, a_ap=a_ap, t_ap=t_ap, sc=sc: e.tensor_scalar(a_ap, t_ap, sc, None, ALU.mult),
                             reads=[("Tst", 2 * h + dc), "pcoef"], writes=[("Acc", 2 * h + dc)])
                        g2048 = GAM[h] ** 2048
                        S.op("pool", lambda e, t_ap=t_ap, g2048=g2048: e.tensor_scalar(t_ap, t_ap, g2048, None, ALU.mult),
                             reads=[("Tst", 2 * h + dc)], writes=[("Tst", 2 * h + dc)])
                for i in range(4):
                    sl = self.nxt("Lb", 2)
                    for half in range(2):
                        r0 = ((2 * m + half) * 4 + i) * 64
                        self.dma("sp", Lb[sl][half * 64:(half + 1) * 64, :, :].rearrange("p a b -> p (a b)"), Lall[r0:r0 + 64, :],
                                 reads=[("Lall", 2 * m + half)], writes=[("Lb", sl)], semkey=("Lb", sl, half))
                    for h in range(4):
                        for dc in range(2):
                            a_ap, t_ap, l_ap = Acc[:, 2 * h + dc, :], Tst[:, 2 * h + dc, :], Lb[sl][:, 2 * h + dc, :]
                            sc = pcoef[:, 4 + 4 * i + h:4 + 4 * i + h + 1]
                            S.op("dve", lambda e, a_ap=a_ap, l_ap=l_ap, sc=sc: e.scalar_tensor_tensor(a_ap, l_ap, sc, a_ap, ALU.mult, ALU.add),
                                 reads=[("Lb", sl), "pcoef", ("Acc", 2 * h + dc)], writes=[("Acc", 2 * h + dc)])
                            cT = GAM[h] ** (512 * (3 - i))
                            S.op("dve", lambda e, t_ap=t_ap, l_ap=l_ap, cT=cT: e.scalar_tensor_tensor(t_ap, l_ap, cT, t_ap, ALU.mult, ALU.add),
                                 reads=[("Lb", sl), ("Tst", 2 * h + dc)], writes=[("Tst", 2 * h + dc)])
                self.dma("pool", Sst[m], Acc[:, :, :].rearrange("p a b -> p (a b)"), reads=[("Acc", a) for a in range(8)], writes=["Sst"], semkey="Sst")
            self.aoff = save
            S.barrier()
            S.mute = cfg.upto < 3 or 3 in cfg.skip

            def ffn(l, P, subs_x, x_of, xkey_of, tagn):
                self.rms_to_bf16(lambda s: x_of(s), lambda s: [xkey_of(s)],
                                 lambda s: hb[:P, s, :], lambda s: [("hb", s)], (0, 1), P, tagn)
                for s in (0, 1):
                    self.transposes([hb[:P, s, kc * 128:(kc + 1) * 128] for kc in range(8)], [("hb", s)],
                                    hT[:, :, s * P:(s + 1) * P], [("hT", s)], P, 128)
                T = 2 * P
                wn = "w_ffn_in%d" % l
                for g0 in range(0, 22, 4):
                    ng = min(4, 22 - g0)
                    wg, wgk = self.wslab(wn, 0, 8, 128, g0 * 128, ng * 128)
                    wu, wuk = self.wslab(wn, 0, 8, 128, DFF + g0 * 128, ng * 128)
                    for jj in range(ng):
                        oc = g0 + jj
                        b = 2 + self.nxt("ffb", 4)
                        pk = ("ps", b)
                        for (wap, wk, off) in ((wg, wgk, 0), (wu, wuk, 256)):
                            for kc in range(8):
                                o_ap = PS[b][:, off:off + T]
                                l_ap = wap[:, kc, jj * 128:(jj + 1) * 128]
                                r_ap = hT[:, kc, 0:T]
                                S.op("pe", lambda e, o_ap=o_ap, l_ap=l_ap, r_ap=r_ap, kc=kc: e.matmul(o_ap, l_ap, r_ap, start=(kc == 0), stop=(kc == 7)),
                                     reads=[("hT", 0), ("hT", 1), wk], writes=[pk])
                        g_ap = PS[b][:, 0:T]
                        u_ap = PS[b][:, 256:256 + T]
                        sl = self.nxt("sgt", 1)
                        t_ap = sgt[:, 0:T]
                        S.op("act", lambda e, t_ap=t_ap, g_ap=g_ap: e.activation(t_ap, g_ap, AF.Silu), reads=[pk], writes=["sgt"])
                        h_ap = hidT[:, oc, 0:T]
                        S.op("dve", lambda e, h_ap=h_ap, t_ap=t_ap, u_ap=u_ap: e.tensor_tensor(h_ap, t_ap, u_ap, ALU.mult),
                             reads=["sgt", pk], writes=["hidT"])
                self.linear_T(hidT, ["hidT"], "w_ffn_out%d" % l, DFF, 128, 0, D, 512, (0, 1), P,
                              lambda s, j, c0, w, a, k: subs_x(s, c0, w, a, k), bank_l0)

            def l0_resid(P):
                return lambda s, c0, w, a, k: resid_evac(s, 0, c0, w, a, k, P)

            for t in range(NT0):
                prompt = t < 2 * NB
                m, hf = t // 2, t % 2
                load_rope(t)
                if prompt and hf == 0:
                    self.dma("sp", Sf[:, :, :].rearrange("p a b -> p (a b)"), Sst[m], reads=["Sst"],
                             writes=[("Sf", h) for h in range(4)], semkey="Sfl")
                    for h in range(4):
                        cast_state(h)
                P = load_x(t)
                norm_T(P, "n0")
                proj_qkvg(P, False)
                for s in (0, 1):
                    if not prompt:
                        sq = (t - 2 * NB) * 2 + s
                        self.dma("sp", Sf[:, :, :], st0[sq].rearrange("h (dc p) e -> p (h dc) e", p=128), reads=[],
                                 writes=[("Sf", h) for h in range(4)], semkey="Sfl")
                        for h in range(4):
                            cast_state(h)
                    ret_core(s, P)
                    if not prompt:
                        self.dma("pool", sts[sq].rearrange("h (dc p) e -> p (h dc) e", p=128), Sf[:, :, :],
                                 reads=[("Sf", h) for h in range(4)], writes=[], semkey="Sfs")
                if prompt and m == NB - 1 and hf == 1:
                    self.dma("pool", stp.rearrange("h (dc p) e -> p (h dc) e", p=128), Sf[:, :, :],
                             reads=[("Sf", h) for h in range(4)], writes=[], semkey="Sfs")
                self.linear_T(ygT, [("ygT", 0), ("ygT", 1)], "w_ret_out", 2048, 128, 0, D, 512, (0, 1), P,
                              lambda s, j, c0, w, a, k: resid_evac(s, j, c0, w, a, k, P), bank_l0)
                def dbg_store(P=P, t=t, m=m, hf=hf, prompt=prompt):
                    if prompt:
                        self.dma("pool", yp[m, hf * 256:(hf + 1) * 256, :].rearrange("(s p) d -> p s d", p=128), xt[:, :, :],
                                 reads=[("xt", 0), ("xt", 1)], writes=[], semkey="o_y")
                    else:
                        j2_ = (t - 2 * NB) * 2
                        self.dma("pool", ys[j2_:j2_ + 2, :, :].rearrange("s p d -> p s d"), xt[:64, :, :],
                                 reads=[("xt", 0), ("xt", 1)], writes=[], semkey="o_y")
                if cfg.dbg == 1:
                    dbg_store()
                ffn(0, P, l0_resid(P), lambda s: xt[:P, s, :], lambda s: ("xt", s), "n0")
                if cfg.dbg == 2:
                    dbg_store()
                norm_T(P, "n0")

                def kv_evac(s, j, c0, w, ps_ap, pk):
                    sk = ("kvss", s)
                    ssa, rsa = self.ss[:P, 4 + s:5 + s], self.rstd[:P, 4 + s:5 + s]
                    S.op("dve", lambda e: e.memset(ssa, 0.0), writes=[sk])
                    S.op("act", lambda e: e.activation(self.junk[:P, 0:256], ps_ap[:, 0:256], AF.Square, accum_out=ssa),
                         reads=[pk], writes=[sk, "junk"])
                    self.rsqrt(rsa, ssa, 1.0 / 256, [sk], [("kvrs", s)])
                    co = ckvo[:P, s, :]
                    S.op("dve", lambda e: e.scalar_tensor_tensor(co, ps_ap[:, 0:256], rsa, gck[:P, :], ALU.mult, ALU.mult),
                         reads=[pk, ("kvrs", s), "gck"], writes=[("ckvo", s)])
                    self.copy("pool", cbf[:P, s, 0:256], co, reads=[("ckvo", s)], writes=[("cbf", s)])
                    x1, x2 = ps_ap[:, 256:272], ps_ap[:, 272:288]
                    cs, sn = cosm[:P, s, :], sinm[:P, s, :]
                    t1, t2, t3, t4 = [k_[:P, :] for k_ in krt]
                    for (tt, xx, tb, i_, tbk) in ((t1, x1, cs, 0, "cosm"), (t2, x2, sn, 1, "sinm"), (t3, x2, cs, 2, "cosm"), (t4, x1, sn, 3, "sinm")):
                        S.op("dve", lambda e, tt=tt, xx=xx, tb=tb: e.tensor_tensor(tt, xx, tb, ALU.mult),
                             reads=[pk, tbk], writes=[("krt", i_)])
                    S.op("pool", lambda e: e.tensor_tensor(kro[:P, s, 0:16], t1, t2, ALU.subtract),
                         reads=[("krt", 0), ("krt", 1)], writes=[("kro", s)])
                    S.op("pool", lambda e: e.tensor_tensor(kro[:P, s, 16:32], t3, t4, ALU.add),
                         reads=[("krt", 2), ("krt", 3)], writes=[("kro", s)])
                    self.copy("pool", cbf[:P, s, 256:288], kro[:P, s, :], reads=[("kro", s)], writes=[("cbf", s)])

                self.linear_T(hT, [("hT", 0), ("hT", 1)], "w_dkv", 1024, 128, 0, 288, 288, (0, 1), P, kv_evac, bank_l0)
                if prompt:
                    r0 = m * 512 + hf * 256
                    self.dma("pool", ckvp[m, hf * 256:(hf + 1) * 256, :].rearrange("(s p) c -> p s c", p=128), ckvo[:, :, :],
                             reads=[("ckvo", 0), ("ckvo", 1)], writes=[], semkey="o_ckv")
                    self.dma("pool", krp[m, hf * 256:(hf + 1) * 256, :].rearrange("(s p) c -> p s c", p=128), kro[:, :, :],
                             reads=[("kro", 0), ("kro", 1)], writes=[], semkey="o_kr")
                    self.dma("pool", cloc[r0:r0 + 256, :].rearrange("(s p) c -> p s c", p=128), cbf[:, :, :],
                             reads=[("cbf", 0), ("cbf", 1)], writes=["cloc"], semkey="o_cb")
                    self.dma("pool", x2s[r0:r0 + 256, :].rearrange("(s p) c -> p s c", p=128), xt[:, :, :],
                             reads=[("xt", 0), ("xt", 1)], writes=["x2s"], semkey="o_x2")
                else:
                    j2 = (t - 2 * NB) * 2
                    self.dma("pool", ckvs[j2:j2 + 2, :, :].rearrange("s p c -> p s c"), ckvo[:64, :, :],
                             reads=[("ckvo", 0), ("ckvo", 1)], writes=[], semkey="o_ckv")
                    self.dma("pool", krs[j2:j2 + 2, :, :].rearrange("s p c -> p s c"), kro[:64, :, :],
                             reads=[("kro", 0), ("kro", 1)], writes=[], semkey="o_kr")
                    self.dma("pool", clocs[j2:j2 + 2, :, :].rearrange("s p c -> p s c"), cbf[:64, :, :],
                             reads=[("cbf", 0), ("cbf", 1)], writes=["clocs"], semkey="o_cb")
                    r0 = NB * 512 + j2 * 64
                    self.dma("pool", x2s[r0:r0 + 128, :].rearrange("(s p) c -> p s c", p=64), xt[:64, :, :],
                             reads=[("xt", 0), ("xt", 1)], writes=["x2s"], semkey="o_x2")
            CR = min(1024, NB * 512)
            for ch in range(NB * 512 // CR):
                i_ap = cloc[ch * CR:(ch + 1) * CR, :]
                o_ap = call[ch * 4 * CR:(ch + 1) * 4 * CR, :]
                S.op("pool", lambda e, i_ap=i_ap, o_ap=o_ap: e.collective_compute(
                    "AllGather", ALU.bypass, replica_groups=[[0, 1, 2, 3], [4, 5, 6, 7]], ins=[i_ap.opt()], outs=[o_ap.opt()]),
                     reads=["cloc"], writes=["call"], dma=True, semkey="cc2", inc=1)
            S.barrier()
            S.mute = cfg.upto < 4 or 4 in cfg.skip

            self.aoff = 0
            wuk = self.view(1024, [128, 2, 1024], BF16)
            wuv = self.view(1024, [128, 2, 1024], BF16)
            ctf = [self.view(1152, [128, 4, 288], F32) for _ in range(2)]
            ctb = [self.view(576, [128, 4, 288], BF16) for _ in range(2)]
            ckT = self.view(512, [128, 2, 512], BF16)
            krT = self.view(256, [32, 512], BF16)
            kst = [self.view(4096, [64, 16, 512], BF16) for _ in range(2)]
            vst = [self.view(2080, [128, 4, 16, 65], BF16) for _ in range(2)]
            self.dma("sp", wuk[:, :, :], self.wb["w_uk"].rearrange("(k p) n -> p k n", p=128), reads=[], writes=["wuk"], semkey="wuk")
            self.dma("sp", wuv[:, :, :], self.wb["w_uv"].rearrange("(k p) n -> p k n", p=128), reads=[], writes=["wuv"], semkey="wuv")
            for sl in range(2):
                v_ = vst[sl]
                for kt_ in range(4):
                    S.op("dve", lambda e, v_=v_, kt_=kt_: e.memset(v_[:, kt_, :, 64:65], 1.0), writes=[("vst", sl)])

            def kv_block(src_fn, KT_dst, V_dst, k0, tiles):
                sl = self.nxt("kvsl", 2)
                src_fn(sl)
                nkeys = sum(tiles)
                cb_ = ctb[sl]
                col = 0
                for kt, nk in enumerate(tiles):
                    self.transposes([cb_[:nk, kt, c * 128:(c + 1) * 128] for c in range(2)], [("ctb", sl)],
                                    ckT[:, :, col:col + nk], ["ckT"], nk, 128)
                    self.transposes([cb_[:nk, kt, 256:288]], [("ctb", sl)],
                                    krT[:, col:col + nk].rearrange("p (a b) -> p a b", a=1), ["krT"], nk, 32)
                    col += nk
                ks = kst[sl]
                for hp in range(8):
                    b = 2 + self.nxt("kvb", 4)
                    for cc in range(2):
                        o_ap, l_ap, r_ap = PS[b][:, 0:nkeys], wuk[:, cc, hp * 128:(hp + 1) * 128], ckT[:, cc, 0:nkeys]
                        S.op("pe", lambda e, o_ap=o_ap, l_ap=l_ap, r_ap=r_ap, cc=cc: e.matmul(o_ap, l_ap, r_ap, start=(cc == 0), stop=(cc == 1)),
                             reads=["wuk", "ckT"], writes=[("ps", b)])
                    self.copy("dve", ks[:, 2 * hp, 0:nkeys], PS[b][0:64, 0:nkeys], reads=[("ps", b)], writes=[("kst", sl)])
                    self.copy("act", ks[:, 2 * hp + 1, 0:nkeys], PS[b][64:128, 0:nkeys], reads=[("ps", b)], writes=[("kst", sl)])
                self.dma("pool", KT_dst[:, 0:64, k0:k0 + nkeys].rearrange("h r k -> r h k"), ks[:, :, 0:nkeys],
                         reads=[("kst", sl)], writes=["KT"], semkey=("kst", sl))
                self.dma("sp", KT_dst[:, 64:96, k0:k0 + nkeys].rearrange("h r k -> r h k"),
                         krT[:, 0:nkeys].unsqueeze(1).to_broadcast([32, 16, nkeys]),
                         reads=["krT"], writes=["KT"], semkey=("krs", sl))
                vs_ = vst[sl]
                col = 0
                for kt, nk in enumerate(tiles):
                    for half in range(2):
                        b = 6 + half
                        for cc in range(2):
                            o_ap, l_ap, r_ap = PS[b][:nk, :], ckT[:, cc, col:col + nk], wuv[:, cc, half * 512:(half + 1) * 512]
                            S.op("pe", lambda e, o_ap=o_ap, l_ap=l_ap, r_ap=r_ap, cc=cc: e.matmul(o_ap, l_ap, r_ap, start=(cc == 0), stop=(cc == 1)),
                                 reads=["wuv", "ckT"], writes=[("ps", b)])
                        self.copy(self.ev_engine(), vs_[:nk, kt, half * 8:(half + 1) * 8, 0:64],
                                  PS[b][:nk, :].rearrange("p (a b) -> p a b", b=64), reads=[("ps", b)], writes=[("vst", sl)])
                    col += nk
                nt = len(tiles)
                if tiles[0] == 128:
                    self.dma("pool", V_dst[k0:k0 + nkeys, :].rearrange("(t p) c -> p t c", p=128),
                             vs_[:, 0:nt, :, :].rearrange("p t a b -> p t (a b)"), reads=[("vst", sl)], writes=["V"], semkey=("vst", sl))
                else:
                    self.dma("pool", V_dst[k0:k0 + nkeys, :], vs_[:nkeys, 0, :, :].rearrange("p a b -> p (a b)"),
                             reads=[("vst", sl)], writes=["V"], semkey=("vst", sl))

            for j in range(4 * NB):
                m_, c_ = j // 4, j % 4
                r0 = ((m_ * 512) // CR) * 4 * CR + c_ * CR + (m_ * 512) % CR

                def src(sl, r0=r0):
                    self.dma("sp", ctb[sl][:, :, :], call[r0:r0 + 512, :].rearrange("(t p) c -> p t c", p=128),
                             reads=["call"], writes=[("ctb", sl)], semkey=("ctb", sl))
                kv_block(src, KTp, Vp, j * 512, [128] * 4)
            for q in range(4):
                for kbk in range(PAST // 512):
                    def src(sl, q=q, kbk=kbk):
                        self.dma("sp", ctf[sl][:, :, 0:256], cckv[q, kbk * 512:(kbk + 1) * 512, :].rearrange("(t p) c -> p t c", p=128),
                                 reads=[], writes=[("ctf", sl)], semkey=("ctf", sl))
                        self.dma("sp", ctf[sl][:, :, 256:288], ckr[q, kbk * 512:(kbk + 1) * 512, :].rearrange("(t p) c -> p t c", p=128),
                                 reads=[], writes=[("ctf", sl)], semkey=("ctf2", sl))
                        for kt_ in range(4):
                            self.copy(self.ev_engine(), ctb[sl][:, kt_, :], ctf[sl][:, kt_, :], reads=[("ctf", sl)], writes=[("ctb", sl)])
                    kv_block(src, KTs[q], Vs[q], kbk * 512, [128] * 4)

                def src(sl, q=q):
                    self.dma("sp", ctb[sl][:64, 0, :], clocs[q], reads=["clocs"], writes=[("ctb", sl)], semkey=("ctb", sl))
                kv_block(src, KTs[q], Vs[q], PAST, [64])
            S.barrier()
            S.mute = cfg.upto < 5

            self.aoff = 0
            xt4 = self.view(4096, [128, 4, D], F32)
            hb = self.view(1024, [128, 2, D], BF16)
            hT = self.view(1024, [128, 8, 256], BF16)
            cqb = self.view(384, [128, 2, 384], BF16)
            cqT = self.view(384, [128, 3, 256], BF16)
            qtok = self.view(3072, [128, 4, 16, 96], BF16)
            QT = self.view(1024, [96, 4, 512], BF16)
            KTt = [self.view(1024, [96, 4, 512], BF16) for _ in range(3)]
            Vt = [self.view(520, [128, 4, 260], BF16) for _ in range(3)]
            PT = [self.view(256, [128, 512], BF16) for _ in range(4)]
            PT2 = [self.view(512, [128, 1024], BF16) for _ in range(3)]
            OTs = [self.view(512, [65, 512], F32) for _ in range(2)]
            oT = self.view(4096, [64, 16, 512], BF16)
            rD = self.view(512, [64, 512], F32)
            hidT = self.view(2816, [128, 22, 256], BF16)
            sgt = self.view(256, [128, 256], F32)
            yo = self.view(2048, [128, 2, D], F32)
            cosm = self.view(32, [128, 2, 16], F32)
            sinm = self.view(32, [128, 2, 16], F32)
            krt = [self.view(16, [128, 16], F32) for _ in range(4)]

            def bank_l1(si, par):
                return 2 + si + 2 * par

            def attention(qc0, Tq, hg, KT_src, V_src, steps, maskinfo):
                if cfg.dbg == 3:
                    return
                LAG = 3
                nsteps = len(steps)
                slot = {}

                def load(j):
                    (k0, tiles, mi) = steps[j]
                    nkeys = sum(tiles)
                    sl = self.nxt("kvt", 3)
                    slot[j] = sl
                    kt_ap, v_ap = KTt[sl], Vt[sl]
                    self.dma("sp", kt_ap[:, :, 0:nkeys], KT_src[hg * 4:(hg + 1) * 4, :, k0:k0 + nkeys].rearrange("h r k -> r h k"),
                             reads=["KT"], writes=[("KTt", sl)], semkey=("KTt", sl))
                    if tiles[0] == 128:
                        self.dma("sp", v_ap[:, 0:len(tiles), :], V_src[k0:k0 + nkeys, hg * 260:(hg + 1) * 260].rearrange("(t p) c -> p t c", p=128),
                                 reads=["V"], writes=[("Vt", sl)], semkey=("Vt", sl))
                    else:
                        self.dma("sp", v_ap[:nkeys, 0, :], V_src[k0:k0 + nkeys, hg * 260:(hg + 1) * 260],
                                 reads=["V"], writes=[("Vt", sl)], semkey=("Vt", sl))

                tl = []
                for j, (k0, tiles, mi) in enumerate(steps):
                    col = 0
                    for kt, nk in enumerate(tiles):
                        for hl in range(4):
                            tl.append((j, kt, nk, col, hl, mi, j == 0 and kt == 0, j == nsteps - 1 and kt == len(tiles) - 1,
                                       kt == len(tiles) - 1 and hl == 3))
                        col += nk
                n = len(tl)
                for j in range(min(3, nsteps)):
                    load(j)
                nextload = 3
                if Tq == 512 and cfg.pairs:
                    units = []
                    for j, (k0, tiles, mi) in enumerate(steps):
                        col = 0
                        for kt, nk in enumerate(tiles):
                            for pr in range(2):
                                units.append((j, kt, nk, col, pr, mi, j == 0 and kt == 0, j == nsteps - 1 and kt == len(tiles) - 1,
                                              kt == len(tiles) - 1 and pr == 1))
                            col += nk
                    nu = len(units)
                    gsl = {}
                    for i in range(nu + 1):
                        if i < nu:
                            (j, kt, nk, col, pr, mi, first, last, endstep) = units[i]
                            sl = slot[j]
                            g = self.nxt("scp", 2)
                            gsl[i] = g
                            for hh in range(2):
                                hl = 2 * pr + hh
                                b_ = 2 * g + hh
                                sc_ap = PS[b_][:nk, 0:512]
                                l_ap, r_ap = KTt[sl][:, hl, col:col + nk], QT[:, hl, qc0:qc0 + 512]
                                msk = mi is not None
                                S.op("pe", lambda e, sc_ap=sc_ap, l_ap=l_ap, r_ap=r_ap, msk=msk: e.matmul(sc_ap, l_ap, r_ap, start=True, stop=not msk),
                                     reads=[("KTt", sl), "QT"], writes=[("ps", b_)])
                                if msk:
                                    l2, r2 = aind[:, kt, 0:nk], bm[:, mi, 0:512]
                                    S.op("pe", lambda e, sc_ap=sc_ap, l2=l2, r2=r2: e.matmul(sc_ap, l2, r2, start=False, stop=True),
                                         reads=["aind", "bm"], writes=[("ps", b_)])
                        t = i - 1
                        if t >= 0:
                            (j, kt, nk, col, pr, mi, first, last, endstep) = units[t]
                            sl = slot[j]
                            g = gsl[t]
                            pi = self.nxt("pt2", 3)
                            p_ap = PT2[pi][:nk, :]
                            sc2 = psall[:nk, 2 * g * 512:(2 * g + 2) * 512]
                            S.op("act", lambda e, p_ap=p_ap, sc2=sc2: e.activation(p_ap, sc2, AF.Exp, scale=MLA_SCALE),
                                 reads=[("ps", 2 * g), ("ps", 2 * g + 1)], writes=[("PT2", pi)])
                            for hh in range(2):
                                hl = 2 * pr + hh
                                ob = 4 + hl
                                o_ap = PS[ob][:65, 0:512]
                                lv = Vt[sl][:nk, kt, hl * 65:(hl + 1) * 65]
                                pp = PT2[pi][:nk, hh * 512:(hh + 1) * 512]
                                S.op("pe", lambda e, o_ap=o_ap, lv=lv, pp=pp, first=first, last=last: e.matmul(o_ap, lv, pp, start=first, stop=last),
                                     reads=[("Vt", sl), ("PT2", pi)], writes=[("ps", ob)])
                            if endstep and nextload < nsteps:
                                load(nextload)
                                nextload += 1
                    n = 0
                scb, pts = {}, {}
                for i in range((n + LAG) if n else 0):
                    if i < n:
                        (j, kt, nk, col, hl, mi, first, last, endstep) = tl[i]
                        sl = slot[j]
                        sb_ = self.nxt("scb", 4)
                        scb[i] = sb_
                        sc_ap = PS[sb_][:nk, 0:Tq]
                        l_ap, r_ap = KTt[sl][:, hl, col:col + nk], QT[:, hl, qc0:qc0 + Tq]
                        msk = mi is not None
                        S.op("pe", lambda e, sc_ap=sc_ap, l_ap=l_ap, r_ap=r_ap, msk=msk: e.matmul(sc_ap, l_ap, r_ap, start=True, stop=not msk),
                             reads=[("KTt", sl), "QT"], writes=[("ps", sb_)])
                        if msk:
                            l2, r2 = aind[:, kt, 0:nk], bm[:, mi, 0:Tq]
                            S.op("pe", lambda e, sc_ap=sc_ap, l2=l2, r2=r2: e.matmul(sc_ap, l2, r2, start=False, stop=True),
                                 reads=["aind", "bm"], writes=[("ps", sb_)])
                    t = i - LAG
                    if t >= 0:
                        (j, kt, nk, col, hl, mi, first, last, endstep) = tl[t]
                        sl = slot[j]
                        sb_ = scb[t]
                        sc_ap = PS[sb_][:nk, 0:Tq]
                        pi = self.nxt("pt", 4)
                        p_ap = PT[pi][:nk, 0:Tq]
                        S.op("act", lambda e, p_ap=p_ap, sc_ap=sc_ap: e.activation(p_ap, sc_ap, AF.Exp, scale=MLA_SCALE),
                             reads=[("ps", sb_)], writes=[("PT", pi)])
                        ob = 4 + hl
                        o_ap = PS[ob][:65, 0:Tq]
                        lv = Vt[sl][:nk, kt, hl * 65:(hl + 1) * 65]
                        S.op("pe", lambda e, o_ap=o_ap, lv=lv, p_ap=p_ap, first=first, last=last: e.matmul(o_ap, lv, p_ap, start=first, stop=last),
                             reads=[("Vt", sl), ("PT", pi)], writes=[("ps", ob)])
                        if endstep and nextload < nsteps:
                            load(nextload)
                            nextload += 1
                for hl in range(4 if cfg.dbg != 4 else 0):
                    ob = 4 + hl
                    oi = self.nxt("ots", 2)
                    os_ap = OTs[oi][:, 0:Tq]
                    self.copy(self.ev_engine(), os_ap, PS[ob][:65, 0:Tq], reads=[("ps", ob)], writes=[("OTs", oi)])
                    drow = OTs[oi][64:65, 0:Tq]
                    S.op("act", lambda e, drow=drow: e.activation(drow, drow, AF.Ln), reads=[("OTs", oi)], writes=[("OTs", oi)])
                    S.op("act", lambda e, drow=drow: e.activation(drow, drow, AF.Exp, scale=-1.0), reads=[("OTs", oi)], writes=[("OTs", oi)])
                    db = self.nxt("scb", 4)
                    d_ap = PS[db][:64, 0:Tq]
                    S.op("pe", lambda e, d_ap=d_ap, os_ap=os_ap: e.matmul(d_ap, esel[:, :], os_ap, start=True, stop=True),
                         reads=["esel", ("OTs", oi)], writes=[("ps", db)])
                    out_ap = oT[:, hg * 4 + hl, qc0:qc0 + Tq]
                    num = OTs[oi][:64, 0:Tq]
                    S.op("dve", lambda e, out_ap=out_ap, num=num, d_ap=d_ap: e.tensor_tensor(out_ap, num, d_ap, ALU.mult),
                         reads=[("OTs", oi), ("ps", db)], writes=[("oT", hg)])

            for blk in range(NB + 1):
                prompt = blk < NB
                P = 128 if prompt else 64
                T4 = 4 * P
                if prompt:
                    src = x2s[blk * 512:(blk + 1) * 512, :].rearrange("(s p) d -> p s d", p=128)
                else:
                    src = x2s[NB * 512:NB * 512 + 256, :].rearrange("(s p) d -> p s d", p=64)
                self.dma("sp", xt4[:P, :, :], src, reads=["x2s"], writes=[("x4", s) for s in range(4)], semkey="xt")
                for hf in range(2):
                    t = 2 * blk + hf
                    self.dma("sp", cosm[:, :, :], cosm_d[t], reads=[], writes=["cosm"], semkey="cosm")
                    self.dma("sp", sinm[:, :, :], sinm_d[t], reads=[], writes=["sinm"], semkey="sinm")
                    self.rms_to_bf16(lambda s: xt4[:P, 2 * hf + s, :], lambda s: [("x4", 2 * hf + s)],
                                     lambda s: hb[:P, s, :], lambda s: [("hb", s)], (0, 1), P, "n1")
                    for s in (0, 1):
                        self.transposes([hb[:P, s, kc * 128:(kc + 1) * 128] for kc in range(8)], [("hb", s)],
                                        hT[:, :, s * P:(s + 1) * P], [("hT", s)], P, 128)

                    def cq_evac(s, j, c0, w, ps_ap, pk):
                        sk = ("cqss", s)
                        ssa, rsa = self.ss[:P, 4 + s:5 + s], self.rstd[:P, 4 + s:5 + s]
                        S.op("dve", lambda e: e.memset(ssa, 0.0), writes=[sk])
                        S.op("act", lambda e: e.activation(self.junk[:P, 0:384], ps_ap, AF.Square, accum_out=ssa),
                             reads=[pk], writes=[sk, "junk"])
                        self.rsqrt(rsa, ssa, 1.0 / 384, [sk], [("cqrs", s)])
                        S.op("act", lambda e: e.activation(cqb[:P, s, :], ps_ap, AF.Copy, scale=rsa),
                             reads=[pk, ("cqrs", s)], writes=[("cqb", s)])
                        self.transposes([cqb[:P, s, c * 128:(c + 1) * 128] for c in range(3)], [("cqb", s)],
                                        cqT[:, :, s * P:(s + 1) * P], [("cqT", s)], P, 128)
                    if cfg.dbg == 5:
                        S.mute = True
                    self.linear_T(hT, [("hT", 0), ("hT", 1)], "w_dq", 1024, 128, 0, 384, 384, (0, 1), P, cq_evac, bank_l1)
                    if cfg.dbg == 6:
                        S.mute = True

                    def q_evac(s, j, c0, w, ps_ap, pk):
                        s4 = 2 * hf + s
                        pv = ps_ap.rearrange("p (a b) -> p a b", b=96)
                        for hh in range(4):
                            self.copy(self.ev_engine(), qtok[:P, s4, 4 * j + hh, 0:64], ps_ap[:, hh * 96:hh * 96 + 64],
                                      reads=[pk], writes=[("qtok", s4)])
                        for hh in range(4):
                            hd = 4 * j + hh
                            a1, a2 = ps_ap[:, hh * 96 + 64:hh * 96 + 80], ps_ap[:, hh * 96 + 80:hh * 96 + 96]
                            c1, s1 = cosm[:P, s, :], sinm[:P, s, :]
                            t1, t2, t3, t4 = [k_[:P, :] for k_ in krt]
                            for (tt, xx, tb, i_, tbk) in ((t1, a1, c1, 0, "cosm"), (t2, a2, s1, 1, "sinm"), (t3, a2, c1, 2, "cosm"), (t4, a1, s1, 3, "sinm")):
                                S.op("dve", lambda e, tt=tt, xx=xx, tb=tb: e.tensor_tensor(tt, xx, tb, ALU.mult),
                                     reads=[pk, tbk], writes=[("krt", i_)])
                            o1, o2 = qtok[:P, s4, hd, 64:80], qtok[:P, s4, hd, 80:96]
                            S.op("dve", lambda e, o1=o1, t1=t1, t2=t2: e.tensor_tensor(o1, t1, t2, ALU.subtract),
                                 reads=[("krt", 0), ("krt", 1)], writes=[("qtok", s4)])
                            S.op("dve", lambda e, o2=o2, t3=t3, t4=t4: e.tensor_tensor(o2, t3, t4, ALU.add),
                                 reads=[("krt", 2), ("krt", 3)], writes=[("qtok", s4)])
                    self.linear_T(cqT, [("cqT", 0), ("cqT", 1)], "w_uq", 384, 128, 0, 1536, 384, (0, 1), P, q_evac, bank_l1)

                if cfg.dbg == 7:
                    S.mute = True
                for hg in range(4):
                    for hl in range(4):
                        hd = hg * 4 + hl
                        b = self.nxt("tpb", 2)
                        pk = ("ps", b)
                        for s4 in range(4):
                            o_ap, s_ap = PSB[b][:96, s4 * P:(s4 + 1) * P], qtok[:P, s4, hd, :]
                            S.op("pe", lambda e, o_ap=o_ap, s_ap=s_ap: e.transpose(o_ap, s_ap, self.ident[:P, :P]),
                                 reads=[("qtok", s4), "ident"], writes=[pk])
                        self.copy(self.ev_engine(), QT[:, hl, 0:T4], PSB[b][:96, 0:T4], reads=[pk], writes=["QT"])
                    if prompt:
                        steps = []
                        for kbk in range(4 * blk + 4):
                            mi = kbk - 4 * blk if kbk >= 4 * blk else None
                            steps.append((kbk * 512, [128] * 4, mi))
                        attention(0, 512, hg, KTp, Vp, steps, None)
                    else:
                        for q in range(4):
                            steps = [(kbk * 512, [128] * 4, None) for kbk in range(PAST // 512)] + [(PAST, [64], None)]
                            attention(q * 64, 64, hg, KTs[q], Vs[q], steps, None)

                if cfg.dbg == 8:
                    S.mute = True
                def o_resid(s, j, c0, w, ps_ap, pk):
                    x_ap = xt4[:P, s, c0:c0 + w]
                    S.op("dve", lambda e, x_ap=x_ap, ps_ap=ps_ap: e.tensor_tensor(x_ap, ps_ap, x_ap, ALU.add),
                         reads=[pk, ("x4", s)], writes=[("x4", s)])
                self.linear_T(oT, [("oT", g_) for g_ in range(4)], "w_o", 1024, 64, 0, D, 512, (0, 1, 2, 3), P, o_resid,
                              lambda si, par: si + 4 * par)

                for hf in range(2):
                    def f_resid(s, c0, w, ps_ap, pk, hf=hf):
                        x_ap = xt4[:P, 2 * hf + s, c0:c0 + w]
                        S.op("dve", lambda e, x_ap=x_ap, ps_ap=ps_ap: e.tensor_tensor(x_ap, ps_ap, x_ap, ALU.add),
                             reads=[pk, ("x4", 2 * hf + s)], writes=[("x4", 2 * hf + s)])
                    ffn(1, P, f_resid, lambda s, hf=hf: xt4[:P, 2 * hf + s, :], lambda s, hf=hf: ("x4", 2 * hf + s), "n1")
                    for s in (0, 1):
                        s4 = 2 * hf + s
                        ssk = ("fss", s)
                        ssa, rsa = self.ss[:P, 6 + s:7 + s], self.rstd[:P, 6 + s:7 + s]
                        xa = xt4[:P, s4, :]
                        S.op("dve", lambda e, ssa=ssa: e.memset(ssa, 0.0), writes=[ssk])
                        S.op("act", lambda e, xa=xa, ssa=ssa: e.activation(self.junk[:P, :D], xa, AF.Square, accum_out=ssa),
                             reads=[("x4", s4)], writes=[ssk, "junk"])
                        self.rsqrt(rsa, ssa, 1.0 / D, [ssk], [("frs", s)])
                        ya = yo[:P, s, :]
                        S.op("dve", lambda e, ya=ya, xa=xa, rsa=rsa: e.scalar_tensor_tensor(ya, xa, rsa, gfn[:P, :], ALU.mult, ALU.mult),
                             reads=[("x4", s4), ("frs", s), "gfn"], writes=[("yo", s)])
                    if prompt:
                        self.dma("pool", yp[blk, hf * 256:(hf + 1) * 256, :].rearrange("(s p) d -> p s d", p=128), yo[:, :, :],
                                 reads=[("yo", 0), ("yo", 1)], writes=[], semkey="o_y")
                    else:
                        self.dma("pool", ys[2 * hf:2 * hf + 2, :, :].rearrange("s p d -> p s d"), yo[:64, :, :],
                                 reads=[("yo", 0), ("yo", 1)], writes=[], semkey="o_y")

            S.emit()
        return self.nc


def _tables(cfg, c):
    NB, NT0, PAST = cfg.NB, cfg.NT0, cfg.PAST
    g = np.array(GAM, dtype=np.float64)
    p = np.arange(128, dtype=np.float64)
    dec = np.zeros((128, 12), np.float64)
    for h in range(4):
        dec[:, h] = g[h] ** (p + 1)
        dec[:, 4 + h] = g[h] ** (127 - p) / 16.0
        dec[:64, 8 + h] = g[h] ** (63 - p[:64]) / 16.0
    m128 = np.zeros((128, 4, 128), np.float64)
    m64 = np.zeros((128, 4, 128), np.float64)
    kk, qq = np.meshgrid(np.arange(128), np.arange(128), indexing="ij")
    for h in range(4):
        m128[:, h, :] = np.where(qq >= kk, g[h] ** -128.0, 0.0)
        m64[:64, h, :64] = np.where(qq[:64, :64] >= kk[:64, :64], g[h] ** -64.0, 0.0)
    pos = np.zeros((NT0, 128, 2), np.float32)
    for t in range(2 * NB):
        m, hf = t // 2, t % 2
        for s in range(2):
            pos[t, :, s] = (4 * m + c) * 512 + hf * 256 + s * 128 + np.arange(128)
    for t in range(2 * NB, NT0):
        pos[t, :64, :] = (PAST + np.arange(64))[:, None]
    tabs = {}
    for nm, half in (("r", 128), ("m", 16)):
        inv = (np.float32(10000.0) ** (-np.arange(half, dtype=np.float32) / np.float32(half))).astype(np.float32)
        ang = (pos[..., None] * inv[None, None, None, :]).astype(np.float32)
        tabs["cos" + nm] = np.cos(ang).astype(np.float32)
        tabs["sin" + nm] = np.sin(ang).astype(np.float32)
    pco = np.zeros((128, 24), np.float64)
    for h in range(4):
        pco[:, h] = g[h] ** (512.0 * c)
        for i in range(4):
            pco[:, 4 + 4 * i + h] = g[h] ** (512.0 * (c - 1 - i)) if i < c else 0.0
    aind = np.zeros((8, 4, 128), np.float32)
    for kt in range(4):
        for k in range(128):
            aind[(kt * 128 + k) // 64, kt, k] = 1.0
    bmk = np.zeros((8, 4, 512), np.float32)
    qch = np.arange(512) // 64
    for i in range(4):
        for r in range(8):
            if i == c:
                bmk[r, i, :] = np.where(qch < r, NEG, 0.0)
            elif i > c:
                bmk[r, i, :] = NEG
    esel = np.zeros((65, 64), np.float32)
    esel[64, :] = 1.0
    out = {
        "dec": dec.astype(np.float32), "m128": m128.astype(np.float32), "m64": m64.astype(np.float32),
        "cosr": tabs["cosr"], "sinr": tabs["sinr"], "cosm": tabs["cosm"], "sinm": tabs["sinm"],
        "pcoef": pco.astype(np.float32), "aind": aind.astype(ml_dtypes.bfloat16), "bm": bmk.astype(ml_dtypes.bfloat16),
        "ident": np.eye(128, dtype=np.float32).astype(ml_dtypes.bfloat16), "esel": esel,
    }
    return out


_PROG = {}


def run(cfg, inputs):
    NB, PAST, SEQ = cfg.NB, cfg.PAST, cfg.SEQ
    key = (NB, PAST, cfg.upto, cfg.dbg)
    if key not in _PROG:
        _PROG[key] = Prog(cfg).build()
    nc = _PROG[key]
    f = lambda a: np.ascontiguousarray(np.asarray(a, dtype=np.float32))
    I = {k: f(v) for k, v in inputs.items()}
    wd = {
        "w_ret_in": I["w_ret_in"][0], "w_ret_out": I["w_ret_out"][0],
        "w_ffn_in0": I["w_ffn_in"][0], "w_ffn_out0": I["w_ffn_out"][0],
        "w_ffn_in1": I["w_ffn_in"][1], "w_ffn_out1": I["w_ffn_out"][1],
        "w_dkv": I["w_dkv"], "w_dq": I["w_dq"][0], "w_uq": I["w_uq"][0], "w_o": I["w_mla_out"][0],
        "w_uk": I["w_uk"].reshape(256, 1024), "w_uv": I["w_uv"].reshape(256, 1024),
    }
    gl = [I["g_mix"][0], I["g_ffn"][0], I["g_ffn"][1], I["g_kv_in"], I["g_mix"][1]]
    gtab = np.zeros((128, 43), np.float32)
    for i, gv in enumerate(gl):
        gtab[:, 8 * i:8 * i + 8] = gv.reshape(8, 128).T
    gtab[:, 40:43] = I["g_q"][0].reshape(3, 128).T
    gckv = np.ascontiguousarray(np.broadcast_to(I["g_ckv"][None, :], (128, 256)))
    gfin = np.ascontiguousarray(np.broadcast_to(I["g_final"][None, :], (128, D)))
    in_maps = []
    for core in range(8):
        b, c = core // 4, core % 4
        d = dict(wd)
        d["xp"] = np.ascontiguousarray(I["x_prompt"][b].reshape(SEQ // 512, 512, D)[c::4])
        d["xs"] = np.ascontiguousarray(I["x_sample"][4 * core:4 * core + 4])
        d["st0"] = np.ascontiguousarray(I["state_ret"][0, 4 * core:4 * core + 4])
        d["cckv"] = np.ascontiguousarray(I["cache_ckv"][4 * core:4 * core + 4])
        d["ckr"] = np.ascontiguousarray(I["cache_krope"][4 * core:4 * core + 4])
        d["gtab"], d["gckv"], d["gfin"] = gtab, gckv, gfin
        d.update(_tables(cfg, c))
        in_maps.append(d)
    res = run_bass_kernel_spmd(nc, in_maps, core_ids=list(range(8)))
    R = res.results
    y_p = np.zeros((2, SEQ, D), np.float32)
    ckv_p = np.zeros((2, SEQ, 256), np.float32)
    kr_p = np.zeros((2, SEQ, 32), np.float32)
    st_p = np.zeros((1, 2, 4, 256, 512), np.float32)
    y_s = np.zeros((32, 64, D), np.float32)
    st_s = np.zeros((1, 32, 4, 256, 512), np.float32)
    ckv_s = np.zeros((32, 64, 256), np.float32)
    kr_s = np.zeros((32, 64, 32), np.float32)
    for core in range(8):
        b, c = core // 4, core % 4
        r = R[core]
        y_p[b].reshape(SEQ // 512, 512, D)[c::4] = np.asarray(r["yp"], dtype=np.float32)
        ckv_p[b].reshape(SEQ // 512, 512, 256)[c::4] = np.asarray(r["ckvp"], dtype=np.float32)
        kr_p[b].reshape(SEQ // 512, 512, 32)[c::4] = np.asarray(r["krp"], dtype=np.float32)
        if c == 3:
            st_p[0, b] = np.asarray(r["stp"], dtype=np.float32)
        y_s[4 * core:4 * core + 4] = np.asarray(r["ys"], dtype=np.float32)
        st_s[0, 4 * core:4 * core + 4] = np.asarray(r["sts"], dtype=np.float32)
        ckv_s[4 * core:4 * core + 4] = np.asarray(r["ckvs"], dtype=np.float32)
        kr_s[4 * core:4 * core + 4] = np.asarray(r["krs"], dtype=np.float32)
    return (y_p, y_s, st_p, st_s, ckv_p, kr_p, ckv_s, kr_s)


def kernel(**inputs):
    return run(Cfg(8, 4096), inputs)
```

```python
import contextlib
import numpy as np
import ml_dtypes
import concourse.bass as bass
import concourse.mybir as mybir
from concourse.bass_utils import run_bass_kernel_spmd

F32 = mybir.dt.float32
BF16 = mybir.dt.bfloat16
AF = mybir.ActivationFunctionType
ALU = mybir.AluOpType

D = 1024
RET_IN = 6144
DFF = 2816
EPS = 1e-6
NEG = -30000.0
MLA_SCALE = 96 ** -0.5
GAM = [1.0 - 2.0 ** (-5.0 - h) for h in range(4)]

ALL_ENG = ("pe", "act", "dve", "pool", "sp")


class Op:
    __slots__ = ("eng", "fn", "deps", "dma", "semkey", "ticket", "signal", "idx", "inc", "raw")

    def __init__(self, eng, fn, dma, semkey, idx, inc):
        self.eng, self.fn, self.dma, self.semkey, self.idx, self.inc = eng, fn, dma, semkey, idx, inc
        self.deps = []
        self.raw = set()
        self.ticket = None
        self.signal = False


class _Rec:
    def __getattr__(self, name):
        def f(*a, **k):
            self.call = (name, a, k)
            return self
        return f


class Sched:
    def __init__(self, nc):
        self.nc = nc
        self.ops = []
        self.last_w = {}
        self.readers = {}
        self.last_dma_on_sem = {}
        self.last_compute = {}
        self.mute = False

    def op(self, eng, fn, reads=(), writes=(), dma=False, semkey=None, inc=None):
        if self.mute:
            return None
        rec = _Rec()
        fn(rec)
        o = Op(eng, rec.call, dma, semkey, len(self.ops), inc if inc is not None else (16 if dma else 1))
        deps = {}
        raw = set()
        for r in reads:
            lw = self.last_w.get(r)
            if lw is not None:
                deps[lw.idx] = lw
                raw.add(lw.idx)
        for w in writes:
            lw = self.last_w.get(w)
            if lw is not None:
                deps[lw.idx] = lw
            for rd in self.readers.get(w, ()):
                deps[rd.idx] = rd
        if dma:
            prev = self.last_dma_on_sem.get(semkey)
            if prev is not None:
                deps[prev.idx] = prev
            self.last_dma_on_sem[semkey] = o
        else:
            self.last_compute[eng] = o
        o.deps = list(deps.values())
        o.raw = raw if eng != "pe" else set()
        for r in reads:
            self.readers.setdefault(r, []).append(o)
        for w in writes:
            self.last_w[w] = o
            self.readers[w] = []
        self.ops.append(o)
        return o

    def barrier(self):
        if self.mute:
            return
        lasts = list(self.last_compute.values()) + list(self.last_dma_on_sem.values())
        for e in ALL_ENG:
            o = Op(e, None, False, None, len(self.ops), 1)
            o.deps = list(lasts)
            self.ops.append(o)
        self.last_w = {}
        self.readers = {}

    def emit(self):
        nc = self.nc
        ops = self.ops
        for o in ops:
            for d in o.deps:
                if d.dma or d.eng != o.eng or o.dma or d.idx in o.raw:
                    d.signal = True
        for o in ops:
            if o.dma:
                o.signal = True
        cnt = {}
        for o in ops:
            if not o.signal or o.fn is None:
                o.signal = False if o.fn is None else o.signal
                continue
            key = ("dma", o.semkey) if o.dma else ("eng", o.eng)
            cnt[key] = cnt.get(key, 0) + o.inc
            o.ticket = (key, cnt[key])
        keys = list(cnt.keys())
        self.n_sems = len(keys)
        print("sched: %d ops, %d semaphores" % (len(ops), len(keys)))
        with contextlib.ExitStack() as st:
            sems = {}
            for i, k in enumerate(keys):
                sems[k] = st.enter_context(nc.semaphore("s%d" % i))
            block = st.enter_context(nc.Block())
            per_eng = {e: [o for o in ops if o.eng == e] for e in ALL_ENG}
            final = dict(cnt)

            def run(engname, engobj):
                waited = {}
                for o in per_eng[engname]:
                    need = {}
                    for d in o.deps:
                        if d.ticket is None:
                            continue
                        if (not d.dma) and d.eng == engname and not o.dma and d.idx not in o.raw:
                            continue
                        k, v = d.ticket
                        if need.get(k, 0) < v:
                            need[k] = v
                    for k, v in need.items():
                        if waited.get(k, 0) >= v:
                            continue
                        engobj.wait_ge(sems[k], v)
                        waited[k] = v
                    if o.fn is None:
                        continue
                    name_, a_, k_ = o.fn
                    ins = getattr(engobj, name_)(*a_, **k_)
                    if o.signal:
                        ins.then_inc(sems[o.ticket[0]], o.inc)
                if engname == "sp":
                    for k, v in final.items():
                        engobj.wait_ge(sems[k], v)

            block.sync(lambda e: run("sp", e))
            block.tensor(lambda e: run("pe", e))
            block.scalar(lambda e: run("act", e))
            block.vector(lambda e: run("dve", e))
            block.gpsimd(lambda e: run("pool", e))


class Cfg:
    def __init__(self, nb=8, past=4096, upto=9, dbg=0):
        self.upto = upto
        self.dbg = dbg
        self.pairs = False
        import os
        self.skip = [int(v) for v in os.environ.get("KSKIP", "").split(",") if v]
        self.NB = nb
        self.SEQ = nb * 4 * 512
        self.PAST = past
        self.NT0 = 2 * nb + 2


W_SPECS = [
    ("w_ret_in", 1024, 6144, 0), ("w_ret_out", 2048, 1024, None),
    ("w_ffn_in0", 1024, 5632, 8), ("w_ffn_out0", 2816, 1024, None),
    ("w_ffn_in1", 1024, 5632, 16), ("w_ffn_out1", 2816, 1024, None),
    ("w_dkv", 1024, 288, 24), ("w_dq", 1024, 384, 32), ("w_uq", 384, 1536, 40),
    ("w_o", 1024, 1024, None), ("w_uk", 256, 1024, None), ("w_uv", 256, 1024, None),
]


class Prog:
    def __init__(self, cfg):
        self.cfg = cfg
        self.nc = bass.Bass("TRN2", target_bir_lowering=False)
        self.S = Sched(self.nc)
        self.rr = 0
        self.wslot = 0
        self.ctr = {}

    def din(self, name, shape, dt=F32):
        return self.nc.dram_tensor(name, list(shape), dt, kind="ExternalInput").ap()

    def dout(self, name, shape, dt=F32):
        return self.nc.dram_tensor(name, list(shape), dt, kind="ExternalOutput").ap()

    def dint(self, name, shape, dt):
        return self.nc.dram_tensor(name, list(shape), dt).ap()

    def view(self, nf32, shape, dt):
        off = self.aoff
        self.aoff += nf32
        assert self.aoff <= self.ARENA, (self.aoff, self.ARENA)
        a = self.AR[:shape[0], off:off + nf32]
        if dt != F32:
            a = a.bitcast(dt)
        if len(shape) == 3:
            a = a.rearrange("p (a b) -> p a b", b=shape[2])
        elif len(shape) == 4:
            a = a.rearrange("p (a b c) -> p a b c", b=shape[2], c=shape[3])
        return a

    def nxt(self, name, n):
        v = self.ctr.get(name, 0)
        self.ctr[name] = v + 1
        return v % n

    def dma(self, eng, out, in_, reads, writes, semkey):
        self.S.op(eng, lambda e: e.dma_start(out=out, in_=in_), reads=reads, writes=writes, dma=True, semkey=semkey)

    def ev_engine(self):
        self.rr += 1
        return ("dve", "act")[self.rr % 2]

    def copy(self, eng, out, in_, reads, writes):
        if eng == "act":
            self.S.op("act", lambda e: e.activation(out, in_, AF.Copy), reads=reads, writes=writes)
        else:
            self.S.op(eng, lambda e: e.tensor_copy(out, in_), reads=reads, writes=writes)

    def wslab(self, wname, r0, nk, kp, c0, ncols):
        slot = self.wslot % 4
        self.wslot += 1
        buf = self.WB[slot]
        key = ("W", slot)
        dst = buf[:kp, 0:nk * ncols].rearrange("p (k n) -> p k n", n=ncols)
        src = self.wb[wname][r0:r0 + nk * kp, c0:c0 + ncols].rearrange("(k p) n -> p k n", p=kp)
        self.dma("sp", dst, src, reads=[("wb", wname)], writes=[key], semkey=key)
        return dst, key

    def transposes(self, srcs, src_keys, dst, dst_keys, P, rows):
        n = len(srcs)
        for g0 in range(0, n, 8):
            g1 = min(n, g0 + 8)
            b = self.nxt("tpb", 2)
            pk = ("ps", b)
            pb = self.PSB[b]
            for i in range(g0, g1):
                o_ap = pb[:rows, (i - g0) * P:(i - g0 + 1) * P]
                s_ap = srcs[i]
                self.S.op("pe", lambda e, o_ap=o_ap, s_ap=s_ap: e.transpose(o_ap, s_ap, self.ident[:P, :P]),
                          reads=list(src_keys) + ["ident"], writes=[pk])
            src_v = pb[:rows, 0:(g1 - g0) * P].rearrange("p (a b) -> p a b", b=P)
            self.copy(self.ev_engine(), dst[:rows, g0:g1, :], src_v, reads=[pk], writes=list(dst_keys))

    def rsqrt(self, dst, src, scale, rkeys, wkeys):
        self.S.op("dve", lambda e: e.tensor_scalar(dst, src, scale, EPS, ALU.mult, ALU.add), reads=list(rkeys), writes=list(wkeys))
        self.S.op("act", lambda e: e.activation(dst, dst, AF.Sqrt), reads=list(wkeys), writes=list(wkeys))
        self.S.op("dve", lambda e: e.reciprocal(dst, dst), reads=list(wkeys), writes=list(wkeys))

    def rms_to_bf16(self, x_ap_fn, xkeys, out_fn, okeys, subs, P, tag):
        for s in subs:
            ssk = (tag, "ss", s)
            ss = self.ss[:P, s:s + 1]
            rs = self.rstd[:P, s:s + 1]
            self.S.op("dve", lambda e, ss=ss: e.memset(ss, 0.0), writes=[ssk])
            xa = x_ap_fn(s)
            self.S.op("act", lambda e, xa=xa, ss=ss: e.activation(self.junk[:P, :D], xa, AF.Square, accum_out=ss),
                      reads=list(xkeys(s)), writes=[ssk, "junk"])
            self.rsqrt(rs, ss, 1.0 / D, [ssk], [(tag, "rs", s)])
            oa = out_fn(s)
            self.S.op("act", lambda e, oa=oa, xa=xa, rs=rs: e.activation(oa, xa, AF.Copy, scale=rs),
                      reads=list(xkeys(s)) + [(tag, "rs", s)], writes=list(okeys(s)))

    def linear_T(self, actT, act_keys, wname, K, kp, col0, ncols_total, slabw, subs, P, evac, bank_of):
        nkc = K // kp
        kgs = [(a, min(8, nkc - a)) for a in range(0, nkc, 8)]
        j = 0
        for c0 in range(0, ncols_total, slabw):
            w = min(slabw, ncols_total - c0)
            par = self.nxt("linpar", 2)
            for gi, (k0, nk) in enumerate(kgs):
                wap, wkey = self.wslab(wname, k0 * kp, nk, kp, col0 + c0, w)
                for si, s in enumerate(subs):
                    b = bank_of(si, par)
                    pk = ("ps", b)
                    for kc in range(nk):
                        first = (gi == 0 and kc == 0)
                        last = (gi == len(kgs) - 1 and kc == nk - 1)
                        o_ap = self.PS[b][:P, 0:w]
                        l_ap = actT[:kp, k0 + kc, si * P:(si + 1) * P]
                        r_ap = wap[:kp, kc, :]
                        self.S.op("pe", lambda e, o_ap=o_ap, l_ap=l_ap, r_ap=r_ap, first=first, last=last:
                                  e.matmul(o_ap, l_ap, r_ap, start=first, stop=last),
                                  reads=list(act_keys) + [wkey], writes=[pk])
            for si, s in enumerate(subs):
                b = bank_of(si, par)
                evac(s, j, c0, w, self.PS[b][:P, 0:w], ("ps", b))
            j += 1

    def build(self):
        cfg, nc, S = self.cfg, self.nc, self.S
        NB, NT0, PAST, SEQ = cfg.NB, cfg.NT0, cfg.PAST, cfg.SEQ
        NKP = SEQ
        NKS = PAST + 64
        xp = self.din("xp", [NB, 512, D])
        xs = self.din("xs", [4, 64, D])
        st0 = self.din("st0", [4, 4, 256, 512])
        cckv = self.din("cckv", [4, PAST, 256])
        ckr = self.din("ckr", [4, PAST, 32])
        win = {}
        for name, K, N, g in W_SPECS:
            win[name] = self.din(name, [K, N])
        gtab = self.din("gtab", [128, 43])
        gckv = self.din("gckv", [128, 256])
        gfin = self.din("gfin", [128, D])
        dec_d = self.din("dec", [128, 12])
        m128_d = self.din("m128", [128, 4, 128])
        m64_d = self.din("m64", [128, 4, 128])
        cosr_d = self.din("cosr", [NT0, 128, 2, 128])
        sinr_d = self.din("sinr", [NT0, 128, 2, 128])
        cosm_d = self.din("cosm", [NT0, 128, 2, 16])
        sinm_d = self.din("sinm", [NT0, 128, 2, 16])
        pcoef_d = self.din("pcoef", [128, 24])
        aind_d = self.din("aind", [8, 4, 128], BF16)
        bm_d = self.din("bm", [8, 4, 512], BF16)
        ident_d = self.din("ident", [128, 128], BF16)
        esel_d = self.din("esel", [65, 64])

        yp = self.dout("yp", [NB, 512, D])
        ys = self.dout("ys", [4, 64, D])
        stp = self.dout("stp", [4, 256, 512])
        sts = self.dout("sts", [4, 4, 256, 512])
        ckvp = self.dout("ckvp", [NB, 512, 256])
        krp = self.dout("krp", [NB, 512, 32])
        ckvs = self.dout("ckvs", [4, 64, 256])
        krs = self.dout("krs", [4, 64, 32])

        self.wb = {name: self.dint("wb_" + name, [K, N], BF16) for name, K, N, g in W_SPECS}
        Lloc = self.dint("Lloc", [NB * 128, 4096], F32)
        Lall = self.dint("Lall", [4 * NB * 128, 4096], F32)
        Sst = self.dint("Sst", [NB, 128, 4096], F32)
        x2s = self.dint("x2s", [NB * 512 + 256, D], F32)
        cloc = self.dint("cloc", [NB * 512, 288], BF16)
        call = self.dint("call", [4 * NB * 512, 288], BF16)
        clocs = self.dint("clocs", [4, 64, 288], BF16)
        KTp = self.dint("KTp", [16, 96, NKP], BF16)
        Vp = self.dint("Vp", [NKP, 1040], BF16)
        KTs = self.dint("KTs", [4, 16, 96, NKS], BF16)
        Vs = self.dint("Vs", [4, NKS, 1040], BF16)

        st = contextlib.ExitStack()
        with st:
            sb = lambda name, shape, dt: st.enter_context(nc.sbuf_tensor("s_" + name, list(shape), dt))
            self.ident = sb("ident", [128, 128], BF16)
            dec = sb("dec", [128, 12], F32)
            m128 = sb("m128", [128, 4, 128], F32)
            m64 = sb("m64", [128, 4, 128], F32)
            pcoef = sb("pcoef", [128, 24], F32)
            gck = sb("gck", [128, 256], F32)
            gfn = sb("gfn", [128, D], F32)
            gt = sb("gt", [128, 43], F32)
            aind = sb("aind", [8, 4, 128], BF16)
            bm = sb("bm", [8, 4, 512], BF16)
            esel = sb("esel", [65, 64], F32)
            self.ss = sb("ss", [128, 8], F32)
            self.rstd = sb("rstd", [128, 8], F32)
            ssy = sb("ssy", [128, 8], F32)
            rsy = sb("rsy", [128, 8], F32)
            self.junk = sb("junk", [128, D], BF16)
            self.WB = [sb("wb%d" % i, [128, 4096], BF16) for i in range(4)]
            self.ARENA = 33280
            self.AR = sb("arena", [128, self.ARENA], F32)
            psall = st.enter_context(nc.psum_tensor("psall", [128, 4096], F32))
            self.PS = [psall[:, i * 512:(i + 1) * 512] for i in range(8)]
            self.PSB = [p.bitcast(BF16) for p in self.PS]
            PS, PSB = self.PS, self.PSB

            for (t, d_, k) in ((self.ident, ident_d, "ident"), (dec, dec_d, "dec"), (m128, m128_d, "m128"),
                               (m64, m64_d, "m64"), (pcoef, pcoef_d, "pcoef"), (gck, gckv, "gck"),
                               (gfn, gfin, "gfn"), (gt, gtab, "gt"), (aind, aind_d, "aind"), (bm, bm_d, "bm"),
                               (esel, esel_d, "esel")):
                nd = len(t.shape)
                self.dma("sp", t[tuple([slice(None)] * nd)], d_, reads=[], writes=[k], semkey=("c", self.nxt("csem", 2)))

            CW = 1536
            self.aoff = self.ARENA - 3 * (CW + CW // 2)
            conv_base = self.aoff
            cin = [self.view(CW, [128, CW], F32) for _ in range(3)]
            cout = [self.view(CW // 2, [128, CW], BF16) for _ in range(3)]
            conv_state = {"ci": 0}

            def conv_chunk(name, g, kc, n0, w):
                ci = conv_state["ci"]
                sl = ci % 3
                conv_state["ci"] = ci + 1
                self.dma("sp", cin[sl][:, 0:w], win[name][kc * 128:(kc + 1) * 128, n0:n0 + w],
                         reads=[], writes=[("cin", sl)], semkey=("cin", sl))
                eng = ("dve", "pool", "act")[ci % 3] if g is None else ("dve", "act")[ci % 2]
                o_ap, i_ap = cout[sl][:, 0:w], cin[sl][:, 0:w]
                if g is None:
                    self.copy(eng, o_ap, i_ap, reads=[("cin", sl)], writes=[("cout", sl)])
                else:
                    sc = gt[:, g + kc:g + kc + 1]
                    if eng == "act":
                        S.op("act", lambda e: e.activation(o_ap, i_ap, AF.Copy, scale=sc),
                             reads=[("cin", sl), "gt"], writes=[("cout", sl)])
                    else:
                        S.op(eng, lambda e: e.tensor_scalar(o_ap, i_ap, sc, None, ALU.mult),
                             reads=[("cin", sl), "gt"], writes=[("cout", sl)])
                self.dma("pool", self.wb[name][kc * 128:(kc + 1) * 128, n0:n0 + w], o_ap,
                         reads=[("cout", sl)], writes=[("wb", name)], semkey="wst")

            conv_todo = []
            for name, K, N, g in W_SPECS:
                for kc in range(K // 128):
                    for n0 in range(0, N, CW):
                        conv_todo.append((name, g, kc, n0, min(CW, N - n0)))
            n_first = sum(1 for c_ in conv_todo if c_[0] == "w_ret_in")
            for c_ in conv_todo[:n_first]:
                conv_chunk(*c_)
            conv_todo = conv_todo[n_first:]
            S.mute = cfg.upto < 1 or 1 in cfg.skip

            self.aoff = 0
            xt = self.view(2048, [128, 2, D], F32)
            hb = self.view(1024, [128, 2, D], BF16)
            hT = self.view(1024, [128, 8, 256], BF16)
            qb = self.view(1024, [128, 2, D], BF16)
            kb = self.view(1024, [128, 2, D], BF16)
            vb = self.view(2048, [128, 2, 2048], BF16)
            sgb = self.view(2048, [128, 2, 2048], BF16)
            qT = self.view(512, [128, 8, 128], BF16)
            kT = self.view(512, [128, 8, 128], BF16)
            aTs = self.view(256, [128, 4, 128], BF16)
            ygb = self.view(1024, [128, 2048], BF16)
            ygT = self.view(2048, [128, 16, 256], BF16)
            hidT = self.view(2816, [128, 22, 256], BF16)
            cosr = self.view(256, [128, 2, 128], F32)
            sinr = self.view(256, [128, 2, 128], F32)
            cosm = self.view(32, [128, 2, 16], F32)
            sinm = self.view(32, [128, 2, 16], F32)
            rt = [self.view(128, [128, 128], F32) for _ in range(8)]
            Sf = self.view(4096, [128, 8, 512], F32)
            Sb = self.view(2048, [128, 8, 512], BF16)
            sgt = self.view(256, [128, 256], F32)
            ckvo = self.view(512, [128, 2, 256], F32)
            kro = self.view(64, [128, 2, 32], F32)
            cbf = self.view(288, [128, 2, 288], BF16)
            krt = [self.view(16, [128, 16], F32) for _ in range(4)]
            assert self.aoff <= conv_base, (self.aoff, conv_base)

            def bank_l0(si, par):
                return 2 + si + 2 * par

            def x_src(t):
                if t < 2 * NB:
                    m, hf = t // 2, t % 2
                    return xp[m, hf * 256:(hf + 1) * 256, :].rearrange("(s p) d -> p s d", p=128), 128
                j = t - 2 * NB
                return xs[2 * j:2 * j + 2, :, :].rearrange("s p d -> p s d"), 64

            def load_x(t):
                src, P = x_src(t)
                self.dma("sp", xt[:P, :, :], src, reads=[], writes=[("xt", 0), ("xt", 1)], semkey="xt")
                return P

            def load_rope(t):
                self.dma("sp", cosr[:, :, :], cosr_d[t], reads=[], writes=["cosr"], semkey="cosr")
                self.dma("sp", sinr[:, :, :], sinr_d[t], reads=[], writes=["sinr"], semkey="sinr")
                self.dma("sp", cosm[:, :, :], cosm_d[t], reads=[], writes=["cosm"], semkey="cosm")
                self.dma("sp", sinm[:, :, :], sinm_d[t], reads=[], writes=["sinm"], semkey="sinm")

            def norm_T(P, tag):
                self.rms_to_bf16(lambda s: xt[:P, s, :], lambda s: [("xt", s)],
                                 lambda s: hb[:P, s, :], lambda s: [("hb", s)], (0, 1), P, tag)
                for s in (0, 1):
                    self.transposes([hb[:P, s, kc * 128:(kc + 1) * 128] for kc in range(8)], [("hb", s)],
                                    hT[:, :, s * P:(s + 1) * P], [("hT", s)], P, 128)

            def rope_evac(ps_ap, pk, s, j, dst, dkey, P, dcol):
                for hh in range(2):
                    h = 2 * (j % 2) + hh
                    dsc = dec[:P, dcol + h:dcol + h + 1]
                    x1 = ps_ap[:, hh * 256:hh * 256 + 128]
                    x2 = ps_ap[:, hh * 256 + 128:hh * 256 + 256]
                    cs = cosr[:P, s, :]
                    sn = sinr[:P, s, :]
                    i0 = self.nxt("rt", 2) * 4
                    t1, t2, t3, t4 = rt[i0][:P, :], rt[i0 + 1][:P, :], rt[i0 + 2][:P, :], rt[i0 + 3][:P, :]
                    rk = [("rt", i0 + i) for i in range(4)]
                    for (tt, xx, tb, tk, tbk) in ((t1, x1, cs, rk[0], "cosr"), (t2, x2, sn, rk[1], "sinr"),
                                                  (t3, x2, cs, rk[2], "cosr"), (t4, x1, sn, rk[3], "sinr")):
                        S.op("dve", lambda e, tt=tt, xx=xx, tb=tb, dsc=dsc: e.scalar_tensor_tensor(tt, xx, dsc, tb, ALU.mult, ALU.mult),
                             reads=[pk, "dec", tbk], writes=[tk])
                    c = (j % 2) * 512 + hh * 256
                    o1 = dst[:P, s, c:c + 128]
                    o2 = dst[:P, s, c + 128:c + 256]
                    S.op("pool", lambda e, o1=o1, t1=t1, t2=t2: e.tensor_tensor(o1, t1, t2, ALU.subtract),
                         reads=[rk[0], rk[1]], writes=[dkey])
                    S.op("pool", lambda e, o2=o2, t3=t3, t4=t4: e.tensor_tensor(o2, t3, t4, ALU.add),
                         reads=[rk[2], rk[3]], writes=[dkey])

            def proj_qkvg(P, pass1):
                dk_col = 4 if P == 128 else 8

                def evac(s, j, c0, w, ps_ap, pk):
                    jj = c0 // 512
                    if jj < 2:
                        rope_evac(ps_ap, pk, s, jj, qb, ("qb", s), P, 0)
                    elif jj < 4:
                        rope_evac(ps_ap, pk, s, jj, kb, ("kb", s), P, dk_col)
                    elif jj < 8:
                        o_ap = vb[:P, s, (jj - 4) * 512:(jj - 3) * 512]
                        self.copy(self.ev_engine(), o_ap, ps_ap, reads=[pk], writes=[("vb", s)])
                    else:
                        o_ap = sgb[:P, s, (jj - 8) * 512:(jj - 7) * 512]
                        S.op("act", lambda e, o_ap=o_ap, ps_ap=ps_ap: e.activation(o_ap, ps_ap, AF.Silu),
                             reads=[pk], writes=[("sgb", s)])
                if pass1:
                    self.linear_T(hT, [("hT", 0), ("hT", 1)], "w_ret_in", 1024, 128, 1024, 3072, 512, (0, 1), P,
                                  lambda s, j, c0, w, a, k: evac(s, j, c0 + 1024, w, a, k), bank_l0)
                else:
                    self.linear_T(hT, [("hT", 0), ("hT", 1)], "w_ret_in", 1024, 128, 0, 6144, 512, (0, 1), P, evac, bank_l0)

            def state_update(s, P, h):
                gP = GAM[h] ** P
                for dc in range(2):
                    b = 5 + dc
                    o_ap = PS[b][:, :]
                    l_ap = kb[:P, s, (2 * h + dc) * 128:(2 * h + dc + 1) * 128]
                    r_ap = vb[:P, s, h * 512:(h + 1) * 512]
                    S.op("pe", lambda e, o_ap=o_ap, l_ap=l_ap, r_ap=r_ap: e.matmul(o_ap, l_ap, r_ap, start=True, stop=True),
                         reads=[("kb", s), ("vb", s)], writes=[("ps", b)])
                    sf = Sf[:, 2 * h + dc, :]
                    S.op("dve", lambda e, sf=sf, o_ap=o_ap, gP=gP: e.scalar_tensor_tensor(sf, sf, gP, o_ap, ALU.mult, ALU.add),
                         reads=[("ps", b), ("Sf", h)], writes=[("Sf", h)])

            def cast_state(h):
                for dc in range(2):
                    self.copy("pool", Sb[:, 2 * h + dc, :], Sf[:, 2 * h + dc, :], reads=[("Sf", h)], writes=[("Sb", h)])

            def ret_core(s, P):
                msk = m128 if P == 128 else m64
                mk = "m128" if P == 128 else "m64"
                self.transposes([qb[:P, s, c * 128:(c + 1) * 128] for c in range(8)], [("qb", s)], qT[:, :, :P], ["qT"], P, 128)
                self.transposes([kb[:P, s, c * 128:(c + 1) * 128] for c in range(8)], [("kb", s)], kT[:, :, :P], ["kT"], P, 128)
                for h in range(4):
                    for dc in range(2):
                        o_ap = PS[2][:P, h * 128:h * 128 + P]
                        l_ap, r_ap = kT[:, 2 * h + dc, :P], qT[:, 2 * h + dc, :P]
                        S.op("pe", lambda e, o_ap=o_ap, l_ap=l_ap, r_ap=r_ap, dc=dc: e.matmul(o_ap, l_ap, r_ap, start=(dc == 0), stop=(dc == 1)),
                             reads=["qT", "kT"], writes=[("ps", 2)])
                    a_ap = aTs[:P, h, :P]
                    p_ap = PS[2][:P, h * 128:h * 128 + P]
                    m_ap = msk[:P, h, :P]
                    S.op("dve", lambda e, a_ap=a_ap, p_ap=p_ap, m_ap=m_ap: e.tensor_tensor(a_ap, p_ap, m_ap, ALU.mult),
                         reads=[("ps", 2), mk], writes=[("aTs", h)])
                    yb = 3 + (h % 2)
                    y_ap = PS[yb][:P, :]
                    r_ap = vb[:P, s, h * 512:(h + 1) * 512]
                    S.op("pe", lambda e, y_ap=y_ap, a_ap=a_ap, r_ap=r_ap: e.matmul(y_ap, a_ap, r_ap, start=True, stop=False),
                         reads=[("aTs", h), ("vb", s)], writes=[("ps", yb)])
                    for dc in range(2):
                        l_ap, r2 = qT[:, 2 * h + dc, :P], Sb[:, 2 * h + dc, :]
                        S.op("pe", lambda e, y_ap=y_ap, l_ap=l_ap, r2=r2, dc=dc: e.matmul(y_ap, l_ap, r2, start=False, stop=(dc == 1)),
                             reads=["qT", ("Sb", h)], writes=[("ps", yb)])
                    state_update(s, P, h)
                    cast_state(h)
                    sy = ssy[:P, h:h + 1]
                    ry = rsy[:P, h:h + 1]
                    S.op("dve", lambda e, sy=sy: e.memset(sy, 0.0), writes=[("ssy", h)])
                    S.op("act", lambda e, y_ap=y_ap, sy=sy: e.activation(self.junk[:P, 0:512], y_ap, AF.Square, accum_out=sy),
                         reads=[("ps", yb)], writes=[("ssy", h), "junk"])
                    self.rsqrt(ry, sy, 1.0 / 512, [("ssy", h)], [("rsy", h)])
                    g_ap = sgb[:P, s, h * 512:(h + 1) * 512]
                    o_ap = ygb[:P, h * 512:(h + 1) * 512]
                    S.op("dve", lambda e, o_ap=o_ap, y_ap=y_ap, ry=ry, g_ap=g_ap: e.scalar_tensor_tensor(o_ap, y_ap, ry, g_ap, ALU.mult, ALU.mult),
                         reads=[("ps", yb), ("rsy", h), ("sgb", s)], writes=["ygb"])
                self.transposes([ygb[:P, c * 128:(c + 1) * 128] for c in range(16)], ["ygb"], ygT[:, :, s * P:(s + 1) * P], [("ygT", s)], P, 128)

            def resid_evac(s, j, c0, w, ps_ap, pk, P):
                x_ap = xt[:P, s, c0:c0 + w]
                S.op("dve", lambda e, x_ap=x_ap, ps_ap=ps_ap: e.tensor_tensor(x_ap, ps_ap, x_ap, ALU.add),
                     reads=[pk, ("xt", s)], writes=[("xt", s)])

            for m in range(NB):
                for h in range(4):
                    for dc in range(2):
                        sf = Sf[:, 2 * h + dc, :]
                        S.op("pool", lambda e, sf=sf: e.memset(sf, 0.0), writes=[("Sf", h)])
                for hf in range(2):
                    t = 2 * m + hf
                    load_rope(t)
                    P = load_x(t)
                    norm_T(P, "n0")
                    proj_qkvg(P, True)
                    for s in (0, 1):
                        for h in range(4):
                            state_update(s, P, h)
                    per_tile = -(-len(conv_todo) // max(1, 2 * NB - t)) if t < 2 * NB - 1 else len(conv_todo)
                    for c_ in conv_todo[:per_tile]:
                        conv_chunk(*c_)
                    conv_todo = conv_todo[per_tile:]
                self.dma("pool", Lloc[m * 128:(m + 1) * 128, :], Sf[:, :, :].rearrange("p a b -> p (a b)"),
                         reads=[("Sf", h) for h in range(4)], writes=[("Lloc", m)], semkey="Lst")
                for half in range(2):
                    ch = 2 * m + half
                    i_ap = Lloc[ch * 64:(ch + 1) * 64, :]
                    o_ap = Lall[ch * 256:(ch + 1) * 256, :]
                    S.op("pool", lambda e, i_ap=i_ap, o_ap=o_ap: e.collective_compute(
                        "AllGather", ALU.bypass, replica_groups=[[0, 1, 2, 3], [4, 5, 6, 7]], ins=[i_ap.opt()], outs=[o_ap.opt()]),
                         reads=[("Lloc", m)], writes=[("Lall", ch)], dma=True, semkey="cc1", inc=1)
            S.barrier()
            S.mute = cfg.upto < 2 or 2 in cfg.skip

            save = self.aoff
            self.aoff = 0
            Tst = self.view(4096, [128, 8, 512], F32)
            Acc = self.view(4096, [128, 8, 512], F32)
            Lb = [self.view(4096, [128, 8, 512], F32) for _ in range(2)]
            for a in range(8):
                ta = Tst[:, a, :]
                S.op("pool", lambda e, ta=ta: e.memset(ta, 0.0), writes=[("Tst", a)])
            for m in range(NB):
                for h in range(4):
                    for dc in range(2):
                        a_ap, t_ap = Acc[:, 2 * h + dc, :], Tst[:, 2 * h + dc, :]
                        sc = pcoef[:, h:h + 1]
                        S.op("dve", lambda e, a_ap=a_ap, t_ap=t_ap, sc=sc: e.tensor_scalar(a_ap, t_ap, sc, None, ALU.mult),
                             reads=[("Tst", 2 * h + dc), "pcoef"], writes=[("Acc", 2 * h + dc)])
                        g2048 = GAM[h] ** 2048
                        S.op("pool", lambda e, t_ap=t_ap, g2048=g2048: e.tensor_scalar(t_ap, t_ap, g2048, None, ALU.mult),
                             reads=[("Tst", 2 * h + dc)], writes=[("Tst", 2 * h + dc)])
                for i in range(4):
                    sl = self.nxt("Lb", 2)
                    for half in range(2):
                        r0 = ((2 * m + half) * 4 + i) * 64
                        self.dma("sp", Lb[sl][half * 64:(half + 1) * 64, :, :].rearrange("p a b -> p (a b)"), Lall[r0:r0 + 64, :],
                                 reads=[("Lall", 2 * m + half)], writes=[("Lb", sl)], semkey=("Lb", sl, half))
                    for h in range(4):
                        for dc in range(2):
                            a_ap, t_ap, l_ap = Acc[:, 2 * h + dc, :], Tst[:, 2 * h + dc, :], Lb[sl][:, 2 * h + dc, :]
                            sc = pcoef[:, 4 + 4 * i + h:4 + 4 * i + h + 1]
                            S.op("dve", lambda e, a_ap=a_ap, l_ap=l_ap, sc=sc: e.scalar_tensor_tensor(a_ap, l_ap, sc, a_ap, ALU.mult, ALU.add),
                                 reads=[("Lb", sl), "pcoef", ("Acc", 2 * h + dc)], writes=[("Acc", 2 * h + dc)])
                            cT = GAM[h] ** (512 * (3 - i))
                            S.op("dve", lambda e, t_ap=t_ap, l_ap=l_ap, cT=cT: e.scalar_tensor_tensor(t_ap, l_ap, cT, t_ap, ALU.mult, ALU.add),
                                 reads=[("Lb", sl), ("Tst", 2 * h + dc)], writes=[("Tst", 2 * h + dc)])
                self.dma("pool", Sst[m], Acc[:, :, :].rearrange("p a b -> p (a b)"), reads=[("Acc", a) for a in range(8)], writes=["Sst"], semkey="Sst")
            self.aoff = save
            S.barrier()
            S.mute = cfg.upto < 3 or 3 in cfg.skip

            def ffn(l, P, subs_x, x_of, xkey_of, tagn):
                self.rms_to_bf16(lambda s: x_of(s), lambda s: [xkey_of(s)],
                                 lambda s: hb[:P, s, :], lambda s: [("hb", s)], (0, 1), P, tagn)
                for s in (0, 1):
                    self.transposes([hb[:P, s, kc * 128:(kc + 1) * 128] for kc in range(8)], [("hb", s)],
                                    hT[:, :, s * P:(s + 1) * P], [("hT", s)], P, 128)
                T = 2 * P
                wn = "w_ffn_in%d" % l
                for g0 in range(0, 22, 4):
                    ng = min(4, 22 - g0)
                    wg, wgk = self.wslab(wn, 0, 8, 128, g0 * 128, ng * 128)
                    wu, wuk = self.wslab(wn, 0, 8, 128, DFF + g0 * 128, ng * 128)
                    for jj in range(ng):
                        oc = g0 + jj
                        b = 2 + self.nxt("ffb", 4)
                        pk = ("ps", b)
                        for (wap, wk, off) in ((wg, wgk, 0), (wu, wuk, 256)):
                            for kc in range(8):
                                o_ap = PS[b][:, off:off + T]
                                l_ap = wap[:, kc, jj * 128:(jj + 1) * 128]
                                r_ap = hT[:, kc, 0:T]
                                S.op("pe", lambda e, o_ap=o_ap, l_ap=l_ap, r_ap=r_ap, kc=kc: e.matmul(o_ap, l_ap, r_ap, start=(kc == 0), stop=(kc == 7)),
                                     reads=[("hT", 0), ("hT", 1), wk], writes=[pk])
                        g_ap = PS[b][:, 0:T]
                        u_ap = PS[b][:, 256:256 + T]
                        sl = self.nxt("sgt", 1)
                        t_ap = sgt[:, 0:T]
                        S.op("act", lambda e, t_ap=t_ap, g_ap=g_ap: e.activation(t_ap, g_ap, AF.Silu), reads=[pk], writes=["sgt"])
                        h_ap = hidT[:, oc, 0:T]
                        S.op("dve", lambda e, h_ap=h_ap, t_ap=t_ap, u_ap=u_ap: e.tensor_tensor(h_ap, t_ap, u_ap, ALU.mult),
                             reads=["sgt", pk], writes=["hidT"])
                self.linear_T(hidT, ["hidT"], "w_ffn_out%d" % l, DFF, 128, 0, D, 512, (0, 1), P,
                              lambda s, j, c0, w, a, k: subs_x(s, c0, w, a, k), bank_l0)

            def l0_resid(P):
                return lambda s, c0, w, a, k: resid_evac(s, 0, c0, w, a, k, P)

            for t in range(NT0):
                prompt = t < 2 * NB
                m, hf = t // 2, t % 2
                load_rope(t)
                if prompt and hf == 0:
                    self.dma("sp", Sf[:, :, :].rearrange("p a b -> p (a b)"), Sst[m], reads=["Sst"],
                             writes=[("Sf", h) for h in range(4)], semkey="Sfl")
                    for h in range(4):
                        cast_state(h)
                P = load_x(t)
                norm_T(P, "n0")
                proj_qkvg(P, False)
                for s in (0, 1):
                    if not prompt:
                        sq = (t - 2 * NB) * 2 + s
                        self.dma("sp", Sf[:, :, :], st0[sq].rearrange("h (dc p) e -> p (h dc) e", p=128), reads=[],
                                 writes=[("Sf", h) for h in range(4)], semkey="Sfl")
                        for h in range(4):
                            cast_state(h)
                    ret_core(s, P)
                    if not prompt:
                        self.dma("pool", sts[sq].rearrange("h (dc p) e -> p (h dc) e", p=128), Sf[:, :, :],
                                 reads=[("Sf", h) for h in range(4)], writes=[], semkey="Sfs")
                if prompt and m == NB - 1 and hf == 1:
                    self.dma("pool", stp.rearrange("h (dc p) e -> p (h dc) e", p=128), Sf[:, :, :],
                             reads=[("Sf", h) for h in range(4)], writes=[], semkey="Sfs")
                self.linear_T(ygT, [("ygT", 0), ("ygT", 1)], "w_ret_out", 2048, 128, 0, D, 512, (0, 1), P,
                              lambda s, j, c0, w, a, k: resid_evac(s, j, c0, w, a, k, P), bank_l0)
                def dbg_store(P=P, t=t, m=m, hf=hf, prompt=prompt):
                    if prompt:
                        self.dma("pool", yp[m, hf * 256:(hf + 1) * 256, :].rearrange("(s p) d -> p s d", p=128), xt[:, :, :],
                                 reads=[("xt", 0), ("xt", 1)], writes=[], semkey="o_y")
                    else:
                        j2_ = (t - 2 * NB) * 2
                        self.dma("pool", ys[j2_:j2_ + 2, :, :].rearrange("s p d -> p s d"), xt[:64, :, :],
                                 reads=[("xt", 0), ("xt", 1)], writes=[], semkey="o_y")
                if cfg.dbg == 1:
                    dbg_store()
                ffn(0, P, l0_resid(P), lambda s: xt[:P, s, :], lambda s: ("xt", s), "n0")
                if cfg.dbg == 2:
                    dbg_store()
                norm_T(P, "n0")

                def kv_evac(s, j, c0, w, ps_ap, pk):
                    sk = ("kvss", s)
                    ssa, rsa = self.ss[:P, 4 + s:5 + s], self.rstd[:P, 4 + s:5 + s]
                    S.op("dve", lambda e: e.memset(ssa, 0.0), writes=[sk])
                    S.op("act", lambda e: e.activation(self.junk[:P, 0:256], ps_ap[:, 0:256], AF.Square, accum_out=ssa),
                         reads=[pk], writes=[sk, "junk"])
                    self.rsqrt(rsa, ssa, 1.0 / 256, [sk], [("kvrs", s)])
                    co = ckvo[:P, s, :]
                    S.op("dve", lambda e: e.scalar_tensor_tensor(co, ps_ap[:, 0:256], rsa, gck[:P, :], ALU.mult, ALU.mult),
                         reads=[pk, ("kvrs", s), "gck"], writes=[("ckvo", s)])
                    self.copy("pool", cbf[:P, s, 0:256], co, reads=[("ckvo", s)], writes=[("cbf", s)])
                    x1, x2 = ps_ap[:, 256:272], ps_ap[:, 272:288]
                    cs, sn = cosm[:P, s, :], sinm[:P, s, :]
                    t1, t2, t3, t4 = [k_[:P, :] for k_ in krt]
                    for (tt, xx, tb, i_, tbk) in ((t1, x1, cs, 0, "cosm"), (t2, x2, sn, 1, "sinm"), (t3, x2, cs, 2, "cosm"), (t4, x1, sn, 3, "sinm")):
                        S.op("dve", lambda e, tt=tt, xx=xx, tb=tb: e.tensor_tensor(tt, xx, tb, ALU.mult),
                             reads=[pk, tbk], writes=[("krt", i_)])
                    S.op("pool", lambda e: e.tensor_tensor(kro[:P, s, 0:16], t1, t2, ALU.subtract),
                         reads=[("krt", 0), ("krt", 1)], writes=[("kro", s)])
                    S.op("pool", lambda e: e.tensor_tensor(kro[:P, s, 16:32], t3, t4, ALU.add),
                         reads=[("krt", 2), ("krt", 3)], writes=[("kro", s)])
                    self.copy("pool", cbf[:P, s, 256:288], kro[:P, s, :], reads=[("kro", s)], writes=[("cbf", s)])

                self.linear_T(hT, [("hT", 0), ("hT", 1)], "w_dkv", 1024, 128, 0, 288, 288, (0, 1), P, kv_evac, bank_l0)
                if prompt:
                    r0 = m * 512 + hf * 256
                    self.dma("pool", ckvp[m, hf * 256:(hf + 1) * 256, :].rearrange("(s p) c -> p s c", p=128), ckvo[:, :, :],
                             reads=[("ckvo", 0), ("ckvo", 1)], writes=[], semkey="o_ckv")
                    self.dma("pool", krp[m, hf * 256:(hf + 1) * 256, :].rearrange("(s p) c -> p s c", p=128), kro[:, :, :],
                             reads=[("kro", 0), ("kro", 1)], writes=[], semkey="o_kr")
                    self.dma("pool", cloc[r0:r0 + 256, :].rearrange("(s p) c -> p s c", p=128), cbf[:, :, :],
                             reads=[("cbf", 0), ("cbf", 1)], writes=["cloc"], semkey="o_cb")
                    self.dma("pool", x2s[r0:r0 + 256, :].rearrange("(s p) c -> p s c", p=128), xt[:, :, :],
                             reads=[("xt", 0), ("xt", 1)], writes=["x2s"], semkey="o_x2")
                else:
                    j2 = (t - 2 * NB) * 2
                    self.dma("pool", ckvs[j2:j2 + 2, :, :].rearrange("s p c -> p s c"), ckvo[:64, :, :],
                             reads=[("ckvo", 0), ("ckvo", 1)], writes=[], semkey="o_ckv")
                    self.dma("pool", krs[j2:j2 + 2, :, :].rearrange("s p c -> p s c"), kro[:64, :, :],
                             reads=[("kro", 0), ("kro", 1)], writes=[], semkey="o_kr")
                    self.dma("pool", clocs[j2:j2 + 2, :, :].rearrange("s p c -> p s c"), cbf[:64, :, :],
                             reads=[("cbf", 0), ("cbf", 1)], writes=["clocs"], semkey="o_cb")
                    r0 = NB * 512 + j2 * 64
                    self.dma("pool", x2s[r0:r0 + 128, :].rearrange("(s p) c -> p s c", p=64), xt[:64, :, :],
                             reads=[("xt", 0), ("xt", 1)], writes=["x2s"], semkey="o_x2")
            CR = min(1024, NB * 512)
            for ch in range(NB * 512 // CR):
                i_ap = cloc[ch * CR:(ch + 1) * CR, :]
                o_ap = call[ch * 4 * CR:(ch + 1) * 4 * CR, :]
                S.op("pool", lambda e, i_ap=i_ap, o_ap=o_ap: e.collective_compute(
                    "AllGather", ALU.bypass, replica_groups=[[0, 1, 2, 3], [4, 5, 6, 7]], ins=[i_ap.opt()], outs=[o_ap.opt()]),
                     reads=["cloc"], writes=["call"], dma=True, semkey="cc2", inc=1)
            S.barrier()
            S.mute = cfg.upto < 4 or 4 in cfg.skip

            self.aoff = 0
            wuk = self.view(1024, [128, 2, 1024], BF16)
            wuv = self.view(1024, [128, 2, 1024], BF16)
            ctf = [self.view(1152, [128, 4, 288], F32) for _ in range(2)]
            ctb = [self.view(576, [128, 4, 288], BF16) for _ in range(2)]
            ckT2 = [self.view(512, [128, 2, 512], BF16) for _ in range(2)]
            krT2 = [self.view(256, [32, 512], BF16) for _ in range(2)]
            kst = [self.view(4096, [64, 16, 512], BF16) for _ in range(2)]
            vst = [self.view(2080, [128, 4, 16, 65], BF16) for _ in range(2)]
            self.dma("sp", wuk[:, :, :], self.wb["w_uk"].rearrange("(k p) n -> p k n", p=128), reads=[], writes=["wuk"], semkey="wuk")
            self.dma("sp", wuv[:, :, :], self.wb["w_uv"].rearrange("(k p) n -> p k n", p=128), reads=[], writes=["wuv"], semkey="wuv")
            for sl in range(2):
                v_ = vst[sl]
                for kt_ in range(4):
                    S.op("dve", lambda e, v_=v_, kt_=kt_: e.memset(v_[:, kt_, :, 64:65], 1.0), writes=[("vst", sl)])

            def kv_block(src_fn, KT_dst, V_dst, k0, tiles):
                sl = self.nxt("kvsl", 2)
                src_fn(sl)
                ckT, krT = ckT2[sl], krT2[sl]
                nkeys = sum(tiles)
                cb_ = ctb[sl]
                col = 0
                for kt, nk in enumerate(tiles):
                    self.transposes([cb_[:nk, kt, c * 128:(c + 1) * 128] for c in range(2)], [("ctb", sl)],
                                    ckT[:, :, col:col + nk], [("ckT", sl)], nk, 128)
                    self.transposes([cb_[:nk, kt, 256:288]], [("ctb", sl)],
                                    krT[:, col:col + nk].rearrange("p (a b) -> p a b", a=1), [("krT", sl)], nk, 32)
                    col += nk
                ks = kst[sl]
                for hp in range(8):
                    b = 2 + self.nxt("kvb", 4)
                    for cc in range(2):
                        o_ap, l_ap, r_ap = PS[b][:, 0:nkeys], wuk[:, cc, hp * 128:(hp + 1) * 128], ckT[:, cc, 0:nkeys]
                        S.op("pe", lambda e, o_ap=o_ap, l_ap=l_ap, r_ap=r_ap, cc=cc: e.matmul(o_ap, l_ap, r_ap, start=(cc == 0), stop=(cc == 1)),
                             reads=["wuk", ("ckT", sl)], writes=[("ps", b)])
                    self.copy("dve", ks[:, 2 * hp, 0:nkeys], PS[b][0:64, 0:nkeys], reads=[("ps", b)], writes=[("kst", sl)])
                    self.copy("act", ks[:, 2 * hp + 1, 0:nkeys], PS[b][64:128, 0:nkeys], reads=[("ps", b)], writes=[("kst", sl)])
                self.dma("pool", KT_dst[:, 0:64, k0:k0 + nkeys].rearrange("h r k -> r h k"), ks[:, :, 0:nkeys],
                         reads=[("kst", sl)], writes=["KT"], semkey=("kst", sl))
                self.dma("sp", KT_dst[:, 64:96, k0:k0 + nkeys].rearrange("h r k -> r h k"),
                         krT[:, 0:nkeys].unsqueeze(1).to_broadcast([32, 16, nkeys]),
                         reads=[("krT", sl)], writes=["KT"], semkey=("krs", sl))
                vs_ = vst[sl]
                col = 0
                for kt, nk in enumerate(tiles):
                    for half in range(2):
                        b = 6 + half
                        for cc in range(2):
                            o_ap, l_ap, r_ap = PS[b][:nk, :], ckT[:, cc, col:col + nk], wuv[:, cc, half * 512:(half + 1) * 512]
                            S.op("pe", lambda e, o_ap=o_ap, l_ap=l_ap, r_ap=r_ap, cc=cc: e.matmul(o_ap, l_ap, r_ap, start=(cc == 0), stop=(cc == 1)),
                                 reads=["wuv", ("ckT", sl)], writes=[("ps", b)])
                        self.copy(self.ev_engine(), vs_[:nk, kt, half * 8:(half + 1) * 8, 0:64],
                                  PS[b][:nk, :].rearrange("p (a b) -> p a b", b=64), reads=[("ps", b)], writes=[("vst", sl)])
                    col += nk
                nt = len(tiles)
                if tiles[0] == 128:
                    self.dma("pool", V_dst[k0:k0 + nkeys, :].rearrange("(t p) c -> p t c", p=128),
                             vs_[:, 0:nt, :, :].rearrange("p t a b -> p t (a b)"), reads=[("vst", sl)], writes=["V"], semkey=("vst", sl))
                else:
                    self.dma("pool", V_dst[k0:k0 + nkeys, :], vs_[:nkeys, 0, :, :].rearrange("p a b -> p (a b)"),
                             reads=[("vst", sl)], writes=["V"], semkey=("vst", sl))

            for j in range(4 * NB):
                m_, c_ = j // 4, j % 4
                r0 = ((m_ * 512) // CR) * 4 * CR + c_ * CR + (m_ * 512) % CR

                def src(sl, r0=r0):
                    self.dma("sp", ctb[sl][:, :, :], call[r0:r0 + 512, :].rearrange("(t p) c -> p t c", p=128),
                             reads=["call"], writes=[("ctb", sl)], semkey=("ctb", sl))
                kv_block(src, KTp, Vp, j * 512, [128] * 4)
            for q in range(4):
                for kbk in range(PAST // 512):
                    def src(sl, q=q, kbk=kbk):
                        self.dma("sp", ctf[sl][:, :, 0:256], cckv[q, kbk * 512:(kbk + 1) * 512, :].rearrange("(t p) c -> p t c", p=128),
                                 reads=[], writes=[("ctf", sl)], semkey=("ctf", sl))
                        self.dma("sp", ctf[sl][:, :, 256:288], ckr[q, kbk * 512:(kbk + 1) * 512, :].rearrange("(t p) c -> p t c", p=128),
                                 reads=[], writes=[("ctf", sl)], semkey=("ctf2", sl))
                        for kt_ in range(4):
                            self.copy(self.ev_engine(), ctb[sl][:, kt_, :], ctf[sl][:, kt_, :], reads=[("ctf", sl)], writes=[("ctb", sl)])
                    kv_block(src, KTs[q], Vs[q], kbk * 512, [128] * 4)

                def src(sl, q=q):
                    self.dma("sp", ctb[sl][:64, 0, :], clocs[q], reads=["clocs"], writes=[("ctb", sl)], semkey=("ctb", sl))
                kv_block(src, KTs[q], Vs[q], PAST, [64])
            S.barrier()
            S.mute = cfg.upto < 5

            self.aoff = 0
            xt4 = self.view(4096, [128, 4, D], F32)
            hb = self.view(1024, [128, 2, D], BF16)
            hT = self.view(1024, [128, 8, 256], BF16)
            cqb = self.view(384, [128, 2, 384], BF16)
            cqT = self.view(384, [128, 3, 256], BF16)
            qtok = self.view(3072, [128, 4, 16, 96], BF16)
            QT = self.view(1024, [96, 4, 512], BF16)
            KTt = [self.view(1024, [96, 4, 512], BF16) for _ in range(3)]
            Vt = [self.view(520, [128, 4, 260], BF16) for _ in range(3)]
            PT = [self.view(256, [128, 512], BF16) for _ in range(4)]
            PT2 = [self.view(512, [128, 1024], BF16) for _ in range(3)]
            OTs = [self.view(512, [65, 512], F32) for _ in range(2)]
            oT = self.view(4096, [64, 16, 512], BF16)
            rD = self.view(512, [64, 512], F32)
            hidT = self.view(2816, [128, 22, 256], BF16)
            sgt = self.view(256, [128, 256], F32)
            yo = self.view(2048, [128, 2, D], F32)
            cosm = self.view(32, [128, 2, 16], F32)
            sinm = self.view(32, [128, 2, 16], F32)
            krt = [self.view(16, [128, 16], F32) for _ in range(4)]

            def bank_l1(si, par):
                return 2 + si + 2 * par

            def attention(qc0, Tq, hg, KT_src, V_src, steps, maskinfo):
                if cfg.dbg == 3:
                    return
                LAG = 3
                nsteps = len(steps)
                slot = {}

                def load(j):
                    (k0, tiles, mi) = steps[j]
                    nkeys = sum(tiles)
                    sl = self.nxt("kvt", 3)
                    slot[j] = sl
                    kt_ap, v_ap = KTt[sl], Vt[sl]
                    self.dma("sp", kt_ap[:, :, 0:nkeys], KT_src[hg * 4:(hg + 1) * 4, :, k0:k0 + nkeys].rearrange("h r k -> r h k"),
                             reads=["KT"], writes=[("KTt", sl)], semkey=("KTt", sl))
                    if tiles[0] == 128:
                        self.dma("sp", v_ap[:, 0:len(tiles), :], V_src[k0:k0 + nkeys, hg * 260:(hg + 1) * 260].rearrange("(t p) c -> p t c", p=128),
                                 reads=["V"], writes=[("Vt", sl)], semkey=("Vt", sl))
                    else:
                        self.dma("sp", v_ap[:nkeys, 0, :], V_src[k0:k0 + nkeys, hg * 260:(hg + 1) * 260],
                                 reads=["V"], writes=[("Vt", sl)], semkey=("Vt", sl))

                tl = []
                for j, (k0, tiles, mi) in enumerate(steps):
                    col = 0
                    for kt, nk in enumerate(tiles):
                        for hl in range(4):
                            tl.append((j, kt, nk, col, hl, mi, j == 0 and kt == 0, j == nsteps - 1 and kt == len(tiles) - 1,
                                       kt == len(tiles) - 1 and hl == 3))
                        col += nk
                n = len(tl)
                for j in range(min(3, nsteps)):
                    load(j)
                nextload = 3
                if Tq == 512 and cfg.pairs:
                    units = []
                    for j, (k0, tiles, mi) in enumerate(steps):
                        col = 0
                        for kt, nk in enumerate(tiles):
                            for pr in range(2):
                                units.append((j, kt, nk, col, pr, mi, j == 0 and kt == 0, j == nsteps - 1 and kt == len(tiles) - 1,
                                              kt == len(tiles) - 1 and pr == 1))
                            col += nk
                    nu = len(units)
                    gsl = {}
                    for i in range(nu + 1):
                        if i < nu:
                            (j, kt, nk, col, pr, mi, first, last, endstep) = units[i]
                            sl = slot[j]
                            g = self.nxt("scp", 2)
                            gsl[i] = g
                            for hh in range(2):
                                hl = 2 * pr + hh
                                b_ = 2 * g + hh
                                sc_ap = PS[b_][:nk, 0:512]
                                l_ap, r_ap = KTt[sl][:, hl, col:col + nk], QT[:, hl, qc0:qc0 + 512]
                                msk = mi is not None
                                S.op("pe", lambda e, sc_ap=sc_ap, l_ap=l_ap, r_ap=r_ap, msk=msk: e.matmul(sc_ap, l_ap, r_ap, start=True, stop=not msk),
                                     reads=[("KTt", sl), "QT"], writes=[("ps", b_)])
                                if msk:
                                    l2, r2 = aind[:, kt, 0:nk], bm[:, mi, 0:512]
                                    S.op("pe", lambda e, sc_ap=sc_ap, l2=l2, r2=r2: e.matmul(sc_ap, l2, r2, start=False, stop=True),
                                         reads=["aind", "bm"], writes=[("ps", b_)])
                        t = i - 1
                        if t >= 0:
                            (j, kt, nk, col, pr, mi, first, last, endstep) = units[t]
                            sl = slot[j]
                            g = gsl[t]
                            pi = self.nxt("pt2", 3)
                            p_ap = PT2[pi][:nk, :]
                            sc2 = psall[:nk, 2 * g * 512:(2 * g + 2) * 512]
                            S.op("act", lambda e, p_ap=p_ap, sc2=sc2: e.activation(p_ap, sc2, AF.Exp, scale=MLA_SCALE),
                                 reads=[("ps", 2 * g), ("ps", 2 * g + 1)], writes=[("PT2", pi)])
                            for hh in range(2):
                                hl = 2 * pr + hh
                                ob = 4 + hl
                                o_ap = PS[ob][:65, 0:512]
                                lv = Vt[sl][:nk, kt, hl * 65:(hl + 1) * 65]
                                pp = PT2[pi][:nk, hh * 512:(hh + 1) * 512]
                                S.op("pe", lambda e, o_ap=o_ap, lv=lv, pp=pp, first=first, last=last: e.matmul(o_ap, lv, pp, start=first, stop=last),
                                     reads=[("Vt", sl), ("PT2", pi)], writes=[("ps", ob)])
                            if endstep and nextload < nsteps:
                                load(nextload)
                                nextload += 1
                    n = 0
                scb, pts = {}, {}
                for i in range((n + LAG) if n else 0):
                    if i < n:
                        (j, kt, nk, col, hl, mi, first, last, endstep) = tl[i]
                        sl = slot[j]
                        sb_ = self.nxt("scb", 4)
                        scb[i] = sb_
                        sc_ap = PS[sb_][:nk, 0:Tq]
                        l_ap, r_ap = KTt[sl][:, hl, col:col + nk], QT[:, hl, qc0:qc0 + Tq]
                        msk = mi is not None
                        S.op("pe", lambda e, sc_ap=sc_ap, l_ap=l_ap, r_ap=r_ap, msk=msk: e.matmul(sc_ap, l_ap, r_ap, start=True, stop=not msk),
                             reads=[("KTt", sl), "QT"], writes=[("ps", sb_)])
                        if msk:
                            l2, r2 = aind[:, kt, 0:nk], bm[:, mi, 0:Tq]
                            S.op("pe", lambda e, sc_ap=sc_ap, l2=l2, r2=r2: e.matmul(sc_ap, l2, r2, start=False, stop=True),
                                 reads=["aind", "bm"], writes=[("ps", sb_)])
                    t = i - LAG
                    if t >= 0:
                        (j, kt, nk, col, hl, mi, first, last, endstep) = tl[t]
                        sl = slot[j]
                        sb_ = scb[t]
                        sc_ap = PS[sb_][:nk, 0:Tq]
                        pi = self.nxt("pt", 4)
                        p_ap = PT[pi][:nk, 0:Tq]
                        S.op("act", lambda e, p_ap=p_ap, sc_ap=sc_ap: e.activation(p_ap, sc_ap, AF.Exp, scale=MLA_SCALE),
                             reads=[("ps", sb_)], writes=[("PT", pi)])
                        ob = 4 + hl
                        o_ap = PS[ob][:65, 0:Tq]
                        lv = Vt[sl][:nk, kt, hl * 65:(hl + 1) * 65]
                        S.op("pe", lambda e, o_ap=o_ap, lv=lv, p_ap=p_ap, first=first, last=last: e.matmul(o_ap, lv, p_ap, start=first, stop=last),
                             reads=[("Vt", sl), ("PT", pi)], writes=[("ps", ob)])
                        if endstep and nextload < nsteps:
                            load(nextload)
                            nextload += 1
                for hl in range(4 if cfg.dbg != 4 else 0):
                    ob = 4 + hl
                    oi = self.nxt("ots", 2)
                    os_ap = OTs[oi][:, 0:Tq]
                    self.copy(self.ev_engine(), os_ap, PS[ob][:65, 0:Tq], reads=[("ps", ob)], writes=[("OTs", oi)])
                    drow = OTs[oi][64:65, 0:Tq]
                    S.op("act", lambda e, drow=drow: e.activation(drow, drow, AF.Ln), reads=[("OTs", oi)], writes=[("OTs", oi)])
                    S.op("act", lambda e, drow=drow: e.activation(drow, drow, AF.Exp, scale=-1.0), reads=[("OTs", oi)], writes=[("OTs", oi)])
                    db = self.nxt("scb", 4)
                    d_ap = PS[db][:64, 0:Tq]
                    S.op("pe", lambda e, d_ap=d_ap, os_ap=os_ap: e.matmul(d_ap, esel[:, :], os_ap, start=True, stop=True),
                         reads=["esel", ("OTs", oi)], writes=[("ps", db)])
                    out_ap = oT[:, hg * 4 + hl, qc0:qc0 + Tq]
                    num = OTs[oi][:64, 0:Tq]
                    S.op("dve", lambda e, out_ap=out_ap, num=num, d_ap=d_ap: e.tensor_tensor(out_ap, num, d_ap, ALU.mult),
                         reads=[("OTs", oi), ("ps", db)], writes=[("oT", hg)])

            for blk in range(NB + 1):
                prompt = blk < NB
                P = 128 if prompt else 64
                T4 = 4 * P
                if prompt:
                    src = x2s[blk * 512:(blk + 1) * 512, :].rearrange("(s p) d -> p s d", p=128)
                else:
                    src = x2s[NB * 512:NB * 512 + 256, :].rearrange("(s p) d -> p s d", p=64)
                self.dma("sp", xt4[:P, :, :], src, reads=["x2s"], writes=[("x4", s) for s in range(4)], semkey="xt")
                for hf in range(2):
                    t = 2 * blk + hf
                    self.dma("sp", cosm[:, :, :], cosm_d[t], reads=[], writes=["cosm"], semkey="cosm")
                    self.dma("sp", sinm[:, :, :], sinm_d[t], reads=[], writes=["sinm"], semkey="sinm")
                    self.rms_to_bf16(lambda s: xt4[:P, 2 * hf + s, :], lambda s: [("x4", 2 * hf + s)],
                                     lambda s: hb[:P, s, :], lambda s: [("hb", s)], (0, 1), P, "n1")
                    for s in (0, 1):
                        self.transposes([hb[:P, s, kc * 128:(kc + 1) * 128] for kc in range(8)], [("hb", s)],
                                        hT[:, :, s * P:(s + 1) * P], [("hT", s)], P, 128)

                    def cq_evac(s, j, c0, w, ps_ap, pk):
                        sk = ("cqss", s)
                        ssa, rsa = self.ss[:P, 4 + s:5 + s], self.rstd[:P, 4 + s:5 + s]
                        S.op("dve", lambda e: e.memset(ssa, 0.0), writes=[sk])
                        S.op("act", lambda e: e.activation(self.junk[:P, 0:384], ps_ap, AF.Square, accum_out=ssa),
                             reads=[pk], writes=[sk, "junk"])
                        self.rsqrt(rsa, ssa, 1.0 / 384, [sk], [("cqrs", s)])
                        S.op("act", lambda e: e.activation(cqb[:P, s, :], ps_ap, AF.Copy, scale=rsa),
                             reads=[pk, ("cqrs", s)], writes=[("cqb", s)])
                        self.transposes([cqb[:P, s, c * 128:(c + 1) * 128] for c in range(3)], [("cqb", s)],
                                        cqT[:, :, s * P:(s + 1) * P], [("cqT", s)], P, 128)
                    if cfg.dbg == 5:
                        S.mute = True
                    self.linear_T(hT, [("hT", 0), ("hT", 1)], "w_dq", 1024, 128, 0, 384, 384, (0, 1), P, cq_evac, bank_l1)
                    if cfg.dbg == 6:
                        S.mute = True

                    def q_evac(s, j, c0, w, ps_ap, pk):
                        s4 = 2 * hf + s
                        pv = ps_ap.rearrange("p (a b) -> p a b", b=96)
                        for hh in range(4):
                            self.copy(self.ev_engine(), qtok[:P, s4, 4 * j + hh, 0:64], ps_ap[:, hh * 96:hh * 96 + 64],
                                      reads=[pk], writes=[("qtok", s4)])
                        for hh in range(4):
                            hd = 4 * j + hh
                            a1, a2 = ps_ap[:, hh * 96 + 64:hh * 96 + 80], ps_ap[:, hh * 96 + 80:hh * 96 + 96]
                            c1, s1 = cosm[:P, s, :], sinm[:P, s, :]
                            t1, t2, t3, t4 = [k_[:P, :] for k_ in krt]
                            for (tt, xx, tb, i_, tbk) in ((t1, a1, c1, 0, "cosm"), (t2, a2, s1, 1, "sinm"), (t3, a2, c1, 2, "cosm"), (t4, a1, s1, 3, "sinm")):
                                S.op("dve", lambda e, tt=tt, xx=xx, tb=tb: e.tensor_tensor(tt, xx, tb, ALU.mult),
                                     reads=[pk, tbk], writes=[("krt", i_)])
                            o1, o2 = qtok[:P, s4, hd, 64:80], qtok[:P, s4, hd, 80:96]
                            S.op("dve", lambda e, o1=o1, t1=t1, t2=t2: e.tensor_tensor(o1, t1, t2, ALU.subtract),
                                 reads=[("krt", 0), ("krt", 1)], writes=[("qtok", s4)])
                            S.op("dve", lambda e, o2=o2, t3=t3, t4=t4: e.tensor_tensor(o2, t3, t4, ALU.add),
                                 reads=[("krt", 2), ("krt", 3)], writes=[("qtok", s4)])
                    self.linear_T(cqT, [("cqT", 0), ("cqT", 1)], "w_uq", 384, 128, 0, 1536, 384, (0, 1), P, q_evac, bank_l1)

                if cfg.dbg == 7:
                    S.mute = True
                for hg in range(4):
                    for hl in range(4):
                        hd = hg * 4 + hl
                        b = self.nxt("tpb", 2)
                        pk = ("ps", b)
                        for s4 in range(4):
                            o_ap, s_ap = PSB[b][:96, s4 * P:(s4 + 1) * P], qtok[:P, s4, hd, :]
                            S.op("pe", lambda e, o_ap=o_ap, s_ap=s_ap: e.transpose(o_ap, s_ap, self.ident[:P, :P]),
                                 reads=[("qtok", s4), "ident"], writes=[pk])
                        self.copy(self.ev_engine(), QT[:, hl, 0:T4], PSB[b][:96, 0:T4], reads=[pk], writes=["QT"])
                    if prompt:
                        steps = []
                        for kbk in range(4 * blk + 4):
                            mi = kbk - 4 * blk if kbk >= 4 * blk else None
                            steps.append((kbk * 512, [128] * 4, mi))
                        attention(0, 512, hg, KTp, Vp, steps, None)
                    else:
                        for q in range(4):
                            steps = [(kbk * 512, [128] * 4, None) for kbk in range(PAST // 512)] + [(PAST, [64], None)]
                            attention(q * 64, 64, hg, KTs[q], Vs[q], steps, None)

                if cfg.dbg == 8:
                    S.mute = True
                def o_resid(s, j, c0, w, ps_ap, pk):
                    x_ap = xt4[:P, s, c0:c0 + w]
                    S.op("dve", lambda e, x_ap=x_ap, ps_ap=ps_ap: e.tensor_tensor(x_ap, ps_ap, x_ap, ALU.add),
                         reads=[pk, ("x4", s)], writes=[("x4", s)])
                self.linear_T(oT, [("oT", g_) for g_ in range(4)], "w_o", 1024, 64, 0, D, 512, (0, 1, 2, 3), P, o_resid,
                              lambda si, par: si + 4 * par)

                for hf in range(2):
                    def f_resid(s, c0, w, ps_ap, pk, hf=hf):
                        x_ap = xt4[:P, 2 * hf + s, c0:c0 + w]
                        S.op("dve", lambda e, x_ap=x_ap, ps_ap=ps_ap: e.tensor_tensor(x_ap, ps_ap, x_ap, ALU.add),
                             reads=[pk, ("x4", 2 * hf + s)], writes=[("x4", 2 * hf + s)])
                    ffn(1, P, f_resid, lambda s, hf=hf: xt4[:P, 2 * hf + s, :], lambda s, hf=hf: ("x4", 2 * hf + s), "n1")
                    for s in (0, 1):
                        s4 = 2 * hf + s
                        ssk = ("fss", s)
                        ssa, rsa = self.ss[:P, 6 + s:7 + s], self.rstd[:P, 6 + s:7 + s]
                        xa = xt4[:P, s4, :]
                        S.op("dve", lambda e, ssa=ssa: e.memset(ssa, 0.0), writes=[ssk])
                        S.op("act", lambda e, xa=xa, ssa=ssa: e.activation(self.junk[:P, :D], xa, AF.Square, accum_out=ssa),
                             reads=[("x4", s4)], writes=[ssk, "junk"])
                        self.rsqrt(rsa, ssa, 1.0 / D, [ssk], [("frs", s)])
                        ya = yo[:P, s, :]
                        S.op("dve", lambda e, ya=ya, xa=xa, rsa=rsa: e.scalar_tensor_tensor(ya, xa, rsa, gfn[:P, :], ALU.mult, ALU.mult),
                             reads=[("x4", s4), ("frs", s), "gfn"], writes=[("yo", s)])
                    if prompt:
                        self.dma("pool", yp[blk, hf * 256:(hf + 1) * 256, :].rearrange("(s p) d -> p s d", p=128), yo[:, :, :],
                                 reads=[("yo", 0), ("yo", 1)], writes=[], semkey="o_y")
                    else:
                        self.dma("pool", ys[2 * hf:2 * hf + 2, :, :].rearrange("s p d -> p s d"), yo[:64, :, :],
                                 reads=[("yo", 0), ("yo", 1)], writes=[], semkey="o_y")

            S.emit()
        return self.nc


def _tables(cfg, c):
    NB, NT0, PAST = cfg.NB, cfg.NT0, cfg.PAST
    g = np.array(GAM, dtype=np.float64)
    p = np.arange(128, dtype=np.float64)
    dec = np.zeros((128, 12), np.float64)
    for h in range(4):
        dec[:, h] = g[h] ** (p + 1)
        dec[:, 4 + h] = g[h] ** (127 - p) / 16.0
        dec[:64, 8 + h] = g[h] ** (63 - p[:64]) / 16.0
    m128 = np.zeros((128, 4, 128), np.float64)
    m64 = np.zeros((128, 4, 128), np.float64)
    kk, qq = np.meshgrid(np.arange(128), np.arange(128), indexing="ij")
    for h in range(4):
        m128[:, h, :] = np.where(qq >= kk, g[h] ** -128.0, 0.0)
        m64[:64, h, :64] = np.where(qq[:64, :64] >= kk[:64, :64], g[h] ** -64.0, 0.0)
    pos = np.zeros((NT0, 128, 2), np.float32)
    for t in range(2 * NB):
        m, hf = t // 2, t % 2
        for s in range(2):
            pos[t, :, s] = (4 * m + c) * 512 + hf * 256 + s * 128 + np.arange(128)
    for t in range(2 * NB, NT0):
        pos[t, :64, :] = (PAST + np.arange(64))[:, None]
    tabs = {}
    for nm, half in (("r", 128), ("m", 16)):
        inv = (np.float32(10000.0) ** (-np.arange(half, dtype=np.float32) / np.float32(half))).astype(np.float32)
        ang = (pos[..., None] * inv[None, None, None, :]).astype(np.float32)
        tabs["cos" + nm] = np.cos(ang).astype(np.float32)
        tabs["sin" + nm] = np.sin(ang).astype(np.float32)
    pco = np.zeros((128, 24), np.float64)
    for h in range(4):
        pco[:, h] = g[h] ** (512.0 * c)
        for i in range(4):
            pco[:, 4 + 4 * i + h] = g[h] ** (512.0 * (c - 1 - i)) if i < c else 0.0
    aind = np.zeros((8, 4, 128), np.float32)
    for kt in range(4):
        for k in range(128):
            aind[(kt * 128 + k) // 64, kt, k] = 1.0
    bmk = np.zeros((8, 4, 512), np.float32)
    qch = np.arange(512) // 64
    for i in range(4):
        for r in range(8):
            if i == c:
                bmk[r, i, :] = np.where(qch < r, NEG, 0.0)
            elif i > c:
                bmk[r, i, :] = NEG
    esel = np.zeros((65, 64), np.float32)
    esel[64, :] = 1.0
    out = {
        "dec": dec.astype(np.float32), "m128": m128.astype(np.float32), "m64": m64.astype(np.float32),
        "cosr": tabs["cosr"], "sinr": tabs["sinr"], "cosm": tabs["cosm"], "sinm": tabs["sinm"],
        "pcoef": pco.astype(np.float32), "aind": aind.astype(ml_dtypes.bfloat16), "bm": bmk.astype(ml_dtypes.bfloat16),
        "ident": np.eye(128, dtype=np.float32).astype(ml_dtypes.bfloat16), "esel": esel,
    }
    return out


_PROG = {}


def run(cfg, inputs):
    NB, PAST, SEQ = cfg.NB, cfg.PAST, cfg.SEQ
    key = (NB, PAST, cfg.upto, cfg.dbg)
    if key not in _PROG:
        _PROG[key] = Prog(cfg).build()
    nc = _PROG[key]
    f = lambda a: np.ascontiguousarray(np.asarray(a, dtype=np.float32))
    I = {k: f(v) for k, v in inputs.items()}
    wd = {
        "w_ret_in": I["w_ret_in"][0], "w_ret_out": I["w_ret_out"][0],
        "w_ffn_in0": I["w_ffn_in"][0], "w_ffn_out0": I["w_ffn_out"][0],
        "w_ffn_in1": I["w_ffn_in"][1], "w_ffn_out1": I["w_ffn_out"][1],
        "w_dkv": I["w_dkv"], "w_dq": I["w_dq"][0], "w_uq": I["w_uq"][0], "w_o": I["w_mla_out"][0],
        "w_uk": I["w_uk"].reshape(256, 1024), "w_uv": I["w_uv"].reshape(256, 1024),
    }
    gl = [I["g_mix"][0], I["g_ffn"][0], I["g_ffn"][1], I["g_kv_in"], I["g_mix"][1]]
    gtab = np.zeros((128, 43), np.float32)
    for i, gv in enumerate(gl):
        gtab[:, 8 * i:8 * i + 8] = gv.reshape(8, 128).T
    gtab[:, 40:43] = I["g_q"][0].reshape(3, 128).T
    gckv = np.ascontiguousarray(np.broadcast_to(I["g_ckv"][None, :], (128, 256)))
    gfin = np.ascontiguousarray(np.broadcast_to(I["g_final"][None, :], (128, D)))
    in_maps = []
    for core in range(8):
        b, c = core // 4, core % 4
        d = dict(wd)
        d["xp"] = np.ascontiguousarray(I["x_prompt"][b].reshape(SEQ // 512, 512, D)[c::4])
        d["xs"] = np.ascontiguousarray(I["x_sample"][4 * core:4 * core + 4])
        d["st0"] = np.ascontiguousarray(I["state_ret"][0, 4 * core:4 * core + 4])
        d["cckv"] = np.ascontiguousarray(I["cache_ckv"][4 * core:4 * core + 4])
        d["ckr"] = np.ascontiguousarray(I["cache_krope"][4 * core:4 * core + 4])
        d["gtab"], d["gckv"], d["gfin"] = gtab, gckv, gfin
        d.update(_tables(cfg, c))
        in_maps.append(d)
    res = run_bass_kernel_spmd(nc, in_maps, core_ids=list(range(8)))
    R = res.results
    y_p = np.zeros((2, SEQ, D), np.float32)
    ckv_p = np.zeros((2, SEQ, 256), np.float32)
    kr_p = np.zeros((2, SEQ, 32), np.float32)
    st_p = np.zeros((1, 2, 4, 256, 512), np.float32)
    y_s = np.zeros((32, 64, D), np.float32)
    st_s = np.zeros((1, 32, 4, 256, 512), np.float32)
    ckv_s = np.zeros((32, 64, 256), np.float32)
    kr_s = np.zeros((32, 64, 32), np.float32)
    for core in range(8):
        b, c = core // 4, core % 4
        r = R[core]
        y_p[b].reshape(SEQ // 512, 512, D)[c::4] = np.asarray(r["yp"], dtype=np.float32)
        ckv_p[b].reshape(SEQ // 512, 512, 256)[c::4] = np.asarray(r["ckvp"], dtype=np.float32)
        kr_p[b].reshape(SEQ // 512, 512, 32)[c::4] = np.asarray(r["krp"], dtype=np.float32)
        if c == 3:
            st_p[0, b] = np.asarray(r["stp"], dtype=np.float32)
        y_s[4 * core:4 * core + 4] = np.asarray(r["ys"], dtype=np.float32)
        st_s[0, 4 * core:4 * core + 4] = np.asarray(r["sts"], dtype=np.float32)
        ckv_s[4 * core:4 * core + 4] = np.asarray(r["ckvs"], dtype=np.float32)
        kr_s[4 * core:4 * core + 4] = np.asarray(r["krs"], dtype=np.float32)
    return (y_p, y_s, st_p, st_s, ckv_p, kr_p, ckv_s, kr_s)


def kernel(**inputs):
    return run(Cfg(8, 4096), inputs)
```

```python
import contextlib
import numpy as np
import ml_dtypes
import concourse.bass as bass
import concourse.mybir as mybir
from concourse.bass_utils import run_bass_kernel_spmd

F32 = mybir.dt.float32
BF16 = mybir.dt.bfloat16
AF = mybir.ActivationFunctionType
ALU = mybir.AluOpType

D = 1024
RET_IN = 6144
DFF = 2816
EPS = 1e-6
NEG = -30000.0
MLA_SCALE = 96 ** -0.5
GAM = [1.0 - 2.0 ** (-5.0 - h) for h in range(4)]

ALL_ENG = ("pe", "act", "dve", "pool", "sp")


class Op:
    __slots__ = ("eng", "fn", "deps", "dma", "semkey", "ticket", "signal", "idx", "inc", "raw")

    def __init__(self, eng, fn, dma, semkey, idx, inc):
        self.eng, self.fn, self.dma, self.semkey, self.idx, self.inc = eng, fn, dma, semkey, idx, inc
        self.deps = []
        self.raw = set()
        self.ticket = None
        self.signal = False


class _Rec:
    def __getattr__(self, name):
        def f(*a, **k):
            self.call = (name, a, k)
            return self
        return f


class Sched:
    def __init__(self, nc):
        self.nc = nc
        self.ops = []
        self.last_w = {}
        self.readers = {}
        self.last_dma_on_sem = {}
        self.last_compute = {}
        self.mute = False

    def op(self, eng, fn, reads=(), writes=(), dma=False, semkey=None, inc=None):
        if self.mute:
            return None
        rec = _Rec()
        fn(rec)
        o = Op(eng, rec.call, dma, semkey, len(self.ops), inc if inc is not None else (16 if dma else 1))
        deps = {}
        raw = set()
        for r in reads:
            lw = self.last_w.get(r)
            if lw is not None:
                deps[lw.idx] = lw
                raw.add(lw.idx)
        for w in writes:
            lw = self.last_w.get(w)
            if lw is not None:
                deps[lw.idx] = lw
            for rd in self.readers.get(w, ()):
                deps[rd.idx] = rd
        if dma:
            prev = self.last_dma_on_sem.get(semkey)
            if prev is not None:
                deps[prev.idx] = prev
            self.last_dma_on_sem[semkey] = o
        else:
            self.last_compute[eng] = o
        o.deps = list(deps.values())
        o.raw = raw if eng != "pe" else set()
        for r in reads:
            self.readers.setdefault(r, []).append(o)
        for w in writes:
            self.last_w[w] = o
            self.readers[w] = []
        self.ops.append(o)
        return o

    def barrier(self):
        if self.mute:
            return
        lasts = list(self.last_compute.values()) + list(self.last_dma_on_sem.values())
        for e in ALL_ENG:
            o = Op(e, None, False, None, len(self.ops), 1)
            o.deps = list(lasts)
            self.ops.append(o)
        self.last_w = {}
        self.readers = {}

    def emit(self):
        nc = self.nc
        ops = self.ops
        for o in ops:
            for d in o.deps:
                if d.dma or d.eng != o.eng or o.dma or d.idx in o.raw:
                    d.signal = True
        for o in ops:
            if o.dma:
                o.signal = True
        cnt = {}
        for o in ops:
            if not o.signal or o.fn is None:
                o.signal = False if o.fn is None else o.signal
                continue
            key = ("dma", o.semkey) if o.dma else ("eng", o.eng)
            cnt[key] = cnt.get(key, 0) + o.inc
            o.ticket = (key, cnt[key])
        keys = list(cnt.keys())
        self.n_sems = len(keys)
        print("sched: %d ops, %d semaphores" % (len(ops), len(keys)))
        with contextlib.ExitStack() as st:
            sems = {}
            for i, k in enumerate(keys):
                sems[k] = st.enter_context(nc.semaphore("s%d" % i))
            block = st.enter_context(nc.Block())
            per_eng = {e: [o for o in ops if o.eng == e] for e in ALL_ENG}
            final = dict(cnt)

            def run(engname, engobj):
                waited = {}
                for o in per_eng[engname]:
                    need = {}
                    for d in o.deps:
                        if d.ticket is None:
                            continue
                        if (not d.dma) and d.eng == engname and not o.dma and d.idx not in o.raw:
                            continue
                        k, v = d.ticket
                        if need.get(k, 0) < v:
                            need[k] = v
                    for k, v in need.items():
                        if waited.get(k, 0) >= v:
                            continue
                        engobj.wait_ge(sems[k], v)
                        waited[k] = v
                    if o.fn is None:
                        continue
                    name_, a_, k_ = o.fn
                    ins = getattr(engobj, name_)(*a_, **k_)
                    if o.signal:
                        ins.then_inc(sems[o.ticket[0]], o.inc)
                if engname == "sp":
                    for k, v in final.items():
                        engobj.wait_ge(sems[k], v)

            block.sync(lambda e: run("sp", e))
            block.tensor(lambda e: run("pe", e))
            block.scalar(lambda e: run("act", e))
            block.vector(lambda e: run("dve", e))
            block.gpsimd(lambda e: run("pool", e))


class Cfg:
    def __init__(self, nb=8, past=4096, upto=9, dbg=0):
        self.upto = upto
        self.dbg = dbg
        self.pairs = False
        import os
        self.skip = [int(v) for v in os.environ.get("KSKIP", "").split(",") if v]
        self.NB = nb
        self.SEQ = nb * 4 * 512
        self.PAST = past
        self.NT0 = 2 * nb + 2


W_SPECS = [
    ("w_ret_in", 1024, 6144, 0), ("w_ret_out", 2048, 1024, None),
    ("w_ffn_in0", 1024, 5632, 8), ("w_ffn_out0", 2816, 1024, None),
    ("w_ffn_in1", 1024, 5632, 16), ("w_ffn_out1", 2816, 1024, None),
    ("w_dkv", 1024, 288, 24), ("w_dq", 1024, 384, 32), ("w_uq", 384, 1536, 40),
    ("w_o", 1024, 1024, None), ("w_uk", 256, 1024, None), ("w_uv", 256, 1024, None),
]


class Prog:
    def __init__(self, cfg):
        self.cfg = cfg
        self.nc = bass.Bass("TRN2", target_bir_lowering=False)
        self.S = Sched(self.nc)
        self.rr = 0
        self.wslot = 0
        self.ctr = {}

    def din(self, name, shape, dt=F32):
        return self.nc.dram_tensor(name, list(shape), dt, kind="ExternalInput").ap()

    def dout(self, name, shape, dt=F32):
        return self.nc.dram_tensor(name, list(shape), dt, kind="ExternalOutput").ap()

    def dint(self, name, shape, dt):
        return self.nc.dram_tensor(name, list(shape), dt).ap()

    def view(self, nf32, shape, dt):
        off = self.aoff
        self.aoff += nf32
        assert self.aoff <= self.ARENA, (self.aoff, self.ARENA)
        a = self.AR[:shape[0], off:off + nf32]
        if dt != F32:
            a = a.bitcast(dt)
        if len(shape) == 3:
            a = a.rearrange("p (a b) -> p a b", b=shape[2])
        elif len(shape) == 4:
            a = a.rearrange("p (a b c) -> p a b c", b=shape[2], c=shape[3])
        return a

    def nxt(self, name, n):
        v = self.ctr.get(name, 0)
        self.ctr[name] = v + 1
        return v % n

    def dma(self, eng, out, in_, reads, writes, semkey):
        self.S.op(eng, lambda e: e.dma_start(out=out, in_=in_), reads=reads, writes=writes, dma=True, semkey=semkey)

    def ev_engine(self):
        self.rr += 1
        return ("dve", "act")[self.rr % 2]

    def copy(self, eng, out, in_, reads, writes):
        if eng == "act":
            self.S.op("act", lambda e: e.activation(out, in_, AF.Copy), reads=reads, writes=writes)
        else:
            self.S.op(eng, lambda e: e.tensor_copy(out, in_), reads=reads, writes=writes)

    def wslab(self, wname, r0, nk, kp, c0, ncols):
        slot = self.wslot % 4
        self.wslot += 1
        buf = self.WB[slot]
        key = ("W", slot)
        dst = buf[:kp, 0:nk * ncols].rearrange("p (k n) -> p k n", n=ncols)
        src = self.wb[wname][r0:r0 + nk * kp, c0:c0 + ncols].rearrange("(k p) n -> p k n", p=kp)
        self.dma("sp", dst, src, reads=[("wb", wname)], writes=[key], semkey=key)
        return dst, key

    def transposes(self, srcs, src_keys, dst, dst_keys, P, rows):
        n = len(srcs)
        for g0 in range(0, n, 8):
            g1 = min(n, g0 + 8)
            b = self.nxt("tpb", 2)
            pk = ("ps", b)
            pb = self.PSB[b]
            for i in range(g0, g1):
                o_ap = pb[:rows, (i - g0) * P:(i - g0 + 1) * P]
                s_ap = srcs[i]
                self.S.op("pe", lambda e, o_ap=o_ap, s_ap=s_ap: e.transpose(o_ap, s_ap, self.ident[:P, :P]),
                          reads=list(src_keys) + ["ident"], writes=[pk])
            src_v = pb[:rows, 0:(g1 - g0) * P].rearrange("p (a b) -> p a b", b=P)
            self.copy(self.ev_engine(), dst[:rows, g0:g1, :], src_v, reads=[pk], writes=list(dst_keys))

    def rsqrt(self, dst, src, scale, rkeys, wkeys):
        self.S.op("dve", lambda e: e.tensor_scalar(dst, src, scale, EPS, ALU.mult, ALU.add), reads=list(rkeys), writes=list(wkeys))
        self.S.op("act", lambda e: e.activation(dst, dst, AF.Sqrt), reads=list(wkeys), writes=list(wkeys))
        self.S.op("dve", lambda e: e.reciprocal(dst, dst), reads=list(wkeys), writes=list(wkeys))

    def rms_to_bf16(self, x_ap_fn, xkeys, out_fn, okeys, subs, P, tag):
        for s in subs:
            ssk = (tag, "ss", s)
            ss = self.ss[:P, s:s + 1]
            rs = self.rstd[:P, s:s + 1]
            self.S.op("dve", lambda e, ss=ss: e.memset(ss, 0.0), writes=[ssk])
            xa = x_ap_fn(s)
            self.S.op("act", lambda e, xa=xa, ss=ss: e.activation(self.junk[:P, :D], xa, AF.Square, accum_out=ss),
                      reads=list(xkeys(s)), writes=[ssk, "junk"])
            self.rsqrt(rs, ss, 1.0 / D, [ssk], [(tag, "rs", s)])
            oa = out_fn(s)
            self.S.op("act", lambda e, oa=oa, xa=xa, rs=rs: e.activation(oa, xa, AF.Copy, scale=rs),
                      reads=list(xkeys(s)) + [(tag, "rs", s)], writes=list(okeys(s)))

    def linear_T(self, actT, act_keys, wname, K, kp, col0, ncols_total, slabw, subs, P, evac, bank_of):
        nkc = K // kp
        kgs = [(a, min(8, nkc - a)) for a in range(0, nkc, 8)]
        j = 0
        for c0 in range(0, ncols_total, slabw):
            w = min(slabw, ncols_total - c0)
            par = self.nxt("linpar", 2)
            for gi, (k0, nk) in enumerate(kgs):
                wap, wkey = self.wslab(wname, k0 * kp, nk, kp, col0 + c0, w)
                for si, s in enumerate(subs):
                    b = bank_of(si, par)
                    pk = ("ps", b)
                    for kc in range(nk):
                        first = (gi == 0 and kc == 0)
                        last = (gi == len(kgs) - 1 and kc == nk - 1)
                        o_ap = self.PS[b][:P, 0:w]
                        l_ap = actT[:kp, k0 + kc, si * P:(si + 1) * P]
                        r_ap = wap[:kp, kc, :]
                        self.S.op("pe", lambda e, o_ap=o_ap, l_ap=l_ap, r_ap=r_ap, first=first, last=last:
                                  e.matmul(o_ap, l_ap, r_ap, start=first, stop=last),
                                  reads=list(act_keys) + [wkey], writes=[pk])
            for si, s in enumerate(subs):
                b = bank_of(si, par)
                evac(s, j, c0, w, self.PS[b][:P, 0:w], ("ps", b))
            j += 1

    def build(self):
        cfg, nc, S = self.cfg, self.nc, self.S
        NB, NT0, PAST, SEQ = cfg.NB, cfg.NT0, cfg.PAST, cfg.SEQ
        NKP = SEQ
        NKS = PAST + 64
        xp = self.din("xp", [NB, 512, D])
        xs = self.din("xs", [4, 64, D])
        st0 = self.din("st0", [4, 4, 256, 512])
        cckv = self.din("cckv", [4, PAST, 256])
        ckr = self.din("ckr", [4, PAST, 32])
        win = {}
        for name, K, N, g in W_SPECS:
            win[name] = self.din(name, [K, N])
        gtab = self.din("gtab", [128, 43])
        gckv = self.din("gckv", [128, 256])
        gfin = self.din("gfin", [128, D])
        dec_d = self.din("dec", [128, 12])
        m128_d = self.din("m128", [128, 4, 128])
        m64_d = self.din("m64", [128, 4, 128])
        cosr_d = self.din("cosr", [NT0, 128, 2, 128])
        sinr_d = self.din("sinr", [NT0, 128, 2, 128])
        cosm_d = self.din("cosm", [NT0, 128, 2, 16])
        sinm_d = self.din("sinm", [NT0, 128, 2, 16])
        pcoef_d = self.din("pcoef", [128, 24])
        aind_d = self.din("aind", [8, 4, 128], BF16)
        bm_d = self.din("bm", [8, 4, 512], BF16)
        ident_d = self.din("ident", [128, 128], BF16)
        esel_d = self.din("esel", [65, 64])

        yp = self.dout("yp", [NB, 512, D])
        ys = self.dout("ys", [4, 64, D])
        stp = self.dout("stp", [4, 256, 512])
        sts = self.dout("sts", [4, 4, 256, 512])
        ckvp = self.dout("ckvp", [NB, 512, 256])
        krp = self.dout("krp", [NB, 512, 32])
        ckvs = self.dout("ckvs", [4, 64, 256])
        krs = self.dout("krs", [4, 64, 32])

        self.wb = {name: self.dint("wb_" + name, [K, N], BF16) for name, K, N, g in W_SPECS}
        Lloc = self.dint("Lloc", [NB * 128, 4096], F32)
        Lall = self.dint("Lall", [4 * NB * 128, 4096], F32)
        Sst = self.dint("Sst", [NB, 128, 4096], F32)
        x2s = self.dint("x2s", [NB * 512 + 256, D], F32)
        cloc = self.dint("cloc", [NB * 512, 288], BF16)
        call = self.dint("call", [4 * NB * 512, 288], BF16)
        clocs = self.dint("clocs", [4, 64, 288], BF16)
        KTp = self.dint("KTp", [16, 96, NKP], BF16)
        Vp = self.dint("Vp", [NKP, 1040], BF16)
        KTs = self.dint("KTs", [4, 16, 96, NKS], BF16)
        Vs = self.dint("Vs", [4, NKS, 1040], BF16)

        st = contextlib.ExitStack()
        with st:
            sb = lambda name, shape, dt: st.enter_context(nc.sbuf_tensor("s_" + name, list(shape), dt))
            self.ident = sb("ident", [128, 128], BF16)
            dec = sb("dec", [128, 12], F32)
            m128 = sb("m128", [128, 4, 128], F32)
            m64 = sb("m64", [128, 4, 128], F32)
            pcoef = sb("pcoef", [128, 24], F32)
            gck = sb("gck", [128, 256], F32)
            gfn = sb("gfn", [128, D], F32)
            gt = sb("gt", [128, 43], F32)
            aind = sb("aind", [8, 4, 128], BF16)
            bm = sb("bm", [8, 4, 512], BF16)
            esel = sb("esel", [65, 64], F32)
            self.ss = sb("ss", [128, 8], F32)
            self.rstd = sb("rstd", [128, 8], F32)
            ssy = sb("ssy", [128, 8], F32)
            rsy = sb("rsy", [128, 8], F32)
            self.junk = sb("junk", [128, D], BF16)
            self.WB = [sb("wb%d" % i, [128, 4096], BF16) for i in range(4)]
            self.ARENA = 33280
            self.AR = sb("arena", [128, self.ARENA], F32)
            psall = st.enter_context(nc.psum_tensor("psall", [128, 4096], F32))
            self.PS = [psall[:, i * 512:(i + 1) * 512] for i in range(8)]
            self.PSB = [p.bitcast(BF16) for p in self.PS]
            PS, PSB = self.PS, self.PSB

            for (t, d_, k) in ((self.ident, ident_d, "ident"), (dec, dec_d, "dec"), (m128, m128_d, "m128"),
                               (m64, m64_d, "m64"), (pcoef, pcoef_d, "pcoef"), (gck, gckv, "gck"),
                               (gfn, gfin, "gfn"), (gt, gtab, "gt"), (aind, aind_d, "aind"), (bm, bm_d, "bm"),
                               (esel, esel_d, "esel")):
                nd = len(t.shape)
                self.dma("sp", t[tuple([slice(None)] * nd)], d_, reads=[], writes=[k], semkey=("c", self.nxt("csem", 2)))

            CW = 1536
            self.aoff = self.ARENA - 3 * (CW + CW // 2)
            conv_base = self.aoff
            cin = [self.view(CW, [128, CW], F32) for _ in range(3)]
            cout = [self.view(CW // 2, [128, CW], BF16) for _ in range(3)]
            conv_state = {"ci": 0}

            def conv_chunk(name, g, kc, n0, w):
                ci = conv_state["ci"]
                sl = ci % 3
                conv_state["ci"] = ci + 1
                self.dma("sp", cin[sl][:, 0:w], win[name][kc * 128:(kc + 1) * 128, n0:n0 + w],
                         reads=[], writes=[("cin", sl)], semkey=("cin", sl))
                eng = ("dve", "pool", "act")[ci % 3] if g is None else ("dve", "act")[ci % 2]
                o_ap, i_ap = cout[sl][:, 0:w], cin[sl][:, 0:w]
                if g is None:
                    self.copy(eng, o_ap, i_ap, reads=[("cin", sl)], writes=[("cout", sl)])
                else:
                    sc = gt[:, g + kc:g + kc + 1]
                    if eng == "act":
                        S.op("act", lambda e: e.activation(o_ap, i_ap, AF.Copy, scale=sc),
                             reads=[("cin", sl), "gt"], writes=[("cout", sl)])
                    else:
                        S.op(eng, lambda e: e.tensor_scalar(o_ap, i_ap, sc, None, ALU.mult),
                             reads=[("cin", sl), "gt"], writes=[("cout", sl)])
                self.dma("pool", self.wb[name][kc * 128:(kc + 1) * 128, n0:n0 + w], o_ap,
                         reads=[("cout", sl)], writes=[("wb", name)], semkey="wst")

            conv_todo = []
            for name, K, N, g in W_SPECS:
                for kc in range(K // 128):
                    for n0 in range(0, N, CW):
                        conv_todo.append((name, g, kc, n0, min(CW, N - n0)))
            n_first = sum(1 for c_ in conv_todo if c_[0] == "w_ret_in")
            for c_ in conv_todo[:n_first]:
                conv_chunk(*c_)
            conv_todo = conv_todo[n_first:]
            S.mute = cfg.upto < 1 or 1 in cfg.skip

            self.aoff = 0
            xt = self.view(2048, [128, 2, D], F32)
            hb = self.view(1024, [128, 2, D], BF16)
            hT = self.view(1024, [128, 8, 256], BF16)
            qb = self.view(1024, [128, 2, D], BF16)
            kb = self.view(1024, [128, 2, D], BF16)
            vb = self.view(2048, [128, 2, 2048], BF16)
            sgb = self.view(2048, [128, 2, 2048], BF16)
            qT = self.view(512, [128, 8, 128], BF16)
            kT = self.view(512, [128, 8, 128], BF16)
            aTs = self.view(256, [128, 4, 128], BF16)
            ygb = self.view(1024, [128, 2048], BF16)
            ygT = self.view(2048, [128, 16, 256], BF16)
            hidT = self.view(2816, [128, 22, 256], BF16)
            cosr = self.view(256, [128, 2, 128], F32)
            sinr = self.view(256, [128, 2, 128], F32)
            cosm = self.view(32, [128, 2, 16], F32)
            sinm = self.view(32, [128, 2, 16], F32)
            rt = [self.view(128, [128, 128], F32) for _ in range(8)]
            Sf = self.view(4096, [128, 8, 512], F32)
            Sb = self.view(2048, [128, 8, 512], BF16)
            sgt = self.view(256, [128, 256], F32)
            ckvo = self.view(512, [128, 2, 256], F32)
            kro = self.view(64, [128, 2, 32], F32)
            cbf = self.view(288, [128, 2, 288], BF16)
            krt = [self.view(16, [128, 16], F32) for _ in range(4)]
            assert self.aoff <= conv_base, (self.aoff, conv_base)

            def bank_l0(si, par):
                return 2 + si + 2 * par

            def x_src(t):
                if t < 2 * NB:
                    m, hf = t // 2, t % 2
                    return xp[m, hf * 256:(hf + 1) * 256, :].rearrange("(s p) d -> p s d", p=128), 128
                j = t - 2 * NB
                return xs[2 * j:2 * j + 2, :, :].rearrange("s p d -> p s d"), 64

            def load_x(t):
                src, P = x_src(t)
                self.dma("sp", xt[:P, :, :], src, reads=[], writes=[("xt", 0), ("xt", 1)], semkey="xt")
                return P

            def load_rope(t):
                self.dma("sp", cosr[:, :, :], cosr_d[t], reads=[], writes=["cosr"], semkey="cosr")
                self.dma("sp", sinr[:, :, :], sinr_d[t], reads=[], writes=["sinr"], semkey="sinr")
                self.dma("sp", cosm[:, :, :], cosm_d[t], reads=[], writes=["cosm"], semkey="cosm")
                self.dma("sp", sinm[:, :, :], sinm_d[t], reads=[], writes=["sinm"], semkey="sinm")

            def norm_T(P, tag):
                self.rms_to_bf16(lambda s: xt[:P, s, :], lambda s: [("xt", s)],
                                 lambda s: hb[:P, s, :], lambda s: [("hb", s)], (0, 1), P, tag)
                for s in (0, 1):
                    self.transposes([hb[:P, s, kc * 128:(kc + 1) * 128] for kc in range(8)], [("hb", s)],
                                    hT[:, :, s * P:(s + 1) * P], [("hT", s)], P, 128)

            def rope_evac(ps_ap, pk, s, j, dst, dkey, P, dcol):
                for hh in range(2):
                    h = 2 * (j % 2) + hh
                    dsc = dec[:P, dcol + h:dcol + h + 1]
                    x1 = ps_ap[:, hh * 256:hh * 256 + 128]
                    x2 = ps_ap[:, hh * 256 + 128:hh * 256 + 256]
                    cs = cosr[:P, s, :]
                    sn = sinr[:P, s, :]
                    i0 = self.nxt("rt", 2) * 4
                    t1, t2, t3, t4 = rt[i0][:P, :], rt[i0 + 1][:P, :], rt[i0 + 2][:P, :], rt[i0 + 3][:P, :]
                    rk = [("rt", i0 + i) for i in range(4)]
                    for (tt, xx, tb, tk, tbk) in ((t1, x1, cs, rk[0], "cosr"), (t2, x2, sn, rk[1], "sinr"),
                                                  (t3, x2, cs, rk[2], "cosr"), (t4, x1, sn, rk[3], "sinr")):
                        S.op("dve", lambda e, tt=tt, xx=xx, tb=tb, dsc=dsc: e.scalar_tensor_tensor(tt, xx, dsc, tb, ALU.mult, ALU.mult),
                             reads=[pk, "dec", tbk], writes=[tk])
                    c = (j % 2) * 512 + hh * 256
                    o1 = dst[:P, s, c:c + 128]
                    o2 = dst[:P, s, c + 128:c + 256]
                    S.op("pool", lambda e, o1=o1, t1=t1, t2=t2: e.tensor_tensor(o1, t1, t2, ALU.subtract),
                         reads=[rk[0], rk[1]], writes=[dkey])
                    S.op("pool", lambda e, o2=o2, t3=t3, t4=t4: e.tensor_tensor(o2, t3, t4, ALU.add),
                         reads=[rk[2], rk[3]], writes=[dkey])

            def proj_qkvg(P, pass1):
                dk_col = 4 if P == 128 else 8

                def evac(s, j, c0, w, ps_ap, pk):
                    jj = c0 // 512
                    if jj < 2:
                        rope_evac(ps_ap, pk, s, jj, qb, ("qb", s), P, 0)
                    elif jj < 4:
                        rope_evac(ps_ap, pk, s, jj, kb, ("kb", s), P, dk_col)
                    elif jj < 8:
                        o_ap = vb[:P, s, (jj - 4) * 512:(jj - 3) * 512]
                        self.copy(self.ev_engine(), o_ap, ps_ap, reads=[pk], writes=[("vb", s)])
                    else:
                        o_ap = sgb[:P, s, (jj - 8) * 512:(jj - 7) * 512]
                        S.op("act", lambda e, o_ap=o_ap, ps_ap=ps_ap: e.activation(o_ap, ps_ap, AF.Silu),
                             reads=[pk], writes=[("sgb", s)])
                if pass1:
                    self.linear_T(hT, [("hT", 0), ("hT", 1)], "w_ret_in", 1024, 128, 1024, 3072, 512, (0, 1), P,
                                  lambda s, j, c0, w, a, k: evac(s, j, c0 + 1024, w, a, k), bank_l0)
                else:
                    self.linear_T(hT, [("hT", 0), ("hT", 1)], "w_ret_in", 1024, 128, 0, 6144, 512, (0, 1), P, evac, bank_l0)

            def state_update(s, P, h):
                gP = GAM[h] ** P
                for dc in range(2):
                    b = 5 + dc
                    o_ap = PS[b][:, :]
                    l_ap = kb[:P, s, (2 * h + dc) * 128:(2 * h + dc + 1) * 128]
                    r_ap = vb[:P, s, h * 512:(h + 1) * 512]
                    S.op("pe", lambda e, o_ap=o_ap, l_ap=l_ap, r_ap=r_ap: e.matmul(o_ap, l_ap, r_ap, start=True, stop=True),
                         reads=[("kb", s), ("vb", s)], writes=[("ps", b)])
                    sf = Sf[:, 2 * h + dc, :]
                    S.op("dve", lambda e, sf=sf, o_ap=o_ap, gP=gP: e.scalar_tensor_tensor(sf, sf, gP, o_ap, ALU.mult, ALU.add),
                         reads=[("ps", b), ("Sf", h)], writes=[("Sf", h)])

            def cast_state(h):
                for dc in range(2):
                    self.copy("pool", Sb[:, 2 * h + dc, :], Sf[:, 2 * h + dc, :], reads=[("Sf", h)], writes=[("Sb", h)])

            def ret_core(s, P):
                msk = m128 if P == 128 else m64
                mk = "m128" if P == 128 else "m64"
                self.transposes([qb[:P, s, c * 128:(c + 1) * 128] for c in range(8)], [("qb", s)], qT[:, :, :P], ["qT"], P, 128)
                self.transposes([kb[:P, s, c * 128:(c + 1) * 128] for c in range(8)], [("kb", s)], kT[:, :, :P], ["kT"], P, 128)
                for h in range(4):
                    for dc in range(2):
                        o_ap = PS[2][:P, h * 128:h * 128 + P]
                        l_ap, r_ap = kT[:, 2 * h + dc, :P], qT[:, 2 * h + dc, :P]
                        S.op("pe", lambda e, o_ap=o_ap, l_ap=l_ap, r_ap=r_ap, dc=dc: e.matmul(o_ap, l_ap, r_ap, start=(dc == 0), stop=(dc == 1)),
                             reads=["qT", "kT"], writes=[("ps", 2)])
                    a_ap = aTs[:P, h, :P]
                    p_ap = PS[2][:P, h * 128:h * 128 + P]
                    m_ap = msk[:P, h, :P]
                    S.op("dve", lambda e, a_ap=a_ap, p_ap=p_ap, m_ap=m_ap: e.tensor_tensor(a_ap, p_ap, m_ap, ALU.mult),
                         reads=[("ps", 2), mk], writes=[("aTs", h)])
                    yb = 3 + (h % 2)
                    y_ap = PS[yb][:P, :]
                    r_ap = vb[:P, s, h * 512:(h + 1) * 512]
                    S.op("pe", lambda e, y_ap=y_ap, a_ap=a_ap, r_ap=r_ap: e.matmul(y_ap, a_ap, r_ap, start=True, stop=False),
                         reads=[("aTs", h), ("vb", s)], writes=[("ps", yb)])
                    for dc in range(2):
                        l_ap, r2 = qT[:, 2 * h + dc, :P], Sb[:, 2 * h + dc, :]
                        S.op("pe", lambda e, y_ap=y_ap, l_ap=l_ap, r2=r2, dc=dc: e.matmul(y_ap, l_ap, r2, start=False, stop=(dc == 1)),
                             reads=["qT", ("Sb", h)], writes=[("ps", yb)])
                    state_update(s, P, h)
                    cast_state(h)
                    sy = ssy[:P, h:h + 1]
                    ry = rsy[:P, h:h + 1]
                    S.op("dve", lambda e, sy=sy: e.memset(sy, 0.0), writes=[("ssy", h)])
                    S.op("act", lambda e, y_ap=y_ap, sy=sy: e.activation(self.junk[:P, 0:512], y_ap, AF.Square, accum_out=sy),
                         reads=[("ps", yb)], writes=[("ssy", h), "junk"])
                    self.rsqrt(ry, sy, 1.0 / 512, [("ssy", h)], [("rsy", h)])
                    g_ap = sgb[:P, s, h * 512:(h + 1) * 512]
                    o_ap = ygb[:P, h * 512:(h + 1) * 512]
                    S.op("dve", lambda e, o_ap=o_ap, y_ap=y_ap, ry=ry, g_ap=g_ap: e.scalar_tensor_tensor(o_ap, y_ap, ry, g_ap, ALU.mult, ALU.mult),
                         reads=[("ps", yb), ("rsy", h), ("sgb", s)], writes=["ygb"])
                self.transposes([ygb[:P, c * 128:(c + 1) * 128] for c in range(16)], ["ygb"], ygT[:, :, s * P:(s + 1) * P], [("ygT", s)], P, 128)

            def resid_evac(s, j, c0, w, ps_ap, pk, P):
                x_ap = xt[:P, s, c0:c0 + w]
                S.op("dve", lambda e, x_ap=x_ap, ps_ap=ps_ap: e.tensor_tensor(x_ap, ps_ap, x_ap, ALU.add),
                     reads=[pk, ("xt", s)], writes=[("xt", s)])

            for m in range(NB):
                for h in range(4):
                    for dc in range(2):
                        sf = Sf[:, 2 * h + dc, :]
                        S.op("pool", lambda e, sf=sf: e.memset(sf, 0.0), writes=[("Sf", h)])
                for hf in range(2):
                    t = 2 * m + hf
                    load_rope(t)
                    P = load_x(t)
                    norm_T(P, "n0")
                    proj_qkvg(P, True)
                    for s in (0, 1):
                        for h in range(4):
                            state_update(s, P, h)
                    per_tile = -(-len(conv_todo) // max(1, 2 * NB - t)) if t < 2 * NB - 1 else len(conv_todo)
                    for c_ in conv_todo[:per_tile]:
                        conv_chunk(*c_)
                    conv_todo = conv_todo[per_tile:]
                self.dma("pool", Lloc[m * 128:(m + 1) * 128, :], Sf[:, :, :].rearrange("p a b -> p (a b)"),
                         reads=[("Sf", h) for h in range(4)], writes=[("Lloc", m)], semkey="Lst")
                for half in range(2):
                    ch = 2 * m + half
                    i_ap = Lloc[ch * 64:(ch + 1) * 64, :]
                    o_ap = Lall[ch * 256:(ch + 1) * 256, :]
                    S.op("pool", lambda e, i_ap=i_ap, o_ap=o_ap: e.collective_compute(
                        "AllGather", ALU.bypass, replica_groups=[[0, 1, 2, 3], [4, 5, 6, 7]], ins=[i_ap.opt()], outs=[o_ap.opt()]),
                         reads=[("Lloc", m)], writes=[("Lall", ch)], dma=True, semkey="cc1", inc=1)
            S.barrier()
            S.mute = cfg.upto < 2 or 2 in cfg.skip

            save = self.aoff
            self.aoff = 0
            Tst = self.view(4096, [128, 8, 512], F32)
            Acc = self.view(4096, [128, 8, 512], F32)
            Lb = [self.view(4096, [128, 8, 512], F32) for _ in range(2)]
            for a in range(8):
                ta = Tst[:, a, :]
                S.op("pool", lambda e, ta=ta: e.memset(ta, 0.0), writes=[("Tst", a)])
            for m in range(NB):
                for h in range(4):
                    for dc in range(2):
                        a_ap, t_ap = Acc[:, 2 * h + dc, :], Tst[:, 2 * h + dc, :]
                        sc = pcoef[:, h:h + 1]
                        S.op("dve", lambda e, a_ap=a_ap, t_ap=t_ap, sc=sc: e.tensor_scalar(a_ap, t_ap, sc, None, ALU.mult),
                             reads=[("Tst", 2 * h + dc), "pcoef"], writes=[("Acc", 2 * h + dc)])
                        g2048 = GAM[h] ** 2048
                        S.op("pool", lambda e, t_ap=t_ap, g2048=g2048: e.tensor_scalar(t_ap, t_ap, g2048, None, ALU.mult),
                             reads=[("Tst", 2 * h + dc)], writes=[("Tst", 2 * h + dc)])
                for i in range(4):
                    sl = self.nxt("Lb", 2)
                    for half in range(2):
                        r0 = ((2 * m + half) * 4 + i) * 64
                        self.dma("sp", Lb[sl][half * 64:(half + 1) * 64, :, :].rearrange("p a b -> p (a b)"), Lall[r0:r0 + 64, :],
                                 reads=[("Lall", 2 * m + half)], writes=[("Lb", sl)], semkey=("Lb", sl, half))
                    for h in range(4):
                        for dc in range(2):
                            a_ap, t_ap, l_ap = Acc[:, 2 * h + dc, :], Tst[:, 2 * h + dc, :], Lb[sl][:, 2 * h + dc, :]
                            sc = pcoef[:, 4 + 4 * i + h:4 + 4 * i + h + 1]
                            S.op("dve", lambda e, a_ap=a_ap, l_ap=l_ap, sc=sc: e.scalar_tensor_tensor(a_ap, l_ap, sc, a_ap, ALU.mult, ALU.add),
                                 reads=[("Lb", sl), "pcoef", ("Acc", 2 * h + dc)], writes=[("Acc", 2 * h + dc)])
                            cT = GAM[h] ** (512 * (3 - i))
                            S.op("dve", lambda e, t_ap=t_ap, l_ap=l_ap, cT=cT: e.scalar_tensor_tensor(t_ap, l_ap, cT, t_ap, ALU.mult, ALU.add),
                                 reads=[("Lb", sl), ("Tst", 2 * h + dc)], writes=[("Tst", 2 * h + dc)])
                self.dma("pool", Sst[m], Acc[:, :, :].rearrange("p a b -> p (a b)"), reads=[("Acc", a) for a in range(8)], writes=["Sst"], semkey="Sst")
            self.aoff = save
            S.barrier()
            S.mute = cfg.upto < 3 or 3 in cfg.skip

            def ffn(l, P, subs_x, x_of, xkey_of, tagn):
                self.rms_to_bf16(lambda s: x_of(s), lambda s: [xkey_of(s)],
                                 lambda s: hb[:P, s, :], lambda s: [("hb", s)], (0, 1), P, tagn)
                for s in (0, 1):
                    self.transposes([hb[:P, s, kc * 128:(kc + 1) * 128] for kc in range(8)], [("hb", s)],
                                    hT[:, :, s * P:(s + 1) * P], [("hT", s)], P, 128)
                T = 2 * P
                wn = "w_ffn_in%d" % l
                for g0 in range(0, 22, 4):
                    ng = min(4, 22 - g0)
                    wg, wgk = self.wslab(wn, 0, 8, 128, g0 * 128, ng * 128)
                    wu, wuk = self.wslab(wn, 0, 8, 128, DFF + g0 * 128, ng * 128)
                    for jj in range(ng):
                        oc = g0 + jj
                        b = 2 + self.nxt("ffb", 4)
                        pk = ("ps", b)
                        for (wap, wk, off) in ((wg, wgk, 0), (wu, wuk, 256)):
                            for kc in range(8):
                                o_ap = PS[b][:, off:off + T]
                                l_ap = wap[:, kc, jj * 128:(jj + 1) * 128]
                                r_ap = hT[:, kc, 0:T]
                                S.op("pe", lambda e, o_ap=o_ap, l_ap=l_ap, r_ap=r_ap, kc=kc: e.matmul(o_ap, l_ap, r_ap, start=(kc == 0), stop=(kc == 7)),
                                     reads=[("hT", 0), ("hT", 1), wk], writes=[pk])
                        g_ap = PS[b][:, 0:T]
                        u_ap = PS[b][:, 256:256 + T]
                        sl = self.nxt("sgt", 1)
                        t_ap = sgt[:, 0:T]
                        S.op("act", lambda e, t_ap=t_ap, g_ap=g_ap: e.activation(t_ap, g_ap, AF.Silu), reads=[pk], writes=["sgt"])
                        h_ap = hidT[:, oc, 0:T]
                        S.op("dve", lambda e, h_ap=h_ap, t_ap=t_ap, u_ap=u_ap: e.tensor_tensor(h_ap, t_ap, u_ap, ALU.mult),
                             reads=["sgt", pk], writes=["hidT"])
                self.linear_T(hidT, ["hidT"], "w_ffn_out%d" % l, DFF, 128, 0, D, 512, (0, 1), P,
                              lambda s, j, c0, w, a, k: subs_x(s, c0, w, a, k), bank_l0)

            def l0_resid(P):
                return lambda s, c0, w, a, k: resid_evac(s, 0, c0, w, a, k, P)

            for t in range(NT0):
                prompt = t < 2 * NB
                m, hf = t // 2, t % 2
                load_rope(t)
                if prompt and hf == 0:
                    self.dma("sp", Sf[:, :, :].rearrange("p a b -> p (a b)"), Sst[m], reads=["Sst"],
                             writes=[("Sf", h) for h in range(4)], semkey="Sfl")
                    for h in range(4):
                        cast_state(h)
                P = load_x(t)
                norm_T(P, "n0")
                proj_qkvg(P, False)
                for s in (0, 1):
                    if not prompt:
                        sq = (t - 2 * NB) * 2 + s
                        self.dma("sp", Sf[:, :, :], st0[sq].rearrange("h (dc p) e -> p (h dc) e", p=128), reads=[],
                                 writes=[("Sf", h) for h in range(4)], semkey="Sfl")
                        for h in range(4):
                            cast_state(h)
                    ret_core(s, P)
                    if not prompt:
                        self.dma("pool", sts[sq].rearrange("h (dc p) e -> p (h dc) e", p=128), Sf[:, :, :],
                                 reads=[("Sf", h) for h in range(4)], writes=[], semkey="Sfs")
                if prompt and m == NB - 1 and hf == 1:
                    self.dma("pool", stp.rearrange("h (dc p) e -> p (h dc) e", p=128), Sf[:, :, :],
                             reads=[("Sf", h) for h in range(4)], writes=[], semkey="Sfs")
                self.linear_T(ygT, [("ygT", 0), ("ygT", 1)], "w_ret_out", 2048, 128, 0, D, 512, (0, 1), P,
                              lambda s, j, c0, w, a, k: resid_evac(s, j, c0, w, a, k, P), bank_l0)
                def dbg_store(P=P, t=t, m=m, hf=hf, prompt=prompt):
                    if prompt:
                        self.dma("pool", yp[m, hf * 256:(hf + 1) * 256, :].rearrange("(s p) d -> p s d", p=128), xt[:, :, :],
                                 reads=[("xt", 0), ("xt", 1)], writes=[], semkey="o_y")
                    else:
                        j2_ = (t - 2 * NB) * 2
                        self.dma("pool", ys[j2_:j2_ + 2, :, :].rearrange("s p d -> p s d"), xt[:64, :, :],
                                 reads=[("xt", 0), ("xt", 1)], writes=[], semkey="o_y")
                if cfg.dbg == 1:
                    dbg_store()
                ffn(0, P, l0_resid(P), lambda s: xt[:P, s, :], lambda s: ("xt", s), "n0")
                if cfg.dbg == 2:
                    dbg_store()
                norm_T(P, "n0")

                def kv_evac(s, j, c0, w, ps_ap, pk):
                    sk = ("kvss", s)
                    ssa, rsa = self.ss[:P, 4 + s:5 + s], self.rstd[:P, 4 + s:5 + s]
                    S.op("dve", lambda e: e.memset(ssa, 0.0), writes=[sk])
                    S.op("act", lambda e: e.activation(self.junk[:P, 0:256], ps_ap[:, 0:256], AF.Square, accum_out=ssa),
                         reads=[pk], writes=[sk, "junk"])
                    self.rsqrt(rsa, ssa, 1.0 / 256, [sk], [("kvrs", s)])
                    co = ckvo[:P, s, :]
                    S.op("dve", lambda e: e.scalar_tensor_tensor(co, ps_ap[:, 0:256], rsa, gck[:P, :], ALU.mult, ALU.mult),
                         reads=[pk, ("kvrs", s), "gck"], writes=[("ckvo", s)])
                    self.copy("pool", cbf[:P, s, 0:256], co, reads=[("ckvo", s)], writes=[("cbf", s)])
                    x1, x2 = ps_ap[:, 256:272], ps_ap[:, 272:288]
                    cs, sn = cosm[:P, s, :], sinm[:P, s, :]
                    t1, t2, t3, t4 = [k_[:P, :] for k_ in krt]
                    for (tt, xx, tb, i_, tbk) in ((t1, x1, cs, 0, "cosm"), (t2, x2, sn, 1, "sinm"), (t3, x2, cs, 2, "cosm"), (t4, x1, sn, 3, "sinm")):
                        S.op("dve", lambda e, tt=tt, xx=xx, tb=tb: e.tensor_tensor(tt, xx, tb, ALU.mult),
                             reads=[pk, tbk], writes=[("krt", i_)])
                    S.op("pool", lambda e: e.tensor_tensor(kro[:P, s, 0:16], t1, t2, ALU.subtract),
                         reads=[("krt", 0), ("krt", 1)], writes=[("kro", s)])
                    S.op("pool", lambda e: e.tensor_tensor(kro[:P, s, 16:32], t3, t4, ALU.add),
                         reads=[("krt", 2), ("krt", 3)], writes=[("kro", s)])
                    self.copy("pool", cbf[:P, s, 256:288], kro[:P, s, :], reads=[("kro", s)], writes=[("cbf", s)])

                self.linear_T(hT, [("hT", 0), ("hT", 1)], "w_dkv", 1024, 128, 0, 288, 288, (0, 1), P, kv_evac, bank_l0)
                if prompt:
                    r0 = m * 512 + hf * 256
                    self.dma("pool", ckvp[m, hf * 256:(hf + 1) * 256, :].rearrange("(s p) c -> p s c", p=128), ckvo[:, :, :],
                             reads=[("ckvo", 0), ("ckvo", 1)], writes=[], semkey="o_ckv")
                    self.dma("pool", krp[m, hf * 256:(hf + 1) * 256, :].rearrange("(s p) c -> p s c", p=128), kro[:, :, :],
                             reads=[("kro", 0), ("kro", 1)], writes=[], semkey="o_kr")
                    self.dma("pool", cloc[r0:r0 + 256, :].rearrange("(s p) c -> p s c", p=128), cbf[:, :, :],
                             reads=[("cbf", 0), ("cbf", 1)], writes=["cloc"], semkey="o_cb")
                    self.dma("pool", x2s[r0:r0 + 256, :].rearrange("(s p) c -> p s c", p=128), xt[:, :, :],
                             reads=[("xt", 0), ("xt", 1)], writes=["x2s"], semkey="o_x2")
                else:
                    j2 = (t - 2 * NB) * 2
                    self.dma("pool", ckvs[j2:j2 + 2, :, :].rearrange("s p c -> p s c"), ckvo[:64, :, :],
                             reads=[("ckvo", 0), ("ckvo", 1)], writes=[], semkey="o_ckv")
                    self.dma("pool", krs[j2:j2 + 2, :, :].rearrange("s p c -> p s c"), kro[:64, :, :],
                             reads=[("kro", 0), ("kro", 1)], writes=[], semkey="o_kr")
                    self.dma("pool", clocs[j2:j2 + 2, :, :].rearrange("s p c -> p s c"), cbf[:64, :, :],
                             reads=[("cbf", 0), ("cbf", 1)], writes=["clocs"], semkey="o_cb")
                    r0 = NB * 512 + j2 * 64
                    self.dma("pool", x2s[r0:r0 + 128, :].rearrange("(s p) c -> p s c", p=64), xt[:64, :, :],
                             reads=[("xt", 0), ("xt", 1)], writes=["x2s"], semkey="o_x2")
            CR = min(1024, NB * 512)
            for ch in range(NB * 512 // CR):
                i_ap = cloc[ch * CR:(ch + 1) * CR, :]
                o_ap = call[ch * 4 * CR:(ch + 1) * 4 * CR, :]
                S.op("pool", lambda e, i_ap=i_ap, o_ap=o_ap: e.collective_compute(
                    "AllGather", ALU.bypass, replica_groups=[[0, 1, 2, 3], [4, 5, 6, 7]], ins=[i_ap.opt()], outs=[o_ap.opt()]),
                     reads=["cloc"], writes=["call"], dma=True, semkey="cc2", inc=1)
            S.barrier()
            S.mute = cfg.upto < 4 or 4 in cfg.skip

            self.aoff = 0
            wuk = self.view(1024, [128, 2, 1024], BF16)
            wuv = self.view(1024, [128, 2, 1024], BF16)
            ctf = [self.view(1152, [128, 4, 288], F32) for _ in range(2)]
            ctb = [self.view(576, [128, 4, 288], BF16) for _ in range(2)]
            ckT = self.view(512, [128, 2, 512], BF16)
            krT = self.view(256, [32, 512], BF16)
            kst = [self.view(4096, [64, 16, 512], BF16) for _ in range(2)]
            vst = [self.view(2080, [128, 4, 16, 65], BF16) for _ in range(2)]
            self.dma("sp", wuk[:, :, :], self.wb["w_uk"].rearrange("(k p) n -> p k n", p=128), reads=[], writes=["wuk"], semkey="wuk")
            self.dma("sp", wuv[:, :, :], self.wb["w_uv"].rearrange("(k p) n -> p k n", p=128), reads=[], writes=["wuv"], semkey="wuv")
            for sl in range(2):
                v_ = vst[sl]
                for kt_ in range(4):
                    S.op("dve", lambda e, v_=v_, kt_=kt_: e.memset(v_[:, kt_, :, 64:65], 1.0), writes=[("vst", sl)])

            def kv_block(src_fn, KT_dst, V_dst, k0, tiles):
                sl = self.nxt("kvsl", 2)
                src_fn(sl)
                nkeys = sum(tiles)
                cb_ = ctb[sl]
                col = 0
                for kt, nk in enumerate(tiles):
                    self.transposes([cb_[:nk, kt, c * 128:(c + 1) * 128] for c in range(2)], [("ctb", sl)],
                                    ckT[:, :, col:col + nk], ["ckT"], nk, 128)
                    self.transposes([cb_[:nk, kt, 256:288]], [("ctb", sl)],
                                    krT[:, col:col + nk].rearrange("p (a b) -> p a b", a=1), ["krT"], nk, 32)
                    col += nk
                ks = kst[sl]
                for h in range(16):
                    b = 2 + self.nxt("kvb", 4)
                    for cc in range(2):
                        o_ap, l_ap, r_ap = PS[b][:64, 0:nkeys], wuk[:, cc, h * 64:(h + 1) * 64], ckT[:, cc, 0:nkeys]
                        S.op("pe", lambda e, o_ap=o_ap, l_ap=l_ap, r_ap=r_ap, cc=cc: e.matmul(o_ap, l_ap, r_ap, start=(cc == 0), stop=(cc == 1)),
                             reads=["wuk", "ckT"], writes=[("ps", b)])
                    self.copy(self.ev_engine(), ks[:, h, 0:nkeys], PS[b][:64, 0:nkeys], reads=[("ps", b)], writes=[("kst", sl)])
                self.dma("pool", KT_dst[:, 0:64, k0:k0 + nkeys].rearrange("h r k -> r h k"), ks[:, :, 0:nkeys],
                         reads=[("kst", sl)], writes=["KT"], semkey=("kst", sl))
                self.dma("pool", KT_dst[:, 64:96, k0:k0 + nkeys].rearrange("h r k -> r h k"),
                         krT[:, 0:nkeys].unsqueeze(1).to_broadcast([32, 16, nkeys]),
                         reads=["krT"], writes=["KT"], semkey=("krs", sl))
                vs_ = vst[sl]
                col = 0
                for kt, nk in enumerate(tiles):
                    for half in range(2):
                        b = 6 + half
                        for cc in range(2):
                            o_ap, l_ap, r_ap = PS[b][:nk, :], ckT[:, cc, col:col + nk], wuv[:, cc, half * 512:(half + 1) * 512]
                            S.op("pe", lambda e, o_ap=o_ap, l_ap=l_ap, r_ap=r_ap, cc=cc: e.matmul(o_ap, l_ap, r_ap, start=(cc == 0), stop=(cc == 1)),
                                 reads=["wuv", "ckT"], writes=[("ps", b)])
                        self.copy(self.ev_engine(), vs_[:nk, kt, half * 8:(half + 1) * 8, 0:64],
                                  PS[b][:nk, :].rearrange("p (a b) -> p a b", b=64), reads=[("ps", b)], writes=[("vst", sl)])
                    col += nk
                nt = len(tiles)
                if tiles[0] == 128:
                    self.dma("pool", V_dst[k0:k0 + nkeys, :].rearrange("(t p) c -> p t c", p=128),
                             vs_[:, 0:nt, :, :].rearrange("p t a b -> p t (a b)"), reads=[("vst", sl)], writes=["V"], semkey=("vst", sl))
                else:
                    self.dma("pool", V_dst[k0:k0 + nkeys, :], vs_[:nkeys, 0, :, :].rearrange("p a b -> p (a b)"),
                             reads=[("vst", sl)], writes=["V"], semkey=("vst", sl))

            for j in range(4 * NB):
                m_, c_ = j // 4, j % 4
                r0 = ((m_ * 512) // CR) * 4 * CR + c_ * CR + (m_ * 512) % CR

                def src(sl, r0=r0):
                    self.dma("sp", ctb[sl][:, :, :], call[r0:r0 + 512, :].rearrange("(t p) c -> p t c", p=128),
                             reads=["call"], writes=[("ctb", sl)], semkey=("ctb", sl))
                kv_block(src, KTp, Vp, j * 512, [128] * 4)
            for q in range(4):
                for kbk in range(PAST // 512):
                    def src(sl, q=q, kbk=kbk):
                        self.dma("sp", ctf[sl][:, :, 0:256], cckv[q, kbk * 512:(kbk + 1) * 512, :].rearrange("(t p) c -> p t c", p=128),
                                 reads=[], writes=[("ctf", sl)], semkey=("ctf", sl))
                        self.dma("sp", ctf[sl][:, :, 256:288], ckr[q, kbk * 512:(kbk + 1) * 512, :].rearrange("(t p) c -> p t c", p=128),
                                 reads=[], writes=[("ctf", sl)], semkey=("ctf2", sl))
                        for kt_ in range(4):
                            self.copy(self.ev_engine(), ctb[sl][:, kt_, :], ctf[sl][:, kt_, :], reads=[("ctf", sl)], writes=[("ctb", sl)])
                    kv_block(src, KTs[q], Vs[q], kbk * 512, [128] * 4)

                def src(sl, q=q):
                    self.dma("sp", ctb[sl][:64, 0, :], clocs[q], reads=["clocs"], writes=[("ctb", sl)], semkey=("ctb", sl))
                kv_block(src, KTs[q], Vs[q], PAST, [64])
            S.barrier()
            S.mute = cfg.upto < 5

            self.aoff = 0
            xt4 = self.view(4096, [128, 4, D], F32)
            hb = self.view(1024, [128, 2, D], BF16)
            hT = self.view(1024, [128, 8, 256], BF16)
            cqb = self.view(384, [128, 2, 384], BF16)
            cqT = self.view(384, [128, 3, 256], BF16)
            qtok = self.view(3072, [128, 4, 16, 96], BF16)
            QT = self.view(1024, [96, 4, 512], BF16)
            KTt = [self.view(1024, [96, 4, 512], BF16) for _ in range(3)]
            Vt = [self.view(520, [128, 4, 260], BF16) for _ in range(3)]
            PT = [self.view(256, [128, 512], BF16) for _ in range(4)]
            PT2 = [self.view(512, [128, 1024], BF16) for _ in range(3)]
            OTs = [self.view(512, [65, 512], F32) for _ in range(2)]
            oT = self.view(4096, [64, 16, 512], BF16)
            rD = self.view(512, [64, 512], F32)
            hidT = self.view(2816, [128, 22, 256], BF16)
            sgt = self.view(256, [128, 256], F32)
            yo = self.view(2048, [128, 2, D], F32)
            cosm = self.view(32, [128, 2, 16], F32)
            sinm = self.view(32, [128, 2, 16], F32)
            krt = [self.view(16, [128, 16], F32) for _ in range(4)]

            def bank_l1(si, par):
                return 2 + si + 2 * par

            def attention(qc0, Tq, hg, KT_src, V_src, steps, maskinfo):
                if cfg.dbg == 3:
                    return
                LAG = 3
                nsteps = len(steps)
                slot = {}

                def load(j):
                    (k0, tiles, mi) = steps[j]
                    nkeys = sum(tiles)
                    sl = self.nxt("kvt", 3)
                    slot[j] = sl
                    kt_ap, v_ap = KTt[sl], Vt[sl]
                    self.dma("sp", kt_ap[:, :, 0:nkeys], KT_src[hg * 4:(hg + 1) * 4, :, k0:k0 + nkeys].rearrange("h r k -> r h k"),
                             reads=["KT"], writes=[("KTt", sl)], semkey=("KTt", sl))
                    if tiles[0] == 128:
                        self.dma("sp", v_ap[:, 0:len(tiles), :], V_src[k0:k0 + nkeys, hg * 260:(hg + 1) * 260].rearrange("(t p) c -> p t c", p=128),
                                 reads=["V"], writes=[("Vt", sl)], semkey=("Vt", sl))
                    else:
                        self.dma("sp", v_ap[:nkeys, 0, :], V_src[k0:k0 + nkeys, hg * 260:(hg + 1) * 260],
                                 reads=["V"], writes=[("Vt", sl)], semkey=("Vt", sl))

                tl = []
                for j, (k0, tiles, mi) in enumerate(steps):
                    col = 0
                    for kt, nk in enumerate(tiles):
                        for hl in range(4):
                            tl.append((j, kt, nk, col, hl, mi, j == 0 and kt == 0, j == nsteps - 1 and kt == len(tiles) - 1,
                                       kt == len(tiles) - 1 and hl == 3))
                        col += nk
                n = len(tl)
                for j in range(min(3, nsteps)):
                    load(j)
                nextload = 3
                if Tq == 512 and cfg.pairs:
                    units = []
                    for j, (k0, tiles, mi) in enumerate(steps):
                        col = 0
                        for kt, nk in enumerate(tiles):
                            for pr in range(2):
                                units.append((j, kt, nk, col, pr, mi, j == 0 and kt == 0, j == nsteps - 1 and kt == len(tiles) - 1,
                                              kt == len(tiles) - 1 and pr == 1))
                            col += nk
                    nu = len(units)
                    gsl = {}
                    for i in range(nu + 1):
                        if i < nu:
                            (j, kt, nk, col, pr, mi, first, last, endstep) = units[i]
                            sl = slot[j]
                            g = self.nxt("scp", 2)
                            gsl[i] = g
                            for hh in range(2):
                                hl = 2 * pr + hh
                                b_ = 2 * g + hh
                                sc_ap = PS[b_][:nk, 0:512]
                                l_ap, r_ap = KTt[sl][:, hl, col:col + nk], QT[:, hl, qc0:qc0 + 512]
                                msk = mi is not None
                                S.op("pe", lambda e, sc_ap=sc_ap, l_ap=l_ap, r_ap=r_ap, msk=msk: e.matmul(sc_ap, l_ap, r_ap, start=True, stop=not msk),
                                     reads=[("KTt", sl), "QT"], writes=[("ps", b_)])
                                if msk:
                                    l2, r2 = aind[:, kt, 0:nk], bm[:, mi, 0:512]
                                    S.op("pe", lambda e, sc_ap=sc_ap, l2=l2, r2=r2: e.matmul(sc_ap, l2, r2, start=False, stop=True),
                                         reads=["aind", "bm"], writes=[("ps", b_)])
                        t = i - 1
                        if t >= 0:
                            (j, kt, nk, col, pr, mi, first, last, endstep) = units[t]
                            sl = slot[j]
                            g = gsl[t]
                            pi = self.nxt("pt2", 3)
                            p_ap = PT2[pi][:nk, :]
                            sc2 = psall[:nk, 2 * g * 512:(2 * g + 2) * 512]
                            S.op("act", lambda e, p_ap=p_ap, sc2=sc2: e.activation(p_ap, sc2, AF.Exp, scale=MLA_SCALE),
                                 reads=[("ps", 2 * g), ("ps", 2 * g + 1)], writes=[("PT2", pi)])
                            for hh in range(2):
                                hl = 2 * pr + hh
                                ob = 4 + hl
                                o_ap = PS[ob][:65, 0:512]
                                lv = Vt[sl][:nk, kt, hl * 65:(hl + 1) * 65]
                                pp = PT2[pi][:nk, hh * 512:(hh + 1) * 512]
                                S.op("pe", lambda e, o_ap=o_ap, lv=lv, pp=pp, first=first, last=last: e.matmul(o_ap, lv, pp, start=first, stop=last),
                                     reads=[("Vt", sl), ("PT2", pi)], writes=[("ps", ob)])
                            if endstep and nextload < nsteps:
                                load(nextload)
                                nextload += 1
                    n = 0
                scb, pts = {}, {}
                for i in range((n + LAG) if n else 0):
                    if i < n:
                        (j, kt, nk, col, hl, mi, first, last, endstep) = tl[i]
                        sl = slot[j]
                        sb_ = self.nxt("scb", 4)
                        scb[i] = sb_
                        sc_ap = PS[sb_][:nk, 0:Tq]
                        l_ap, r_ap = KTt[sl][:, hl, col:col + nk], QT[:, hl, qc0:qc0 + Tq]
                        msk = mi is not None
                        S.op("pe", lambda e, sc_ap=sc_ap, l_ap=l_ap, r_ap=r_ap, msk=msk: e.matmul(sc_ap, l_ap, r_ap, start=True, stop=not msk),
                             reads=[("KTt", sl), "QT"], writes=[("ps", sb_)])
                        if msk:
                            l2, r2 = aind[:, kt, 0:nk], bm[:, mi, 0:Tq]
                            S.op("pe", lambda e, sc_ap=sc_ap, l2=l2, r2=r2: e.matmul(sc_ap, l2, r2, start=False, stop=True),
                                 reads=["aind", "bm"], writes=[("ps", sb_)])
                    t = i - LAG
                    if t >= 0:
                        (j, kt, nk, col, hl, mi, first, last, endstep) = tl[t]
                        sl = slot[j]
                        sb_ = scb[t]
                        sc_ap = PS[sb_][:nk, 0:Tq]
                        pi = self.nxt("pt", 4)
                        p_ap = PT[pi][:nk, 0:Tq]
                        S.op("act", lambda e, p_ap=p_ap, sc_ap=sc_ap: e.activation(p_ap, sc_ap, AF.Exp, scale=MLA_SCALE),
                             reads=[("ps", sb_)], writes=[("PT", pi)])
                        ob = 4 + hl
                        o_ap = PS[ob][:65, 0:Tq]
                        lv = Vt[sl][:nk, kt, hl * 65:(hl + 1) * 65]
                        S.op("pe", lambda e, o_ap=o_ap, lv=lv, p_ap=p_ap, first=first, last=last: e.matmul(o_ap, lv, p_ap, start=first, stop=last),
                             reads=[("Vt", sl), ("PT", pi)], writes=[("ps", ob)])
                        if endstep and nextload < nsteps:
                            load(nextload)
                            nextload += 1
                for hl in range(4 if cfg.dbg != 4 else 0):
                    ob = 4 + hl
                    oi = self.nxt("ots", 2)
                    os_ap = OTs[oi][:, 0:Tq]
                    self.copy(self.ev_engine(), os_ap, PS[ob][:65, 0:Tq], reads=[("ps", ob)], writes=[("OTs", oi)])
                    drow = OTs[oi][64:65, 0:Tq]
                    S.op("act", lambda e, drow=drow: e.activation(drow, drow, AF.Ln), reads=[("OTs", oi)], writes=[("OTs", oi)])
                    S.op("act", lambda e, drow=drow: e.activation(drow, drow, AF.Exp, scale=-1.0), reads=[("OTs", oi)], writes=[("OTs", oi)])
                    db = self.nxt("scb", 4)
                    d_ap = PS[db][:64, 0:Tq]
                    S.op("pe", lambda e, d_ap=d_ap, os_ap=os_ap: e.matmul(d_ap, esel[:, :], os_ap, start=True, stop=True),
                         reads=["esel", ("OTs", oi)], writes=[("ps", db)])
                    out_ap = oT[:, hg * 4 + hl, qc0:qc0 + Tq]
                    num = OTs[oi][:64, 0:Tq]
                    S.op("dve", lambda e, out_ap=out_ap, num=num, d_ap=d_ap: e.tensor_tensor(out_ap, num, d_ap, ALU.mult),
                         reads=[("OTs", oi), ("ps", db)], writes=[("oT", hg)])

            for blk in range(NB + 1):
                prompt = blk < NB
                P = 128 if prompt else 64
                T4 = 4 * P
                if prompt:
                    src = x2s[blk * 512:(blk + 1) * 512, :].rearrange("(s p) d -> p s d", p=128)
                else:
                    src = x2s[NB * 512:NB * 512 + 256, :].rearrange("(s p) d -> p s d", p=64)
                self.dma("sp", xt4[:P, :, :], src, reads=["x2s"], writes=[("x4", s) for s in range(4)], semkey="xt")
                for hf in range(2):
                    t = 2 * blk + hf
                    self.dma("sp", cosm[:, :, :], cosm_d[t], reads=[], writes=["cosm"], semkey="cosm")
                    self.dma("sp", sinm[:, :, :], sinm_d[t], reads=[], writes=["sinm"], semkey="sinm")
                    self.rms_to_bf16(lambda s: xt4[:P, 2 * hf + s, :], lambda s: [("x4", 2 * hf + s)],
                                     lambda s: hb[:P, s, :], lambda s: [("hb", s)], (0, 1), P, "n1")
                    for s in (0, 1):
                        self.transposes([hb[:P, s, kc * 128:(kc + 1) * 128] for kc in range(8)], [("hb", s)],
                                        hT[:, :, s * P:(s + 1) * P], [("hT", s)], P, 128)

                    def cq_evac(s, j, c0, w, ps_ap, pk):
                        sk = ("cqss", s)
                        ssa, rsa = self.ss[:P, 4 + s:5 + s], self.rstd[:P, 4 + s:5 + s]
                        S.op("dve", lambda e: e.memset(ssa, 0.0), writes=[sk])
                        S.op("act", lambda e: e.activation(self.junk[:P, 0:384], ps_ap, AF.Square, accum_out=ssa),
                             reads=[pk], writes=[sk, "junk"])
                        self.rsqrt(rsa, ssa, 1.0 / 384, [sk], [("cqrs", s)])
                        S.op("act", lambda e: e.activation(cqb[:P, s, :], ps_ap, AF.Copy, scale=rsa),
                             reads=[pk, ("cqrs", s)], writes=[("cqb", s)])
                        self.transposes([cqb[:P, s, c * 128:(c + 1) * 128] for c in range(3)], [("cqb", s)],
                                        cqT[:, :, s * P:(s + 1) * P], [("cqT", s)], P, 128)
                    if cfg.dbg == 5:
                        S.mute = True
                    self.linear_T(hT, [("hT", 0), ("hT", 1)], "w_dq", 1024, 128, 0, 384, 384, (0, 1), P, cq_evac, bank_l1)
                    if cfg.dbg == 6:
                        S.mute = True

                    def q_evac(s, j, c0, w, ps_ap, pk):
                        s4 = 2 * hf + s
                        pv = ps_ap.rearrange("p (a b) -> p a b", b=96)
                        for hh in range(4):
                            self.copy(self.ev_engine(), qtok[:P, s4, 4 * j + hh, 0:64], ps_ap[:, hh * 96:hh * 96 + 64],
                                      reads=[pk], writes=[("qtok", s4)])
                        for hh in range(4):
                            hd = 4 * j + hh
                            a1, a2 = ps_ap[:, hh * 96 + 64:hh * 96 + 80], ps_ap[:, hh * 96 + 80:hh * 96 + 96]
                            c1, s1 = cosm[:P, s, :], sinm[:P, s, :]
                            t1, t2, t3, t4 = [k_[:P, :] for k_ in krt]
                            for (tt, xx, tb, i_, tbk) in ((t1, a1, c1, 0, "cosm"), (t2, a2, s1, 1, "sinm"), (t3, a2, c1, 2, "cosm"), (t4, a1, s1, 3, "sinm")):
                                S.op("dve", lambda e, tt=tt, xx=xx, tb=tb: e.tensor_tensor(tt, xx, tb, ALU.mult),
                                     reads=[pk, tbk], writes=[("krt", i_)])
                            o1, o2 = qtok[:P, s4, hd, 64:80], qtok[:P, s4, hd, 80:96]
                            S.op("dve", lambda e, o1=o1, t1=t1, t2=t2: e.tensor_tensor(o1, t1, t2, ALU.subtract),
                                 reads=[("krt", 0), ("krt", 1)], writes=[("qtok", s4)])
                            S.op("dve", lambda e, o2=o2, t3=t3, t4=t4: e.tensor_tensor(o2, t3, t4, ALU.add),
                                 reads=[("krt", 2), ("krt", 3)], writes=[("qtok", s4)])
                    self.linear_T(cqT, [("cqT", 0), ("cqT", 1)], "w_uq", 384, 128, 0, 1536, 384, (0, 1), P, q_evac, bank_l1)

                if cfg.dbg == 7:
                    S.mute = True
                for hg in range(4):
                    for hl in range(4):
                        hd = hg * 4 + hl
                        b = self.nxt("tpb", 2)
                        pk = ("ps", b)
                        for s4 in range(4):
                            o_ap, s_ap = PSB[b][:96, s4 * P:(s4 + 1) * P], qtok[:P, s4, hd, :]
                            S.op("pe", lambda e, o_ap=o_ap, s_ap=s_ap: e.transpose(o_ap, s_ap, self.ident[:P, :P]),
                                 reads=[("qtok", s4), "ident"], writes=[pk])
                        self.copy(self.ev_engine(), QT[:, hl, 0:T4], PSB[b][:96, 0:T4], reads=[pk], writes=["QT"])
                    if prompt:
                        steps = []
                        for kbk in range(4 * blk + 4):
                            mi = kbk - 4 * blk if kbk >= 4 * blk else None
                            steps.append((kbk * 512, [128] * 4, mi))
                        attention(0, 512, hg, KTp, Vp, steps, None)
                    else:
                        for q in range(4):
                            steps = [(kbk * 512, [128] * 4, None) for kbk in range(PAST // 512)] + [(PAST, [64], None)]
                            attention(q * 64, 64, hg, KTs[q], Vs[q], steps, None)

                if cfg.dbg == 8:
                    S.mute = True
                def o_resid(s, j, c0, w, ps_ap, pk):
                    x_ap = xt4[:P, s, c0:c0 + w]
                    S.op("dve", lambda e, x_ap=x_ap, ps_ap=ps_ap: e.tensor_tensor(x_ap, ps_ap, x_ap, ALU.add),
                         reads=[pk, ("x4", s)], writes=[("x4", s)])
                self.linear_T(oT, [("oT", g_) for g_ in range(4)], "w_o", 1024, 64, 0, D, 512, (0, 1, 2, 3), P, o_resid,
                              lambda si, par: si + 4 * par)

                for hf in range(2):
                    def f_resid(s, c0, w, ps_ap, pk, hf=hf):
                        x_ap = xt4[:P, 2 * hf + s, c0:c0 + w]
                        S.op("dve", lambda e, x_ap=x_ap, ps_ap=ps_ap: e.tensor_tensor(x_ap, ps_ap, x_ap, ALU.add),
                             reads=[pk, ("x4", 2 * hf + s)], writes=[("x4", 2 * hf + s)])
                    ffn(1, P, f_resid, lambda s, hf=hf: xt4[:P, 2 * hf + s, :], lambda s, hf=hf: ("x4", 2 * hf + s), "n1")
                    for s in (0, 1):
                        s4 = 2 * hf + s
                        ssk = ("fss", s)
                        ssa, rsa = self.ss[:P, 6 + s:7 + s], self.rstd[:P, 6 + s:7 + s]
                        xa = xt4[:P, s4, :]
                        S.op("dve", lambda e, ssa=ssa: e.memset(ssa, 0.0), writes=[ssk])
                        S.op("act", lambda e, xa=xa, ssa=ssa: e.activation(self.junk[:P, :D], xa, AF.Square, accum_out=ssa),
                             reads=[("x4", s4)], writes=[ssk, "junk"])
                        self.rsqrt(rsa, ssa, 1.0 / D, [ssk], [("frs", s)])
                        ya = yo[:P, s, :]
                        S.op("dve", lambda e, ya=ya, xa=xa, rsa=rsa: e.scalar_tensor_tensor(ya, xa, rsa, gfn[:P, :], ALU.mult, ALU.mult),
                             reads=[("x4", s4), ("frs", s), "gfn"], writes=[("yo", s)])
                    if prompt:
                        self.dma("pool", yp[blk, hf * 256:(hf + 1) * 256, :].rearrange("(s p) d -> p s d", p=128), yo[:, :, :],
                                 reads=[("yo", 0), ("yo", 1)], writes=[], semkey="o_y")
                    else:
                        self.dma("pool", ys[2 * hf:2 * hf + 2, :, :].rearrange("s p d -> p s d"), yo[:64, :, :],
                                 reads=[("yo", 0), ("yo", 1)], writes=[], semkey="o_y")

            S.emit()
        return self.nc


def _tables(cfg, c):
    NB, NT0, PAST = cfg.NB, cfg.NT0, cfg.PAST
    g = np.array(GAM, dtype=np.float64)
    p = np.arange(128, dtype=np.float64)
    dec = np.zeros((128, 12), np.float64)
    for h in range(4):
        dec[:, h] = g[h] ** (p + 1)
        dec[:, 4 + h] = g[h] ** (127 - p) / 16.0
        dec[:64, 8 + h] = g[h] ** (63 - p[:64]) / 16.0
    m128 = np.zeros((128, 4, 128), np.float64)
    m64 = np.zeros((128, 4, 128), np.float64)
    kk, qq = np.meshgrid(np.arange(128), np.arange(128), indexing="ij")
    for h in range(4):
        m128[:, h, :] = np.where(qq >= kk, g[h] ** -128.0, 0.0)
        m64[:64, h, :64] = np.where(qq[:64, :64] >= kk[:64, :64], g[h] ** -64.0, 0.0)
    pos = np.zeros((NT0, 128, 2), np.float32)
    for t in range(2 * NB):
        m, hf = t // 2, t % 2
        for s in range(2):
            pos[t, :, s] = (4 * m + c) * 512 + hf * 256 + s * 128 + np.arange(128)
    for t in range(2 * NB, NT0):
        pos[t, :64, :] = (PAST + np.arange(64))[:, None]
    tabs = {}
    for nm, half in (("r", 128), ("m", 16)):
        inv = (np.float32(10000.0) ** (-np.arange(half, dtype=np.float32) / np.float32(half))).astype(np.float32)
        ang = (pos[..., None] * inv[None, None, None, :]).astype(np.float32)
        tabs["cos" + nm] = np.cos(ang).astype(np.float32)
        tabs["sin" + nm] = np.sin(ang).astype(np.float32)
    pco = np.zeros((128, 24), np.float64)
    for h in range(4):
        pco[:, h] = g[h] ** (512.0 * c)
        for i in range(4):
            pco[:, 4 + 4 * i + h] = g[h] ** (512.0 * (c - 1 - i)) if i < c else 0.0
    aind = np.zeros((8, 4, 128), np.float32)
    for kt in range(4):
        for k in range(128):
            aind[(kt * 128 + k) // 64, kt, k] = 1.0
    bmk = np.zeros((8, 4, 512), np.float32)
    qch = np.arange(512) // 64
    for i in range(4):
        for r in range(8):
            if i == c:
                bmk[r, i, :] = np.where(qch < r, NEG, 0.0)
            elif i > c:
                bmk[r, i, :] = NEG
    esel = np.zeros((65, 64), np.float32)
    esel[64, :] = 1.0
    out = {
        "dec": dec.astype(np.float32), "m128": m128.astype(np.float32), "m64": m64.astype(np.float32),
        "cosr": tabs["cosr"], "sinr": tabs["sinr"], "cosm": tabs["cosm"], "sinm": tabs["sinm"],
        "pcoef": pco.astype(np.float32), "aind": aind.astype(ml_dtypes.bfloat16), "bm": bmk.astype(ml_dtypes.bfloat16),
        "ident": np.eye(128, dtype=np.float32).astype(ml_dtypes.bfloat16), "esel": esel,
    }
    return out


_PROG = {}


def run(cfg, inputs):
    NB, PAST, SEQ = cfg.NB, cfg.PAST, cfg.SEQ
    key = (NB, PAST, cfg.upto, cfg.dbg)
    if key not in _PROG:
        _PROG[key] = Prog(cfg).build()
    nc = _PROG[key]
    f = lambda a: np.ascontiguousarray(np.asarray(a, dtype=np.float32))
    I = {k: f(v) for k, v in inputs.items()}
    wd = {
        "w_ret_in": I["w_ret_in"][0], "w_ret_out": I["w_ret_out"][0],
        "w_ffn_in0": I["w_ffn_in"][0], "w_ffn_out0": I["w_ffn_out"][0],
        "w_ffn_in1": I["w_ffn_in"][1], "w_ffn_out1": I["w_ffn_out"][1],
        "w_dkv": I["w_dkv"], "w_dq": I["w_dq"][0], "w_uq": I["w_uq"][0], "w_o": I["w_mla_out"][0],
        "w_uk": I["w_uk"].reshape(256, 1024), "w_uv": I["w_uv"].reshape(256, 1024),
    }
    gl = [I["g_mix"][0], I["g_ffn"][0], I["g_ffn"][1], I["g_kv_in"], I["g_mix"][1]]
    gtab = np.zeros((128, 43), np.float32)
    for i, gv in enumerate(gl):
        gtab[:, 8 * i:8 * i + 8] = gv.reshape(8, 128).T
    gtab[:, 40:43] = I["g_q"][0].reshape(3, 128).T
    gckv = np.ascontiguousarray(np.broadcast_to(I["g_ckv"][None, :], (128, 256)))
    gfin = np.ascontiguousarray(np.broadcast_to(I["g_final"][None, :], (128, D)))
    in_maps = []
    for core in range(8):
        b, c = core // 4, core % 4
        d = dict(wd)
        d["xp"] = np.ascontiguousarray(I["x_prompt"][b].reshape(SEQ // 512, 512, D)[c::4])
        d["xs"] = np.ascontiguousarray(I["x_sample"][4 * core:4 * core + 4])
        d["st0"] = np.ascontiguousarray(I["state_ret"][0, 4 * core:4 * core + 4])
        d["cckv"] = np.ascontiguousarray(I["cache_ckv"][4 * core:4 * core + 4])
        d["ckr"] = np.ascontiguousarray(I["cache_krope"][4 * core:4 * core + 4])
        d["gtab"], d["gckv"], d["gfin"] = gtab, gckv, gfin
        d.update(_tables(cfg, c))
        in_maps.append(d)
    res = run_bass_kernel_spmd(nc, in_maps, core_ids=list(range(8)))
    R = res.results
    y_p = np.zeros((2, SEQ, D), np.float32)
    ckv_p = np.zeros((2, SEQ, 256), np.float32)
    kr_p = np.zeros((2, SEQ, 32), np.float32)
    st_p = np.zeros((1, 2, 4, 256, 512), np.float32)
    y_s = np.zeros((32, 64, D), np.float32)
    st_s = np.zeros((1, 32, 4, 256, 512), np.float32)
    ckv_s = np.zeros((32, 64, 256), np.float32)
    kr_s = np.zeros((32, 64, 32), np.float32)
    for core in range(8):
        b, c = core // 4, core % 4
        r = R[core]
        y_p[b].reshape(SEQ // 512, 512, D)[c::4] = np.asarray(r["yp"], dtype=np.float32)
        ckv_p[b].reshape(SEQ // 512, 512, 256)[c::4] = np.asarray(r["ckvp"], dtype=np.float32)
        kr_p[b].reshape(SEQ // 512, 512, 32)[c::4] = np.asarray(r["krp"], dtype=np.float32)
        if c == 3:
            st_p[0, b] = np.asarray(r["stp"], dtype=np.float32)
        y_s[4 * core:4 * core + 4] = np.asarray(r["ys"], dtype=np.float32)
        st_s[0, 4 * core:4 * core + 4] = np.asarray(r["sts"], dtype=np.float32)
        ckv_s[4 * core:4 * core + 4] = np.asarray(r["ckvs"], dtype=np.float32)
        kr_s[4 * core:4 * core + 4] = np.asarray(r["krs"], dtype=np.float32)
    return (y_p, y_s, st_p, st_s, ckv_p, kr_p, ckv_s, kr_s)


def kernel(**inputs):
    return run(Cfg(8, 4096), inputs)
```
